# Optimizing a Trainium2 kernel written in Bass

```python
import jax, jax.numpy as jnp
from jax import lax
import numpy as np

D_MODEL = 2048
BATCH = 4
SEQ = 4096
DEPTH = 4
DEC_BATCH = 32
DEC_SEQ = 32
PAST_LEN = 2048

CHUNK = 64
HEAD_DIM = 128
MEM_HEADS = 4
MEM_DIM = MEM_HEADS * HEAD_DIM
TOK_DIM = D_MODEL - MEM_DIM
TOK_HEADS = TOK_DIM // HEAD_DIM
N_MEM = 256
D_FF = 4 * D_MODEL
CONV_W = 4
N_A = DEPTH // 2
N_B = DEPTH - N_A
Q_BLOCK = 128
EPS = 1e-6
QKV_DIM = 3 * TOK_DIM
IN_A = QKV_DIM + TOK_DIM + 2 * TOK_HEADS + MEM_DIM
IN_B = 2 * TOK_DIM + MEM_DIM
KVF_DIM = 2 * TOK_DIM + TOK_HEADS
SCALE = HEAD_DIM ** -0.5
F32 = jnp.float32

kernel_name = "yoco_gdn_fox_stream_step"


def rmsnorm(x, g):
    xf = x.astype(F32)
    y = xf * lax.rsqrt(jnp.mean(xf * xf, axis=-1, keepdims=True) + EPS)
    return (y * g.astype(F32)).astype(x.dtype)


def l2norm(x):
    xf = x.astype(F32)
    return xf * lax.rsqrt(jnp.sum(xf * xf, axis=-1, keepdims=True) + EPS)


def causal_conv(x, hist, w):
    T = x.shape[1]
    xp = jnp.concatenate([hist.astype(x.dtype), x], axis=1)
    y = xp[:, 0:T] * w[0]
    for j in range(1, CONV_W):
        y = y + xp[:, j:j + T] * w[j]
    return y, xp[:, T:]


def gdn_chunk(S, inp):
    q, k, v, g, beta = inp
    L = q.shape[2]
    dv = v.shape[-1]
    idx = jnp.arange(L)
    incl = idx[:, None] >= idx[None, :]
    strict = idx[:, None] > idx[None, :]
    G = jnp.cumsum(g, axis=-1)
    decay = jnp.exp(jnp.where(incl, G[..., :, None] - G[..., None, :], -jnp.inf))
    A = jnp.where(strict, beta[..., :, None] * jnp.einsum('bhid,bhjd->bhij', k, k) * decay, 0.0)
    eye = jnp.eye(L, dtype=F32)
    rhs = jnp.concatenate([beta[..., None] * v, (beta * jnp.exp(G))[..., None] * k], axis=-1)
    sol = lax.linalg.triangular_solve(eye + A, rhs, left_side=True, lower=True, unit_diagonal=True)
    u, w = sol[..., :dv], sol[..., dv:]
    v_new = u - jnp.einsum('bhik,bhkv->bhiv', w, S)
    attn = jnp.einsum('bhid,bhjd->bhij', q, k) * decay
    out = (jnp.einsum('bhik,bhkv->bhiv', q * jnp.exp(G)[..., None], S)
           + jnp.einsum('bhij,bhjv->bhiv', attn, v_new))
    GL = G[..., -1:]
    S_new = (jnp.exp(GL)[..., None] * S
             + jnp.einsum('bhik,bhiv->bhkv', k * jnp.exp(GL - G)[..., None], v_new))
    return S_new, out


def gated_delta(q, k, v, g, beta, S0):
    B, T, H, D = q.shape
    L = min(T, CHUNK)
    N = T // L

    def blocks(a):
        a = a.reshape((B, N, L) + a.shape[2:])
        return a.transpose((1, 0, 3, 2) + tuple(range(4, a.ndim)))

    S_fin, outs = lax.scan(gdn_chunk, S0, (blocks(q), blocks(k), blocks(v), blocks(g), blocks(beta)))
    o = outs.transpose(1, 0, 3, 2, 4).reshape(B, T, H, D)
    return o, S_fin


def gdn_mixer(h, conv_hist, s0, w_in, conv_w, a_log, dt_bias, gdn_norm):
    B, T, _ = h.shape
    proj = h @ w_in
    o1 = QKV_DIM + TOK_DIM
    qkv = proj[..., :QKV_DIM]
    z = proj[..., QKV_DIM:o1]
    a = proj[..., o1:o1 + TOK_HEADS]
    b = proj[..., o1 + TOK_HEADS:o1 + 2 * TOK_HEADS]
    mq = proj[..., o1 + 2 * TOK_HEADS:]
    qkv, conv_state = causal_conv(qkv, conv_hist, conv_w)
    qkv = jax.nn.silu(qkv).reshape(B, T, 3, TOK_HEADS, HEAD_DIM)
    q = l2norm(qkv[:, :, 0]) * SCALE
    k = l2norm(qkv[:, :, 1])
    v = qkv[:, :, 2].astype(F32)
    g = -jnp.exp(a_log.astype(F32)) * jax.nn.softplus(a.astype(F32) + dt_bias.astype(F32))
    beta = jax.nn.sigmoid(b.astype(F32))
    o, s_new = gated_delta(q, k, v, g, beta, s0.astype(F32))
    o = rmsnorm(o, gdn_norm) * jax.nn.silu(z.reshape(B, T, TOK_HEADS, HEAD_DIM).astype(F32))
    return o.reshape(B, T, TOK_DIM).astype(h.dtype), conv_state, s_new, mq


def fox_block(q, cq, qpos, k, v, ckT, kpos):
    s = jnp.einsum('bqhd,bkhd->bhqk', q, k).astype(F32) * SCALE
    s = s + jnp.swapaxes(cq, 1, 2)[..., :, None] - ckT[..., None, :]
    s = jnp.where(kpos[None, :] <= qpos[:, None], s, -jnp.inf)
    p = jax.nn.softmax(s, axis=-1).astype(v.dtype)
    return jnp.einsum('bhqk,bkhd->bqhd', p, v)


def fox_attend(q, cq, qpos, k, v, ck, kpos):
    B, Tq, H, D = q.shape
    ckT = jnp.swapaxes(ck, 1, 2)
    if Tq <= Q_BLOCK:
        return fox_block(q, cq, qpos, k, v, ckT, kpos)
    nb = Tq // Q_BLOCK
    qb = jnp.swapaxes(q.reshape(B, nb, Q_BLOCK, H, D), 0, 1)
    cqb = jnp.swapaxes(cq.reshape(B, nb, Q_BLOCK, H), 0, 1)
    pb = qpos.reshape(nb, Q_BLOCK)
    out = lax.map(lambda a: fox_block(a[0], a[1], a[2], k, v, ckT, kpos), (qb, cqb, pb))
    return jnp.swapaxes(out, 0, 1).reshape(B, Tq, H, D)


def fox_mixer(h, w_in, k_all, v_all, c_all, qpos, kpos, P):
    B, T, _ = h.shape
    proj = h @ w_in
    q = proj[..., :TOK_DIM].reshape(B, T, TOK_HEADS, HEAD_DIM)
    gate = proj[..., TOK_DIM:2 * TOK_DIM]
    mq = proj[..., 2 * TOK_DIM:]
    o = fox_attend(q, c_all[:, P:], qpos, k_all, v_all, c_all, kpos)
    return o.reshape(B, T, TOK_DIM) * jax.nn.sigmoid(gate), mq


def shared_kv(x, norm_kv, w_kvf, b_f):
    B, T, _ = x.shape
    kvf = rmsnorm(x, norm_kv) @ w_kvf
    k = kvf[..., :TOK_DIM].reshape(B, T, TOK_HEADS, HEAD_DIM)
    v = kvf[..., TOK_DIM:2 * TOK_DIM].reshape(B, T, TOK_HEADS, HEAD_DIM)
    logf = jax.nn.log_sigmoid(kvf[..., 2 * TOK_DIM:].astype(F32) + b_f.astype(F32))
    return k, v, logf


def memory_kv(mem, norm_mem, w_mem_kv):
    hm = rmsnorm(mem[None], norm_mem[:, None, None, :])
    kv = jnp.einsum('lbnd,lde->lbne', hm, w_mem_kv)
    B, N = mem.shape[0], mem.shape[1]
    mk = kv[..., :MEM_DIM].reshape(DEPTH, B, N, MEM_HEADS, HEAD_DIM)
    mv = kv[..., MEM_DIM:].reshape(DEPTH, B, N, MEM_HEADS, HEAD_DIM)
    return mk, mv


def mem_attend(mq, mk, mv):
    B, T, _ = mq.shape
    q = mq.reshape(B, T, MEM_HEADS, HEAD_DIM)
    s = jnp.einsum('bthd,bnhd->bhtn', q, mk).astype(F32) * SCALE
    p = jax.nn.softmax(s, axis=-1).astype(mv.dtype)
    return jnp.einsum('bhtn,bnhd->bthd', p, mv).reshape(B, T, MEM_DIM)


def trunk(x, conv_hist, gdn_s0, past_k, past_v, past_logf, mem_k, mem_v, p):
    B, T, _ = x.shape
    P = past_k.shape[1]
    conv_out, gdn_out = [], []
    k_new = v_new = logf_new = None
    for l in range(DEPTH):
        if l == N_A:
            k_new, v_new, logf_new = shared_kv(x, p['norm_kv'], p['w_kvf'], p['b_f'])
            k_all = jnp.concatenate([past_k.astype(k_new.dtype), k_new], axis=1)
            v_all = jnp.concatenate([past_v.astype(v_new.dtype), v_new], axis=1)
            c_all = jnp.cumsum(jnp.concatenate([past_logf.astype(F32), logf_new], axis=1), axis=1)
            kpos = jnp.arange(P + T)
            qpos = P + jnp.arange(T)
        h = rmsnorm(x, p['norm_mix_pre'][l])
        if l < N_A:
            tok, cs, gs, mq = gdn_mixer(h, conv_hist[l], gdn_s0[l], p['w_in_a'][l], p['conv_w_a'][l],
                                        p['a_log'][l], p['dt_bias'][l], p['gdn_norm'][l])
            conv_out.append(cs)
            gdn_out.append(gs)
        else:
            tok, mq = fox_mixer(h, p['w_in_b'][l - N_A], k_all, v_all, c_all, qpos, kpos, P)
        mem_o = mem_attend(mq, mem_k[l], mem_v[l])
        mix = jnp.concatenate([tok, mem_o], axis=-1) @ p['w_o'][l]
        x = x + rmsnorm(mix, p['norm_mix_post'][l])
        hf = rmsnorm(x, p['norm_mlp_pre'][l])
        f = jnp.square(jax.nn.relu(hf @ p['w_up'][l])) @ p['w_down'][l]
        x = x + rmsnorm(f, p['norm_mlp_post'][l])
    return x, jnp.stack(conv_out), jnp.stack(gdn_out), k_new, v_new, logf_new


def setup_inputs(seed: int = 0) -> dict:
    key = jax.random.key(seed)
    ks = jax.random.split(key, 32)

    def nrm(k, shape, scale):
        return jax.random.normal(k, shape, F32) * scale

    def gain(k, shape):
        return 1.0 + 0.02 * jax.random.normal(k, shape, F32)

    dt = jnp.exp(jax.random.uniform(ks[7], (N_A, TOK_HEADS), F32, float(np.log(1e-3)), float(np.log(1e-1))))
    w_kvf = nrm(ks[12], (D_MODEL, KVF_DIM), D_MODEL ** -0.5)
    w_kvf = w_kvf.at[:, 2 * TOK_DIM:].multiply(0.3)
    logf_bias = jax.random.uniform(ks[20], (1, 1, TOK_HEADS), F32, 2.0, 6.0)
    return {
        "x_prompt": nrm(ks[0], (BATCH, SEQ, D_MODEL), 1.0),
        "x_sample": nrm(ks[1], (DEC_BATCH, DEC_SEQ, D_MODEL), 1.0),
        "state_gdn": nrm(ks[17], (N_A, DEC_BATCH, TOK_HEADS, HEAD_DIM, HEAD_DIM), 0.1),
        "state_conv": nrm(ks[18], (N_A, DEC_BATCH, CONV_W - 1, QKV_DIM), 1.0),
        "cache_k": nrm(ks[19], (DEC_BATCH, PAST_LEN, TOK_HEADS, HEAD_DIM), 1.0),
        "cache_v": nrm(ks[21], (DEC_BATCH, PAST_LEN, TOK_HEADS, HEAD_DIM), 1.0),
        "cache_logf": jax.nn.log_sigmoid(logf_bias + 0.3 * jax.random.normal(ks[22], (DEC_BATCH, PAST_LEN, TOK_HEADS), F32)),
        "cache_mem_k": nrm(ks[23], (DEPTH, DEC_BATCH, N_MEM, MEM_HEADS, HEAD_DIM), 1.0),
        "cache_mem_v": nrm(ks[24], (DEPTH, DEC_BATCH, N_MEM, MEM_HEADS, HEAD_DIM), 1.0),
        "mem_prompt": nrm(ks[2], (BATCH, N_MEM, D_MODEL), 1.0),
        "norm_mix_pre": gain(ks[3], (DEPTH, D_MODEL)),
        "norm_mix_post": gain(ks[4], (DEPTH, D_MODEL)),
        "norm_mlp_pre": gain(ks[5], (DEPTH, D_MODEL)),
        "norm_mlp_post": gain(ks[6], (DEPTH, D_MODEL)),
        "w_in_a": nrm(ks[8], (N_A, D_MODEL, IN_A), D_MODEL ** -0.5),
        "conv_w_a": nrm(ks[9], (N_A, CONV_W, QKV_DIM), CONV_W ** -0.5),
        "a_log": jnp.log(jax.random.uniform(ks[10], (N_A, TOK_HEADS), F32, 1.0, 16.0)),
        "dt_bias": dt + jnp.log(-jnp.expm1(-dt)),
        "gdn_norm": gain(ks[11], (N_A, HEAD_DIM)),
        "w_in_b": nrm(ks[13], (N_B, D_MODEL, IN_B), D_MODEL ** -0.5),
        "norm_kv": gain(ks[14], (D_MODEL,)),
        "w_kvf": w_kvf,
        "b_f": jax.random.uniform(ks[15], (TOK_HEADS,), F32, 2.0, 6.0),
        "norm_mem": gain(ks[16], (DEPTH, D_MODEL)),
        "w_mem_kv": nrm(ks[25], (DEPTH, D_MODEL, 2 * MEM_DIM), D_MODEL ** -0.5),
        "w_o": nrm(ks[26], (DEPTH, D_MODEL, D_MODEL), D_MODEL ** -0.5),
        "w_up": nrm(ks[27], (DEPTH, D_MODEL, D_FF), D_MODEL ** -0.5),
        "w_down": nrm(ks[28], (DEPTH, D_FF, D_MODEL), D_FF ** -0.5),
    }


def reference(x_prompt, x_sample, state_gdn, state_conv, cache_k, cache_v, cache_logf, cache_mem_k, cache_mem_v,
              mem_prompt, norm_mix_pre, norm_mix_post, norm_mlp_pre, norm_mlp_post, w_in_a, conv_w_a, a_log,
              dt_bias, gdn_norm, w_in_b, norm_kv, w_kvf, b_f, norm_mem, w_mem_kv, w_o, w_up, w_down):
    params = dict(norm_mix_pre=norm_mix_pre, norm_mix_post=norm_mix_post, norm_mlp_pre=norm_mlp_pre,
                  norm_mlp_post=norm_mlp_post, w_in_a=w_in_a, conv_w_a=conv_w_a, a_log=a_log, dt_bias=dt_bias,
                  gdn_norm=gdn_norm, w_in_b=w_in_b, norm_kv=norm_kv, w_kvf=w_kvf, b_f=b_f, w_o=w_o,
                  w_up=w_up, w_down=w_down)
    Bp = x_prompt.shape[0]
    dtp = x_prompt.dtype
    p_mem_k, p_mem_v = memory_kv(mem_prompt, norm_mem, w_mem_kv)
    zero_conv = jnp.zeros((N_A, Bp, CONV_W - 1, QKV_DIM), dtp)
    zero_gdn = jnp.zeros((N_A, Bp, TOK_HEADS, HEAD_DIM, HEAD_DIM), F32)
    empty_kv = jnp.zeros((Bp, 0, TOK_HEADS, HEAD_DIM), dtp)
    empty_logf = jnp.zeros((Bp, 0, TOK_HEADS), F32)
    y_prompt, p_conv, p_gdn, p_k, p_v, p_logf = trunk(
        x_prompt, zero_conv, zero_gdn, empty_kv, empty_kv, empty_logf, p_mem_k, p_mem_v, params)
    y_sample, s_conv, s_gdn, s_k, s_v, s_logf = trunk(
        x_sample, state_conv, state_gdn, cache_k, cache_v, cache_logf, cache_mem_k, cache_mem_v, params)
    return (y_prompt, y_sample, p_gdn, p_conv, p_k, p_v, p_logf, p_mem_k, p_mem_v,
            s_gdn, s_conv, s_k, s_v, s_logf)
```

```python
import numpy as np
from contextlib import ExitStack
import concourse.bass as bass
import concourse.mybir as mybir
from concourse.bass_utils import run_bass_kernel_spmd

F32 = mybir.dt.float32
BF16 = mybir.dt.bfloat16
AF = mybir.ActivationFunctionType
ALU = mybir.AluOpType


class Res:
    __slots__ = ("name", "excl", "lw", "rd", "sem", "semcnt")

    def __init__(self, name, excl=False):
        self.name = name
        self.excl = excl
        self.lw = None
        self.rd = []
        self.sem = None
        self.semcnt = 0


class Op:
    __slots__ = ("eng", "fn", "deps", "sig", "isdma", "sem", "val")

    def __init__(self, eng, fn, isdma):
        self.eng = eng
        self.fn = fn
        self.deps = []
        self.sig = False
        self.isdma = isdma
        self.sem = None
        self.val = 0


class Prog:
    ENGS = ["pe", "act", "dve", "pool", "sp"]

    def __init__(self, nc, es):
        self.nc = nc
        self.es = es
        self.ops = []
        self.esem = {}
        for e in ["pe", "act", "dve", "pool"]:
            self.esem[e] = es.enter_context(nc.semaphore("s_" + e))

    def _handle(self, eng):
        nc = self.nc
        return {"pe": nc.tensor, "act": nc.scalar, "dve": nc.vector,
                "pool": nc.gpsimd, "sp": nc.sync}[eng]

    def _add(self, op, reads, writes):
        deps = {}
        writes = list(writes)
        for r in reads:
            if r.excl:
                writes.append(r)
                continue
            if r.lw is not None:
                deps[id(r.lw)] = (r.lw, True)
        for w in writes:
            if w.lw is not None and id(w.lw) not in deps:
                deps[id(w.lw)] = (w.lw, w.excl)
            for rd in w.rd:
                if id(rd) not in deps:
                    deps[id(rd)] = (rd, False)
        for r in reads:
            if not r.excl:
                if not op.isdma:
                    r.rd = [x for x in r.rd if x.isdma or x.eng != op.eng]
                r.rd.append(op)
        for w in writes:
            w.lw = op
            w.rd = []
        op.deps = [d for d in deps.values() if d[0] is not op]
        self.ops.append(op)
        return op

    def op(self, eng, fn, reads=(), writes=()):
        return self._add(Op(eng, fn, False), reads, writes)

    def dma(self, eng, fn, semres, reads=(), writes=()):
        op = Op(eng, fn, True)
        if semres.sem is None:
            semres.sem = self.es.enter_context(self.nc.semaphore("d_" + semres.name))
        semres.semcnt += 16
        op.sem = semres.sem
        op.val = semres.semcnt
        return self._add(op, reads, writes)

    @staticmethod
    def _needs(d, raw, op):
        if d.isdma:
            return True
        if d.eng != op.eng:
            return True
        if d.eng == "pe":
            return False
        return raw

    def emit(self):
        for op in self.ops:
            for d, raw in op.deps:
                if self._needs(d, raw, op):
                    d.sig = True
        cnt = {e: 0 for e in self.ENGS}
        for op in self.ops:
            if not op.isdma and op.sig:
                cnt[op.eng] += 1
                op.sem = self.esem[op.eng]
                op.val = cnt[op.eng]
        nwait = 0
        for eng in self.ENGS:
            h = self._handle(eng)
            waited = {}
            for op in self.ops:
                if op.eng != eng:
                    continue
                need = {}
                for d, raw in op.deps:
                    if self._needs(d, raw, op):
                        k = id(d.sem)
                        if waited.get(k, 0) < d.val and need.get(k, (None, 0))[1] < d.val:
                            need[k] = (d.sem, d.val)
                for k, (sm, v) in need.items():
                    h.wait_ge(sm, v)
                    waited[k] = v
                    nwait += 1
                ins = op.fn(h)
                if op.isdma:
                    ins.then_inc(op.sem, 16)
                elif op.sig:
                    ins.then_inc(op.sem, 1)
        sp = self._handle("sp")
        seen = {}
        for op in self.ops:
            if op.isdma:
                k = id(op.sem)
                if seen.get(k, (None, 0))[1] < op.val:
                    seen[k] = (op.sem, op.val)
        for sm, v in seen.values():
            sp.wait_ge(sm, v)
        return dict(n_ops=len(self.ops), n_wait=nwait, sig=dict(cnt))


D = 2048
KC = 16
H = 12
HD = 128
MH = 4
NMEM = 256
DFF = 8192
QKV = 4608
TOK = 1536
IN_A = 6680
IN_B = 3584
SEQ = 4096
PAST = 2048
DSEQ = 32
NSEQ = 4
EPS = 1e-6
SCALE = float(HD ** -0.5)
NP = 512
NS = NSEQ * DSEQ
WCOLS = 256
NW = 3
SLAB = 1024
AR_H = 0
AR_TMP = 16 * 1024
AR_BIG = AR_TMP + 40 * 1024
AR_END = AR_BIG + 64 * 1024


def _prod(t):
    r = 1
    for x in t:
        r *= x
    return r


class V:
    def __init__(self, k, off, shape, dt):
        self.k = k
        self.off = off
        self.shape = tuple(shape)
        self.esz = 2 if dt == BF16 else 4
        n = _prod(shape)
        self.nbytes = n * self.esz
        assert off % 4 == 0 and self.nbytes % 4 == 0
        assert off + self.nbytes <= AR_END, (off, self.nbytes)
        ap = k.arena[:, off // 4:(off + self.nbytes) // 4]
        if dt == BF16:
            ap = ap.bitcast(BF16)
        if len(shape) == 2:
            ap = ap.rearrange("p (a b) -> p a b", b=shape[1])
        elif len(shape) == 3:
            ap = ap.rearrange("p (a b c) -> p a b c", b=shape[1], c=shape[2])
        self.ap = ap

    def r(self, i=None, j=None):
        if i is None:
            lo, hi = 0, self.nbytes
        else:
            st = _prod(self.shape[1:]) * self.esz
            lo = i * st
            hi = (i + 1 if j is None else j) * st
        return self.k.slabs(self.off + lo, self.off + hi)


class TileCfg:
    def __init__(self, kind, N, L, t=0):
        self.kind = kind
        self.N = N
        self.L = L
        self.nch = N // L
        self.t = t
        self.nsq = 1 if kind == 'p' else NSEQ


class K:
    def __init__(self, nc, es, ntp=8, do_sample=True, nlayers=4):
        self.nc, self.es = nc, es
        self.P = Prog(nc, es)
        self.ntp = ntp
        self.do_sample = do_sample
        self.nlayers = nlayers
        self._names = 0
        P = self.P
        din = lambda n, sh: nc.dram_tensor(n, sh, F32, kind="ExternalInput").ap()
        dout = lambda n, sh: nc.dram_tensor(n, sh, F32, kind="ExternalOutput").ap()
        self.i = dict(
            xp=din("xp", [SEQ, D]), xs=din("xs", [NS, D]),
            sg=din("sg", [2, NSEQ, H, HD, HD]), sc=din("sc", [2, NSEQ * 3, QKV]),
            ck=din("ck", [NSEQ, PAST, TOK]), cv=din("cv", [NSEQ, PAST, TOK]),
            clf=din("clf", [NSEQ, PAST, H]),
            cmk=din("cmk", [4, NSEQ, NMEM, 512]), cmv=din("cmv", [4, NSEQ, NMEM, 512]),
            mp=din("mp", [NMEM, D]),
            g_pre=din("g_pre", [64, 128]), g_post=din("g_post", [64, 128]),
            g_mpre=din("g_mpre", [64, 128]), g_mpost=din("g_mpost", [64, 128]),
            g_mem=din("g_mem", [64, 128]), g_kv=din("g_kv", [16, 128]),
            convw=din("convw", [288, 128]), a_log=din("a_log", [24]), dt_bias=din("dt_bias", [24]),
            gdn_norm=din("gdn_norm", [2, 128]), b_f=din("b_f", [12]),
            w_in_a=din("w_in_a", [2, D, IN_A]), w_in_b=din("w_in_b", [2, D, IN_B]),
            w_kvf=din("w_kvf", [D, 3084]), w_mem=din("w_mem", [4, D, 1024]),
            w_o=din("w_o", [4, D, D]), w_up=din("w_up", [4, D, DFF]), w_down=din("w_down", [4, DFF, D]),
            cst=din("cst", [128, 640]),
        )
        self.o = dict(
            yp=dout("yp", [SEQ, D]), ys=dout("ys", [NS, D]),
            pg=dout("pg", [2, H, HD, HD]), pc=dout("pc", [2 * 3, QKV]),
            pk=dout("pk", [SEQ, TOK]), pv=dout("pv", [SEQ, TOK]), plf=dout("plf", [SEQ, H]),
            pmk=dout("pmk", [4, NMEM, 512]), pmv=dout("pmv", [4, NMEM, 512]),
            sgo=dout("sgo", [2, NSEQ, H, HD, HD]), sco=dout("sco", [2 * NSEQ * 3, QKV]),
            sk=dout("sk", [NS, TOK]), sv=dout("sv", [NS, TOK]), slf=dout("slf", [NS, H]),
        )
        self.kscr = nc.dram_tensor("kscr", [128, H, SEQ], BF16).ap()
        self.vscr = nc.dram_tensor("vscr", [SEQ, TOK], BF16).ap()
        self.mkscr = nc.dram_tensor("mkscr", [4, 128, MH, NMEM], BF16).ap()
        self.mvscr = nc.dram_tensor("mvscr", [4, NMEM, 512], BF16).ap()
        self.r_kscr = [Res("kscr%d" % i) for i in range(SEQ // NP)]
        self.r_vscr = [Res("vscr%d" % i) for i in range(SEQ // NP)]
        self.r_mscr = [Res("mscr%d" % i) for i in range(4)]

        T = self.T
        self.arena = T("arena", [128, AR_END // 4])
        self.slab = [Res("slab%d" % i) for i in range(AR_END // SLAB)]
        self.xT = T("xT", [128, KC, NP])
        self.xr = [Res("xT%d" % i) for i in range(KC)]
        self.wring = [T("wr%d" % i, [128, KC, WCOLS], BF16) for i in range(NW)]
        self.wres = [Res("wr%d" % i) for i in range(NW)]
        self.wi = 0
        self.wsm = T("wsm", [128, KC, 24], BF16)
        self.r_wsm = Res("wsm")
        self.pb = [es.enter_context(nc.psum_tensor("pb%d" % i, [128, 512], F32)) for i in range(8)]
        self.rpb = [Res("pb%d" % i, True) for i in range(8)]
        self.pinned = set()
        self.bi = 0
        self.cst = T("cst_s", [128, 640])
        self.r_cst = Res("cst")
        self.cstb = T("cstb", [128, 384], BF16)
        self.r_cstb = Res("cstb")
        self.ident = self.cst[:, 0:128]
        self.ones = self.cst[:, 128:256]
        self.triu = self.cst[:, 256:384]
        self.triusn = self.cst[:, 384:512]
        self.elast = self.cst[:, 512:640]
        self.identb = self.cstb[:, 0:128]
        self.onesb = self.cstb[:, 128:256]
        self.triub = self.cstb[:, 256:384]
        self.gains = T("gains", [128, 6, 64])
        self.r_gains = Res("gains")
        self.convw = T("convw_s", [128, 288])
        self.gdng = T("gdng", [128, 2])
        self.vecs = T("vecs", [128, 60])
        self.nA = T("nA", [128, 24])
        self.r_misc = Res("misc")
        self.S = T("S", [128, 2, H, HD])
        self.r_S = [[Res("S%d_%d" % (l, h)) for h in range(H)] for l in range(2)]
        self.chist = T("chist", [128, 2, 36, 3])
        self.r_chist = [Res("chist%d" % l) for l in range(2)]
        self.Cp = T("Cp", [128, SEQ // 128, H])
        self.r_Cp = Res("Cp")
        self.Cs = T("Cs", [128, PAST // 128 + 1, NSEQ * H])
        self.r_Cs = Res("Cs")
        self.sS = [T("sS%d" % i, [128, HD]) for i in range(2)]
        self.r_sS = [Res("sS%d" % i) for i in range(2)]
        self.sSi = 0
        self.cendp = T("cendp", [128, H])
        self.r_cend = Res("cend")

    def T(self, name, shape, dt=F32):
        return self.es.enter_context(self.nc.sbuf_tensor(name, shape, dt))

    def slabs(self, lo, hi):
        return self.slab[lo // SLAB:(hi + SLAB - 1) // SLAB]

    def bank(self):
        for _ in range(8):
            b = self.bi
            self.bi = (self.bi + 1) % 8
            if b not in self.pinned:
                return b
        raise RuntimeError("no psum bank")

    def pin(self):
        b = self.bank()
        self.pinned.add(b)
        return b

    def unpin(self, b):
        self.pinned.discard(b)

    def tmp_reset(self, base=AR_TMP):
        self.tp = base

    def tmp(self, shape, dt=F32):
        v = V(self, self.tp, shape, dt)
        self.tp += (v.nbytes + SLAB - 1) // SLAB * SLAB
        assert self.tp <= AR_BIG, "tmp overflow %d" % self.tp
        return v

    def A(self, out, in_, func, rd, wr, bias=None, scale=None):
        kw = {}
        if bias is not None:
            kw["bias"] = bias
        if scale is not None:
            kw["scale"] = scale
        self.P.op("act", lambda e: e.activation(out=out, in_=in_, func=func, **kw), rd, wr)

    def TT(self, out, in0, in1, op, rd, wr):
        self.P.op("dve", lambda e: e.tensor_tensor(out=out, in0=in0, in1=in1, op=op), rd, wr)

    def TS(self, out, in0, s1, s2, op0, op1, rd, wr):
        if s2 is None:
            self.P.op("dve", lambda e: e.tensor_scalar(out=out, in0=in0, scalar1=s1, scalar2=None, op0=op0), rd, wr)
        else:
            self.P.op("dve", lambda e: e.tensor_scalar(out=out, in0=in0, scalar1=s1, scalar2=s2, op0=op0, op1=op1), rd, wr)

    def STT(self, out, in0, sc, in1, op0, op1, rd, wr):
        self.P.op("dve", lambda e: e.scalar_tensor_tensor(out=out, in0=in0, scalar=sc, in1=in1, op0=op0, op1=op1), rd, wr)

    def CP(self, out, in_, rd, wr):
        self.P.op("dve", lambda e: e.tensor_copy(out=out, in_=in_), rd, wr)

    def REC(self, out, in_, rd, wr):
        self.P.op("dve", lambda e: e.reciprocal(out=out, in_=in_), rd, wr)

    def MS(self, out, val, wr):
        self.P.op("dve", lambda e: e.memset(out, val), (), wr)

    def MM(self, out, lhsT, rhs, st, sp, rd, wr, skip=False):
        if skip:
            self.P.op("pe", lambda e: e.matmul(out, lhsT, rhs, start=st, stop=sp, skip_group_check=True), rd, wr)
        else:
            self.P.op("pe", lambda e: e.matmul(out, lhsT, rhs, start=st, stop=sp), rd, wr)

    def TR(self, out, in_, ident, rd, wr):
        self.P.op("pe", lambda e: e.transpose(out=out, in_=in_, identity=ident), rd, wr)

    def DMA(self, out, in_, semres, rd, wr, eng="sp"):
        self.P.dma(eng, lambda e: e.dma_start(out=out, in_=in_), semres, rd, wr)

    def pbf(self, b):
        return self.pb[b][:].bitcast(BF16)

    def wload(self, src, ncols):
        sl = self.wi
        self.wi = (self.wi + 1) % NW
        dst = self.wring[sl][:, :, 0:ncols]
        self.DMA(dst, src.rearrange("(kc p) c -> p kc c", p=128), self.wres[sl], [], [self.wres[sl]], eng="pool")
        return sl

    def linear_fm(self, W, c0, ncols, act, N, epi, Kdim=D):
        nkb = Kdim // D
        for cb in range(0, ncols, WCOLS):
            nc_ = min(WCOLS, ncols - cb)
            chunks = [(m0, min(128, nc_ - m0)) for m0 in range(0, nc_, 128)]
            banks = [self.pin() for _ in chunks] if nkb > 1 else None
            for kb in range(nkb):
                sl = self.wload(W[kb * D:(kb + 1) * D, c0 + cb:c0 + cb + nc_], nc_)
                for ci, (m0, rows) in enumerate(chunks):
                    b = banks[ci] if banks else self.bank()
                    for kc in range(KC):
                        kk = kb * KC + kc
                        self.MM(self.pb[b][0:rows, 0:N], self.wring[sl][:, kc, m0:m0 + rows], act.ap[:, kk, 0:N],
                                kk == 0, kk == nkb * KC - 1, [self.wres[sl]] + act.r(kk), [self.rpb[b]])
                    if kb == nkb - 1:
                        epi((cb + m0) // 128, rows, self.pb[b][0:rows, 0:N], b)
            if banks:
                for b in banks:
                    self.unpin(b)

    def linear_tm(self, W, c0, ncols, act, toks, epi):
        for cb in range(0, ncols, WCOLS):
            nc_ = min(WCOLS, ncols - cb)
            sl = self.wload(W[:, c0 + cb:c0 + cb + nc_], nc_)
            for ti, (t0, M) in enumerate(toks):
                b = self.bank()
                for kc in range(KC):
                    self.MM(self.pb[b][0:M, 0:nc_], act.ap[:, kc, t0:t0 + M], self.wring[sl][:, kc, 0:nc_],
                            kc == 0, kc == KC - 1, [self.wres[sl]] + act.r(kc), [self.rpb[b]])
                epi(ti, cb, nc_, self.pb[b][0:M, 0:nc_], b)

    def stats(self, srcs, N, Dn, rstd, base_rows=128):
        sq = [self.tmp([N]) for _ in range(2)]
        b = self.pin()
        n = len(srcs)
        for i, (ap, rl) in enumerate(srcs):
            q = sq[i % 2]
            self.A(q.ap[:, 0:N], ap, AF.Square, rl, q.r())
            self.MM(self.pb[b][:, 0:N], self.ones, q.ap[:, 0:N], i == 0, i == n - 1, [self.r_cst] + q.r(), [self.rpb[b]])
        self.A(rstd.ap[:, 0:N], self.pb[b][:, 0:N], AF.Sqrt, [self.rpb[b]], rstd.r(), bias=EPS, scale=1.0 / Dn)
        self.REC(rstd.ap[:, 0:N], rstd.ap[:, 0:N], rstd.r(), rstd.r())
        self.unpin(b)

    def prenorm(self, gi, gl, dst, N):
        self.tmp_reset()
        rstd = self.tmp([N])
        self.stats([(self.xT[:, kc, 0:N], [self.xr[kc]]) for kc in range(KC)], N, D, rstd)
        for kc in range(KC):
            self.STT(dst.ap[:, kc, 0:N], self.xT[:, kc, 0:N], self.gains[:, gi, gl * 16 + kc:gl * 16 + kc + 1],
                     rstd.ap[:, 0:N], ALU.mult, ALU.mult, [self.xr[kc], self.r_gains] + rstd.r(), dst.r(kc))

    def postnorm_res(self, y, gi, gl, N, tmpbase):
        self.tmp_reset(tmpbase)
        rstd = self.tmp([N])
        self.stats([(y.ap[:, kc, 0:N], y.r(kc)) for kc in range(KC)], N, D, rstd)
        for kc in range(KC):
            self.STT(y.ap[:, kc, 0:N], y.ap[:, kc, 0:N], self.gains[:, gi, gl * 16 + kc:gl * 16 + kc + 1],
                     rstd.ap[:, 0:N], ALU.mult, ALU.mult, y.r(kc) + [self.r_gains] + rstd.r(), y.r(kc))
            self.TT(self.xT[:, kc, 0:N], self.xT[:, kc, 0:N], y.ap[:, kc, 0:N], ALU.add,
                    [self.xr[kc]] + y.r(kc), [self.xr[kc]])

    def load_T(self, src, R, dst, wr):
        self.tmp_reset()
        stg = self.tmp([128])
        self.DMA(stg.ap[0:R, :], src, stg.r()[0], [], stg.r())
        b = self.bank()
        self.TR(self.pb[b][:, 0:R], stg.ap[0:R, :], self.ident[0:R, 0:R], stg.r() + [self.r_cst], [self.rpb[b]])
        self.A(dst, self.pb[b][:, 0:R], AF.Copy, [self.rpb[b]], wr)

    def prologue(self):
        i = self.i
        self.DMA(self.cst[:], i["cst"], self.r_cst, [], [self.r_cst])
        self.A(self.cstb[:], self.cst[:, 0:384], AF.Copy, [self.r_cst], [self.r_cstb])
        for gi, nm in enumerate(["g_pre", "g_post", "g_mpre", "g_mpost", "g_mem"]):
            self.load_T(i[nm], 64, self.gains[:, gi, :], [self.r_gains])
        self.load_T(i["g_kv"], 16, self.gains[:, 5, 0:16], [self.r_gains])
        for j in range(3):
            self.load_T(i["convw"][j * 96:(j + 1) * 96, :], 96, self.convw[:, j * 96:(j + 1) * 96], [self.r_misc])
        self.load_T(i["gdn_norm"], 2, self.gdng[:, :], [self.r_misc])
        self.DMA(self.vecs[:, 0:24], i["a_log"].partition_broadcast(128), self.r_misc, [], [self.r_misc])
        self.DMA(self.vecs[:, 24:48], i["dt_bias"].partition_broadcast(128), self.r_misc, [], [self.r_misc])
        self.DMA(self.vecs[:, 48:60], i["b_f"].partition_broadcast(128), self.r_misc, [], [self.r_misc])
        self.A(self.nA[:], self.vecs[:, 0:24], AF.Exp, [self.r_misc], [self.r_misc])
        self.TS(self.nA[:], self.nA[:], -1.0, None, ALU.mult, None, [self.r_misc], [self.r_misc])
        for l in range(2):
            for h in range(H):
                self.MS(self.S[:, l, h, :], 0.0, [self.r_S[l][h]])
            self.MS(self.chist[:, l, :, :], 0.0, [self.r_chist[l]])

    def memory_kv(self):
        N = NMEM
        mT = V(self, AR_BIG, [KC, N], F32)
        hm = V(self, AR_H, [KC, N], BF16)
        stg = [V(self, AR_BIG + 16384 + j * 8192, [D], F32) for j in range(2)]
        ost = [V(self, AR_BIG + 32768 + j * 2048, [512], F32) for j in range(4)]
        osb = [V(self, AR_BIG + 40960 + j * 1024, [512], BF16) for j in range(4)]
        kst = V(self, AR_BIG + 45056, [MH, N], BF16)
        for tb in range(2):
            self.DMA(stg[tb].ap[:, :], self.i["mp"][tb * 128:(tb + 1) * 128, :], stg[tb].r()[0], [], stg[tb].r())
            for g in range(4):
                b = self.bank()
                for j in range(4):
                    kc = g * 4 + j
                    self.TR(self.pb[b][:, j * 128:(j + 1) * 128], stg[tb].ap[:, kc * 128:(kc + 1) * 128], self.ident,
                            stg[tb].r() + [self.r_cst], [self.rpb[b]])
                self.A(mT.ap[:, g * 4:(g + 1) * 4, tb * 128:(tb + 1) * 128],
                       self.pb[b][:, :].rearrange("p (j t) -> p j t", t=128), AF.Copy, [self.rpb[b]], mT.r(g * 4, g * 4 + 4))
        self.tmp_reset()
        rstd = self.tmp([N])
        self.stats([(mT.ap[:, kc, :], mT.r(kc)) for kc in range(KC)], N, D, rstd)
        oc = [0]
        for l in range(4):
            for kc in range(KC):
                self.STT(hm.ap[:, kc, :], mT.ap[:, kc, :], self.gains[:, 4, l * 16 + kc:l * 16 + kc + 1], rstd.ap[:, :],
                         ALU.mult, ALU.mult, mT.r(kc) + [self.r_gains] + rstd.r(), hm.r(kc))
            W = self.i["w_mem"][l]

            def epi_k(m, rows, ps, b, l=l):
                self.A(kst.ap[:, m, :], ps, AF.Copy, [self.rpb[b]], kst.r(m))
            self.linear_fm(W, 0, 512, hm, N, epi_k)
            self.DMA(self.mkscr[l], kst.ap[:, :, :], kst.r()[0], kst.r(), [self.r_mscr[l]])

            def epi_tm(ti, cb, ncb, ps, b, l=l, which=0):
                j = oc[0] % 4
                oc[0] += 1
                dst = self.o["pmk" if which == 0 else "pmv"]
                self.A(ost[j].ap[:, 0:ncb], ps, AF.Copy, [self.rpb[b]], ost[j].r())
                self.DMA(dst[l, ti * 128:(ti + 1) * 128, cb:cb + ncb], ost[j].ap[:, 0:ncb], ost[j].r()[0], ost[j].r(), [])
                if which == 1:
                    self.CP(osb[j].ap[:, 0:ncb], ost[j].ap[:, 0:ncb], ost[j].r(), osb[j].r())
                    self.DMA(self.mvscr[l, ti * 128:(ti + 1) * 128, cb:cb + ncb], osb[j].ap[:, 0:ncb], osb[j].r()[0],
                             osb[j].r(), [self.r_mscr[l]])
            self.linear_tm(W, 0, 512, hm, [(0, 128), (128, 128)], lambda *a, l=l: epi_tm(*a, l=l, which=0))
            self.linear_tm(W, 512, 512, hm, [(0, 128), (128, 128)], lambda *a, l=l: epi_tm(*a, l=l, which=1))

    def load_xT(self, src, N):
        stg = [V(self, AR_BIG + j * 8192, [D], F32) for j in range(2)]
        for tb in range(N // 128):
            st = stg[tb % 2]
            self.DMA(st.ap[:, :], src[tb * 128:(tb + 1) * 128, :], st.r()[0], [], st.r())
            for g in range(4):
                b = self.bank()
                for j in range(4):
                    kc = g * 4 + j
                    self.TR(self.pb[b][:, j * 128:(j + 1) * 128], st.ap[:, kc * 128:(kc + 1) * 128], self.ident,
                            st.r() + [self.r_cst], [self.rpb[b]])
                self.A(self.xT[:, g * 4:(g + 1) * 4, tb * 128:(tb + 1) * 128],
                       self.pb[b][:, :].rearrange("p (j t) -> p j t", t=128), AF.Copy,
                       [self.rpb[b]], self.xr[g * 4:(g + 1) * 4])

    def store_y(self, dst, N):
        stg = [V(self, AR_BIG + j * 8192, [D], F32) for j in range(2)]
        for tb in range(N // 128):
            st = stg[tb % 2]
            for g in range(4):
                b = self.bank()
                for j in range(4):
                    kc = g * 4 + j
                    self.TR(self.pb[b][:, j * 128:(j + 1) * 128], self.xT[:, kc, tb * 128:(tb + 1) * 128], self.ident,
                            [self.xr[kc], self.r_cst], [self.rpb[b]])
                self.A(st.ap[:, g * 512:(g + 1) * 512], self.pb[b][:, :], AF.Copy, [self.rpb[b]], st.r())
            self.DMA(dst[tb * 128:(tb + 1) * 128, :], st.ap[:, :], st.r()[0], st.r(), [])

    def mlp(self, l, N):
        hf = V(self, AR_H, [KC, N], BF16)
        self.prenorm(2, l, hf, N)
        hid = V(self, AR_BIG, [64, N], BF16)
        self.tmp_reset(AR_TMP + 16384)
        rl = [self.tmp([N]) for _ in range(3)]
        cnt = [0]

        def epi_up(m, rows, ps, b):
            r = rl[cnt[0] % 3]
            cnt[0] += 1
            self.A(r.ap[:, 0:N], ps, AF.Relu, [self.rpb[b]], r.r())
            self.TT(hid.ap[:, m, 0:N], r.ap[:, 0:N], r.ap[:, 0:N], ALU.mult, r.r(), hid.r(m))
        self.linear_fm(self.i["w_up"][l], 0, DFF, hf, N, epi_up)
        y = V(self, AR_H, [KC, N], F32)

        def epi_dn(m, rows, ps, b):
            self.A(y.ap[:, m, 0:N], ps, AF.Copy, [self.rpb[b]], y.r(m))
        self.linear_fm(self.i["w_down"][l], 0, D, hid, N, epi_dn, Kdim=DFF)
        self.postnorm_res(y, 3, l, N, AR_H + KC * N * 4 if KC * N * 4 > 16384 else AR_TMP)

    def mem_attend(self, l, tc, mq, cat):
        N = tc.N
        kT = self.tmp([MH, NMEM], BF16)
        vv = self.tmp([2, 512], BF16)
        pT = [self.tmp([N], BF16) for _ in range(2)]
        rden = self.tmp([N])
        nq = N // tc.nsq
        for sq in range(tc.nsq):
            q0 = sq * nq
            if tc.kind == 'p':
                self.DMA(kT.ap[:, :, :], self.mkscr[l], kT.r()[0], [self.r_mscr[l]], kT.r())
                self.DMA(vv.ap[:, :, :], self.mvscr[l].rearrange("(b p) c -> p b c", p=128), vv.r()[0], [self.r_mscr[l]], vv.r())
            else:
                stg = self.tmp_s
                for nb in range(2):
                    self.DMA(stg.ap[:, 0:512], self.i["cmk"][l, sq, nb * 128:(nb + 1) * 128, :], stg.r()[0], [], stg.r())
                    b = self.bank()
                    for hm in range(MH):
                        self.TR(self.pb[b][:, hm * 128:(hm + 1) * 128], stg.ap[:, hm * 128:(hm + 1) * 128], self.ident,
                                stg.r() + [self.r_cst], [self.rpb[b]])
                    self.A(kT.ap[:, :, nb * 128:(nb + 1) * 128], self.pb[b][:, :].rearrange("p (h n) -> p h n", n=128),
                           AF.Copy, [self.rpb[b]], kT.r())
                    self.DMA(stg.ap[:, 0:512], self.i["cmv"][l, sq, nb * 128:(nb + 1) * 128, :], stg.r()[0], [], stg.r())
                    self.A(vv.ap[:, nb, :], stg.ap[:, 0:512], AF.Copy, stg.r(), vv.r())
            for hm in range(MH):
                bo, bd = self.pin(), self.pin()
                for nb in range(2):
                    b = self.bank()
                    self.MM(self.pb[b][:, 0:nq], kT.ap[:, hm, nb * 128:(nb + 1) * 128], mq.ap[:, hm, q0:q0 + nq], True, True,
                            kT.r() + mq.r(hm), [self.rpb[b]])
                    p = pT[nb]
                    self.A(p.ap[:, 0:nq], self.pb[b][:, 0:nq], AF.Exp, [self.rpb[b]], p.r(), scale=SCALE)
                    self.MM(self.pb[bo][:, 0:nq], vv.ap[:, nb, hm * 128:(hm + 1) * 128], p.ap[:, 0:nq], nb == 0, nb == 1,
                            vv.r() + p.r(), [self.rpb[bo]])
                    self.MM(self.pb[bd][:, 0:nq], self.onesb, p.ap[:, 0:nq], nb == 0, nb == 1,
                            [self.r_cstb] + p.r(), [self.rpb[bd]])
                self.REC(rden.ap[:, 0:nq], self.pb[bd][:, 0:nq], [self.rpb[bd]], rden.r())
                self.TT(cat.ap[:, H + hm, q0:q0 + nq], self.pb[bo][:, 0:nq], rden.ap[:, 0:nq], ALU.mult,
                        [self.rpb[bo]] + rden.r(), cat.r(H + hm))
                self.unpin(bo)
                self.unpin(bd)

    def out_proj(self, l, tc, cat):
        N = tc.N
        y = V(self, AR_BIG, [KC, N], F32)

        def epi(m, rows, ps, b):
            self.A(y.ap[:, m, 0:N], ps, AF.Copy, [self.rpb[b]], y.r(m))
        self.linear_fm(self.i["w_o"][l], 0, D, cat, N, epi)
        self.postnorm_res(y, 1, l, N, AR_TMP)

    def gdn_layer(self, l, tc):
        N, L, nch = tc.N, tc.L, tc.nch
        nsq = tc.nsq
        hT = V(self, AR_H, [KC, N], BF16)
        self.prenorm(0, l, hT, N)
        qn = V(self, AR_BIG, [H, N], BF16)
        kn = V(self, AR_BIG + 12288, [H, N], BF16)
        vT = V(self, AR_BIG + 24576, [H, N], BF16)
        zT = V(self, AR_BIG + 36864, [H, N], BF16)
        mq = V(self, AR_BIG + 49152, [MH, N], BF16)
        sm = AR_BIG + 53248
        ab = V(self, sm, [nch, 24], F32)
        gam = V(self, sm + 1024, [nch, H], F32)
        bet = V(self, sm + 1536, [nch, H], F32)
        Gc = V(self, sm + 2048, [nch, H], F32)
        eG = V(self, sm + 2560, [nch, H], F32)
        eGm = V(self, sm + 3072, [nch, H], F32)
        eGL = V(self, sm + 3584, [nch, H], F32)
        t1 = V(self, sm + 4096, [nch, H], F32)
        t2 = V(self, sm + 4608, [nch, H], F32)
        W = self.i["w_in_a"][l]
        seqw = N // nsq
        self.tmp_reset()
        cbs = [self.tmp([nsq, seqw + 3]) for _ in range(2)]
        acc = self.tmp([N])
        qk = self.tmp([N])
        rs = self.tmp([N])
        sqv = self.tmp([N])
        hist = self.chist[:, l, :, :] if tc.kind == 'p' else None
        if tc.kind == 's':
            hs = V(self, sm + 5120, [36, nsq * 3], F32)
            st = self.tmp([QKV])
            R = nsq * 3
            self.DMA(st.ap[0:R, :], self.i["sc"][l], st.r()[0], [], st.r())
            for g in range(9):
                b = self.bank()
                for j in range(4):
                    jj = g * 4 + j
                    self.TR(self.pb[b][:, j * R:(j + 1) * R], st.ap[0:R, jj * 128:(jj + 1) * 128], self.ident[0:R, 0:R],
                            st.r() + [self.r_cst], [self.rpb[b]])
                self.A(hs.ap[:, g * 4:(g + 1) * 4, :], self.pb[b][:, 0:4 * R].rearrange("p (j r) -> p j r", r=R), AF.Copy,
                       [self.rpb[b]], hs.r())
        newh = V(self, sm + 7168, [36, nsq * 3], F32) if tc.kind == 's' else None
        cw = self.convw
        ci = [0]

        def epi_qkv(m, rows, ps, b):
            cb = cbs[ci[0] % 2]
            ci[0] += 1
            if tc.kind == 'p':
                self.CP(cb.ap[:, 0, 0:3], self.chist[:, l, m, :], [self.r_chist[l]], cb.r())
            else:
                self.CP(cb.ap[:, :, 0:3], hs.ap[:, m, :].rearrange("p (s t) -> p s t", t=3), hs.r(), cb.r())
            self.A(cb.ap[:, :, 3:3 + seqw], ps.rearrange("p (s t) -> p s t", t=seqw), AF.Copy, [self.rpb[b]], cb.r())
            if tc.kind == 'p':
                self.CP(self.chist[:, l, m, :], cb.ap[:, 0, seqw:seqw + 3], cb.r(), [self.r_chist[l]])
            else:
                self.CP(newh.ap[:, m, :].rearrange("p (s t) -> p s t", t=3), cb.ap[:, :, seqw:seqw + 3], cb.r(), newh.r())
            a3 = acc.ap[:, 0:N].rearrange("p (s t) -> p s t", t=seqw)
            wcol = lambda tap: cw[:, l * 144 + tap * 36 + m:l * 144 + tap * 36 + m + 1]
            self.TS(a3, cb.ap[:, :, 3:3 + seqw], wcol(3), None, ALU.mult, None, cb.r() + [self.r_misc], acc.r())
            for tap in range(3):
                self.STT(a3, cb.ap[:, :, tap:tap + seqw], wcol(tap), a3, ALU.mult, ALU.add,
                         cb.r() + [self.r_misc] + acc.r(), acc.r())
            kind, h = m // H, m % H
            if kind == 2:
                self.A(vT.ap[:, h, 0:N], acc.ap[:, 0:N], AF.Silu, acc.r(), vT.r(h))
                return
            self.A(qk.ap[:, 0:N], acc.ap[:, 0:N], AF.Silu, acc.r(), qk.r())
            self.A(sqv.ap[:, 0:N], qk.ap[:, 0:N], AF.Square, qk.r(), sqv.r())
            b2 = self.bank()
            self.MM(self.pb[b2][:, 0:N], self.ones, sqv.ap[:, 0:N], True, True, [self.r_cst] + sqv.r(), [self.rpb[b2]])
            self.A(rs.ap[:, 0:N], self.pb[b2][:, 0:N], AF.Sqrt, [self.rpb[b2]], rs.r(), bias=EPS, scale=1.0)
            self.REC(rs.ap[:, 0:N], rs.ap[:, 0:N], rs.r(), rs.r())
            dst = qn if kind == 0 else kn
            self.STT(dst.ap[:, h, 0:N], qk.ap[:, 0:N], SCALE if kind == 0 else 1.0, rs.ap[:, 0:N], ALU.mult, ALU.mult,
                     qk.r() + rs.r(), dst.r(h))
        self.linear_fm(W, 0, QKV, hT, N, epi_qkv)

        def epi_z(m, rows, ps, b):
            self.A(zT.ap[:, m, 0:N], ps, AF.Silu, [self.rpb[b]], zT.r(m))
        self.linear_fm(W, QKV, TOK, hT, N, epi_z)

        def epi_mq(m, rows, ps, b):
            self.A(mq.ap[:, m, 0:N], ps, AF.Copy, [self.rpb[b]], mq.r(m))
        self.linear_fm(W, QKV + TOK + 24, 512, hT, N, epi_mq)
        o1 = QKV + TOK
        self.DMA(self.wsm[:, :, :], W[:, o1:o1 + 24].rearrange("(kc p) c -> p kc c", p=128), self.r_wsm, [], [self.r_wsm], eng="pool")
        bab = self.bank()
        for c in range(nch):
            for kc in range(KC):
                self.MM(self.pb[bab][0:L, c * 24:(c + 1) * 24], hT.ap[:, kc, c * L:(c + 1) * L], self.wsm[:, kc, :],
                        kc == 0, kc == KC - 1, hT.r(kc) + [self.r_wsm], [self.rpb[bab]])
        self.A(ab.ap[0:L, :, :], self.pb[bab][0:L, 0:nch * 24].rearrange("p (c f) -> p c f", f=24), AF.Copy, [self.rpb[bab]], ab.r())
        self.conv_state_out(l, tc, newh)
        bc = lambda ap: ap.unsqueeze(1).to_broadcast([L, nch, H])
        av, bv = ab.ap[0:L, :, 0:H], ab.ap[0:L, :, H:2 * H]
        x_, ax, g_, be = t1.ap[0:L], t2.ap[0:L], gam.ap[0:L], bet.ap[0:L]
        self.TT(x_, av, bc(self.vecs[0:L, 24 + l * H:24 + (l + 1) * H]), ALU.add, ab.r() + [self.r_misc], t1.r())
        self.STT(ax, x_, -1.0, x_, ALU.mult, ALU.max, t1.r(), t2.r())
        self.A(ax, ax, AF.Exp, t2.r(), t2.r(), scale=-1.0)
        self.A(ax, ax, AF.Ln, t2.r(), t2.r(), bias=1.0)
        self.STT(x_, x_, 0.0, ax, ALU.max, ALU.add, t1.r() + t2.r(), t1.r())
        self.TT(g_, x_, bc(self.nA[0:L, l * H:(l + 1) * H]), ALU.mult, t1.r() + [self.r_misc], gam.r())
        self.A(be, bv, AF.Sigmoid, ab.r(), bet.r())
        g2 = gam.ap[0:L].rearrange("p c h -> p (c h)")
        b1 = self.bank()
        self.MM(self.pb[b1][0:L, 0:nch * H], self.triu[0:L, 0:L], g2, True, True, [self.r_cst] + gam.r(), [self.rpb[b1]])
        self.A(Gc.ap[0:L].rearrange("p c h -> p (c h)"), self.pb[b1][0:L, 0:nch * H], AF.Copy, [self.rpb[b1]], Gc.r())
        self.A(eG.ap[0:L].rearrange("p c h -> p (c h)"), self.pb[b1][0:L, 0:nch * H], AF.Exp, [self.rpb[b1]], eG.r())
        b2 = self.bank()
        self.MM(self.pb[b2][:, 0:nch * H], self.ones[0:L, :], g2, True, True, [self.r_cst] + gam.r(), [self.rpb[b2]])
        self.A(eGL.ap.rearrange("p c h -> p (c h)"), self.pb[b2][:, 0:nch * H], AF.Exp, [self.rpb[b2]], eGL.r())
        self.TT(eGm.ap[0:L].rearrange("p c h -> p (c h)"), self.pb[b2][0:L, 0:nch * H], Gc.ap[0:L].rearrange("p c h -> p (c h)"),
                ALU.subtract, [self.rpb[b2]] + Gc.r(), eGm.r())
        self.A(eGm.ap[0:L], eGm.ap[0:L], AF.Exp, eGm.r(), eGm.r())
        cat = V(self, AR_H, [KC, N], BF16)
        for h in range(H):
            self.gdn_head(l, tc, h, qn, kn, vT, zT, gam, bet, Gc, eG, eGm, eGL, cat)
        self.tmp_reset()
        self.tmp_s = self.tmp([512])
        self.mem_attend(l, tc, mq, cat)
        self.out_proj(l, tc, cat)

    def conv_state_out(self, l, tc, newh):
        if tc.kind == 'p':
            if tc.t != self.ntp - 1:
                return
            src = lambda j0, j1: self.chist[:, l, j0:j1, :]
            rr = [self.r_chist[l]]
            R, dst = 3, self.o["pc"][l * 3:(l + 1) * 3, :]
        else:
            src = lambda j0, j1: newh.ap[:, j0:j1, :]
            rr = newh.r()
            R, dst = 3 * NSEQ, self.o["sco"][l * 3 * NSEQ:(l + 1) * 3 * NSEQ, :]
        self.tmp_reset(AR_TMP + 16384)
        st = self.tmp([QKV])
        for j in range(36):
            b = self.bank()
            inp = self.chist[:, l, j, :] if tc.kind == 'p' else newh.ap[:, j, :]
            self.TR(self.pb[b][0:R, 0:128], inp, self.ident, rr + [self.r_cst], [self.rpb[b]])
            self.A(st.ap[0:R, j * 128:(j + 1) * 128], self.pb[b][0:R, 0:128], AF.Copy, [self.rpb[b]], st.r())
        self.DMA(dst, st.ap[0:R, :], st.r()[0], st.r(), [])

    def gdn_head(self, l, tc, h, qn, kn, vT, zT, gam, bet, Gc, eG, eGm, eGL, cat):
        N, L, nch = tc.N, tc.L, tc.nch
        self.tmp_reset()
        T_ = self.tmp
        gU = T_([nch, L])
        EG = T_([N])
        QgT = T_([N], BF16)
        dec = T_([nch, L])
        nNb = T_([nch, L])
        Wt = T_([nch, L])
        Xt = T_([nch, L])
        Wb = [[T_([nch, L], BF16) for _ in range(2)] for _ in range(2)]
        Xtb = T_([nch, L], BF16)
        attnT = T_([nch, L], BF16)
        Vtok = T_([nch, HD], BF16)
        Kg = T_([nch, HD], BF16)
        Kp = T_([nch, HD], BF16)
        nyw = T_([N], BF16)
        vnew = [T_([HD], BF16) for _ in range(2)]
        Sb = [T_([HD], BF16) for _ in range(2)]
        o2 = T_([N])
        rstd = T_([N])
        ot = T_([N])
        cstr = [self.r_cst]
        bcl = lambda v: v.ap[0:L, :, h:h + 1].to_broadcast([L, nch, L])
        bcd = lambda v: v.ap[0:L, :, h:h + 1].to_broadcast([L, nch, HD])
        mask_b = lambda m: m[0:L, 0:L].unsqueeze(1).to_broadcast([L, nch, L])
        self.TT(gU.ap[0:L], bcl(gam), mask_b(self.triu), ALU.mult, gam.r() + cstr, gU.r())
        bg = self.bank()
        self.MM(self.pb[bg][:, 0:N], self.ones[0:L, :], gU.ap[0:L].rearrange("p c i -> p (c i)"), True, True, cstr + gU.r(), [self.rpb[bg]])
        self.A(EG.ap[:, 0:N], self.pb[bg][:, 0:N], AF.Exp, [self.rpb[bg]], EG.r())
        self.TT(QgT.ap[:, 0:N], qn.ap[:, h, 0:N], EG.ap[:, 0:N], ALU.mult, qn.r(h) + EG.r(), QgT.r())
        self.TT(dec.ap[0:L], self.pb[bg][0:L, 0:N].rearrange("p (c i) -> p c i", i=L), bcl(Gc), ALU.subtract, [self.rpb[bg]] + Gc.r(), dec.r())
        self.TS(dec.ap[0:L], dec.ap[0:L], 0.0, None, ALU.min, None, dec.r(), dec.r())
        self.A(dec.ap[0:L], dec.ap[0:L], AF.Exp, dec.r(), dec.r())
        self.TT(nNb.ap[0:L], dec.ap[0:L], mask_b(self.triusn), ALU.mult, dec.r() + cstr, nNb.r())
        self.TT(nNb.ap[0:L], nNb.ap[0:L], bcl(bet), ALU.mult, nNb.r() + bet.r(), nNb.r())
        self.TT(dec.ap[0:L], dec.ap[0:L], mask_b(self.triu), ALU.mult, dec.r() + cstr, dec.r())
        bk, bq = self.bank(), self.bank()
        for c in range(nch):
            cs = slice(c * L, (c + 1) * L)
            self.MM(self.pb[bk][0:L, cs], kn.ap[:, h, cs], kn.ap[:, h, cs], True, True, kn.r(h), [self.rpb[bk]])
        for c in range(nch):
            cs = slice(c * L, (c + 1) * L)
            self.MM(self.pb[bq][0:L, cs], kn.ap[:, h, cs], qn.ap[:, h, cs], True, True, kn.r(h) + qn.r(h), [self.rpb[bq]])
        p3 = lambda b: self.pb[b][0:L, 0:N].rearrange("p (c i) -> p c i", i=L)
        self.TT(Wt.ap[0:L], p3(bk), nNb.ap[0:L], ALU.mult, [self.rpb[bk]] + nNb.r(), Wt.r())
        self.TT(attnT.ap[0:L], p3(bq), dec.ap[0:L], ALU.mult, [self.rpb[bq]] + dec.r(), attnT.r())
        self.TT(Xt.ap[0:L], Wt.ap[0:L], mask_b(self.ident), ALU.add, Wt.r() + cstr, Xt.r())
        cur = Wb[0]
        self.A(cur[0].ap[0:L], Wt.ap[0:L], AF.Copy, Wt.r(), cur[0].r())
        bt = self.bank()
        ptb = self.pbf(bt)
        for c in range(nch):
            self.TR(ptb[0:L, c * L:(c + 1) * L], cur[0].ap[0:L, c, :], self.identb[0:L, 0:L], cur[0].r() + [self.r_cstb], [self.rpb[bt]])
        self.A(cur[1].ap[0:L], ptb[0:L, 0:N].rearrange("p (c i) -> p c i", i=L), AF.Copy, [self.rpb[bt]], cur[1].r())
        self.A(Xtb.ap[0:L], Xt.ap[0:L], AF.Copy, Xt.r(), Xtb.r())
        nsteps = {64: 5, 32: 4}[L]
        for k in range(1, nsteps + 1):
            nxt = Wb[k % 2]
            last = k == nsteps
            b_p = self.bank()
            for c in range(nch):
                cs = slice(c * L, (c + 1) * L)
                self.MM(self.pb[b_p][0:L, cs], cur[0].ap[0:L, c, :], cur[1].ap[0:L, c, :], True, True, cur[0].r() + cur[1].r(), [self.rpb[b_p]])
            self.A(nxt[1].ap[0:L], p3(b_p), AF.Copy, [self.rpb[b_p]], nxt[1].r())
            if not last:
                b_t = self.bank()
                for c in range(nch):
                    cs = slice(c * L, (c + 1) * L)
                    self.MM(self.pb[b_t][0:L, cs], cur[1].ap[0:L, c, :], cur[0].ap[0:L, c, :], True, True, cur[0].r() + cur[1].r(), [self.rpb[b_t]])
                self.A(nxt[0].ap[0:L], p3(b_t), AF.Copy, [self.rpb[b_t]], nxt[0].r())
            b_x = self.bank()
            for c in range(nch):
                cs = slice(c * L, (c + 1) * L)
                self.MM(self.pb[b_x][0:L, cs], nxt[1].ap[0:L, c, :], Xtb.ap[0:L, c, :], True, True, nxt[1].r() + Xtb.r(), [self.rpb[b_x]])
            self.TT(Xt.ap[0:L], Xt.ap[0:L], p3(b_x), ALU.add, Xt.r() + [self.rpb[b_x]], Xt.r())
            self.A(Xtb.ap[0:L], Xt.ap[0:L], AF.Copy, Xt.r(), Xtb.r())
            cur = nxt
        bv_ = self.bank()
        pv = self.pbf(bv_)
        for c in range(nch):
            self.TR(pv[0:L, c * HD:(c + 1) * HD], vT.ap[:, h, c * L:(c + 1) * L], self.identb, vT.r(h) + [self.r_cstb], [self.rpb[bv_]])
        self.A(Vtok.ap[0:L], pv[0:L, 0:nch * HD].rearrange("p (c d) -> p c d", d=HD), AF.Copy, [self.rpb[bv_]], Vtok.r())
        bk_ = self.bank()
        pk = self.pbf(bk_)
        for c in range(nch):
            self.TR(pk[0:L, c * HD:(c + 1) * HD], kn.ap[:, h, c * L:(c + 1) * L], self.identb, kn.r(h) + [self.r_cstb], [self.rpb[bk_]])
        pk3 = pk[0:L, 0:nch * HD].rearrange("p (c d) -> p c d", d=HD)
        self.TT(Kg.ap[0:L], pk3, bcd(eG), ALU.mult, [self.rpb[bk_]] + eG.r(), Kg.r())
        self.TT(Kp.ap[0:L], pk3, bcd(eGm), ALU.mult, [self.rpb[bk_]] + eGm.r(), Kp.r())
        by = self.bank()
        for c in range(nch):
            cs = slice(c * L, (c + 1) * L)
            self.MM(self.pb[by][:, cs], Kg.ap[0:L, c, :], Xtb.ap[0:L, c, :], True, True, Kg.r() + Xtb.r(), [self.rpb[by]])
        self.A(nyw.ap[:, 0:N], self.pb[by][:, 0:N], AF.Copy, [self.rpb[by]], nyw.r(), scale=-1.0)
        bo = self.pin()
        for c in range(nch):
            cs = slice(c * L, (c + 1) * L)
            if tc.kind == 'p':
                S_ap, S_r = self.S[:, l, h, :], [self.r_S[l][h]]
                fresh = (c == 0)
            else:
                si = self.sSi
                self.sSi = (self.sSi + 1) % 2
                S_ap, S_r = self.sS[si][:, :], [self.r_sS[si]]
                self.DMA(S_ap, self.i["sg"][l, c, h], self.r_sS[si], [], S_r)
                fresh = True
            sb = Sb[c % 2]
            self.A(sb.ap[:, :], S_ap, AF.Copy, S_r, sb.r())
            vn = vnew[c % 2]
            b1 = self.bank()
            self.MM(self.pb[b1][0:L, 0:HD], Xtb.ap[0:L, c, :], Vtok.ap[0:L, c, :], True, False, Xtb.r() + Vtok.r(), [self.rpb[b1]])
            self.MM(self.pb[b1][0:L, 0:HD], nyw.ap[:, cs], sb.ap[:, :], False, True, nyw.r() + sb.r(), [self.rpb[b1]])
            self.A(vn.ap[0:L, :], self.pb[b1][0:L, 0:HD], AF.Copy, [self.rpb[b1]] + bet.r(), vn.r(), scale=bet.ap[0:L, c, h:h + 1])
            self.MM(self.pb[bo][:, cs], sb.ap[:, :], QgT.ap[:, cs], True, False, sb.r() + QgT.r(), [self.rpb[bo]])
            self.MM(self.pb[bo][:, cs], vn.ap[0:L, :], attnT.ap[0:L, c, :], False, True, vn.r() + attnT.r(), [self.rpb[bo]])
            b2 = self.bank()
            self.MM(self.pb[b2][:, 0:HD], Kp.ap[0:L, c, :], vn.ap[0:L, :], True, True, Kp.r() + vn.r(), [self.rpb[b2]])
            self.STT(S_ap, S_ap, eGL.ap[:, c, h:h + 1], self.pb[b2][:, 0:HD], ALU.mult, ALU.add, S_r + eGL.r() + [self.rpb[b2]], S_r)
            if tc.kind == 's':
                self.DMA(self.o["sgo"][l, c, h], S_ap, S_r[0], S_r, [])
            elif tc.t == self.ntp - 1 and c == nch - 1:
                self.DMA(self.o["pg"][l, h], S_ap, S_r[0], S_r, [])
        self.A(o2.ap[:, 0:N], self.pb[bo][:, 0:N], AF.Square, [self.rpb[bo]], o2.r())
        bs = self.bank()
        self.MM(self.pb[bs][:, 0:N], self.ones, o2.ap[:, 0:N], True, True, cstr + o2.r(), [self.rpb[bs]])
        self.A(rstd.ap[:, 0:N], self.pb[bs][:, 0:N], AF.Sqrt, [self.rpb[bs]], rstd.r(), bias=EPS, scale=1.0 / HD)
        self.REC(rstd.ap[:, 0:N], rstd.ap[:, 0:N], rstd.r(), rstd.r())
        self.TT(ot.ap[:, 0:N], self.pb[bo][:, 0:N], rstd.ap[:, 0:N], ALU.mult, [self.rpb[bo]] + rstd.r(), ot.r())
        self.unpin(bo)
        self.STT(cat.ap[:, h, 0:N], ot.ap[:, 0:N], self.gdng[:, l:l + 1], zT.ap[:, h, 0:N], ALU.mult, ALU.mult,
                 ot.r() + [self.r_misc] + zT.r(h), cat.r(h))

    def kv_share(self, tc):
        N = tc.N
        hT = V(self, AR_H, [KC, N], BF16)
        self.prenorm(5, 0, hT, N)
        W = self.i["w_kvf"]
        kst = V(self, AR_BIG if tc.kind == 'p' else AR_BIG + 40960, [H, N], BF16)
        ost = [V(self, AR_BIG + 16384 + j * 1024, [WCOLS], F32) for j in range(6)]
        osb = [V(self, AR_BIG + 24576 + j * 1024, [WCOLS], BF16) for j in range(6)]
        oc = [0]
        if tc.kind == 'p':
            toks = [(tb * 128, 128) for tb in range(N // 128)]
            row0 = tc.t * NP
            rows_of = lambda ti: slice(row0 + ti * 128, row0 + (ti + 1) * 128)
            M_of = lambda ti: 128
            ko, vo, lo = self.o["pk"], self.o["pv"], self.o["plf"]
        else:
            toks = [(sq * DSEQ, DSEQ) for sq in range(NSEQ)]
            rows_of = lambda ti: slice(ti * DSEQ, (ti + 1) * DSEQ)
            M_of = lambda ti: DSEQ
            ko, vo, lo = self.o["sk"], self.o["sv"], self.o["slf"]
            self.Vnew = V(self, AR_BIG + 44032, [NSEQ, TOK], BF16)
        self.KTnew = kst

        def epi_k(m, rows, ps, b):
            self.A(kst.ap[:, m, 0:N], ps, AF.Copy, [self.rpb[b]], kst.r(m))
        self.linear_fm(W, 0, TOK, hT, N, epi_k)
        if tc.kind == 'p':
            self.DMA(self.kscr[:, :, tc.t * NP:(tc.t + 1) * NP], kst.ap[:, :, :], kst.r()[0], kst.r(), [self.r_kscr[tc.t]])

        def epi_tm(ti, cb, ncb, ps, b, which=0):
            j = oc[0] % 6
            oc[0] += 1
            M = M_of(ti)
            self.A(ost[j].ap[0:M, 0:ncb], ps, AF.Copy, [self.rpb[b]], ost[j].r())
            self.DMA((ko if which == 0 else vo)[rows_of(ti), cb:cb + ncb], ost[j].ap[0:M, 0:ncb], ost[j].r()[0], ost[j].r(), [])
            if which == 1:
                if tc.kind == 'p':
                    self.CP(osb[j].ap[0:M, 0:ncb], ost[j].ap[0:M, 0:ncb], ost[j].r(), osb[j].r())
                    self.DMA(self.vscr[rows_of(ti), cb:cb + ncb], osb[j].ap[0:M, 0:ncb], osb[j].r()[0], osb[j].r(), [self.r_vscr[tc.t]])
                else:
                    self.CP(self.Vnew.ap[0:M, ti, cb:cb + ncb], ost[j].ap[0:M, 0:ncb], ost[j].r(), self.Vnew.r())
        self.linear_tm(W, 0, TOK, hT, toks, lambda *a: epi_tm(*a, which=0))
        self.linear_tm(W, TOK, TOK, hT, toks, lambda *a: epi_tm(*a, which=1))
        self.DMA(self.wsm[:, :, 0:H], W[:, 2 * TOK:2 * TOK + H].rearrange("(kc p) c -> p kc c", p=128), self.r_wsm, [], [self.r_wsm], eng="pool")
        self.tmp_reset()
        lf = self.tmp([len(toks), H])
        t1 = self.tmp([len(toks), H])
        t2 = self.tmp([len(toks), H])
        M = toks[0][1]
        nt = len(toks)
        b = self.bank()
        for ti, (t0, M_) in enumerate(toks):
            for kc in range(KC):
                self.MM(self.pb[b][0:M, ti * H:(ti + 1) * H], hT.ap[:, kc, t0:t0 + M], self.wsm[:, kc, 0:H], kc == 0, kc == KC - 1,
                        hT.r(kc) + [self.r_wsm], [self.rpb[b]])
        x_, ax = t1.ap[0:M], t2.ap[0:M]
        self.TT(x_, self.pb[b][0:M, 0:nt * H].rearrange("p (t h) -> p t h", h=H),
                self.vecs[0:M, 48:60].unsqueeze(1).to_broadcast([M, nt, H]), ALU.add, [self.rpb[b], self.r_misc], t1.r())
        self.STT(ax, x_, -1.0, x_, ALU.mult, ALU.max, t1.r(), t2.r())
        self.A(ax, ax, AF.Exp, t2.r(), t2.r(), scale=-1.0)
        self.A(ax, ax, AF.Ln, t2.r(), t2.r(), bias=1.0)
        self.STT(lf.ap[0:M], x_, 0.0, ax, ALU.min, ALU.subtract, t1.r() + t2.r(), lf.r())
        for ti in range(nt):
            self.DMA(lo[rows_of(ti), :], lf.ap[0:M, ti, :], lf.r()[0], lf.r(), [])
        return lf

    def cumsum_prompt(self, tc, lf):
        for ti in range(tc.N // 128):
            blk = tc.t * (NP // 128) + ti
            b = self.bank()
            first = blk == 0
            self.MM(self.pb[b][:, 0:H], self.triu, lf.ap[:, ti, :], True, first, [self.r_cst] + lf.r(), [self.rpb[b]])
            if not first:
                self.MM(self.pb[b][:, 0:H], self.elast, self.Cp[:, blk - 1, :], False, True, [self.r_cst, self.r_Cp], [self.rpb[b]])
            self.A(self.Cp[:, blk, :], self.pb[b][:, 0:H], AF.Copy, [self.rpb[b]], [self.r_Cp])
        blk = tc.t * (NP // 128) + tc.N // 128 - 1
        b = self.bank()
        self.MM(self.pb[b][:, 0:H], self.elast, self.Cp[:, blk, :], True, True, [self.r_cst, self.r_Cp], [self.rpb[b]])
        self.A(self.cendp[:, :], self.pb[b][:, 0:H], AF.Copy, [self.rpb[b]], [self.r_cend])

    def fox_in(self, l, tc):
        N = tc.N
        hT = V(self, AR_H, [KC, N], BF16)
        self.prenorm(0, l, hT, N)
        qT = V(self, AR_BIG, [H, N], BF16)
        sg = V(self, AR_BIG + 12288, [H, N], BF16)
        mq = V(self, AR_BIG + 24576, [MH, N], BF16)
        W = self.i["w_in_b"][l - 2]

        def epi_q(m, rows, ps, b):
            self.A(qT.ap[:, m, 0:N], ps, AF.Copy, [self.rpb[b]], qT.r(m))

        def epi_g(m, rows, ps, b):
            self.A(sg.ap[:, m, 0:N], ps, AF.Sigmoid, [self.rpb[b]], sg.r(m))

        def epi_mq(m, rows, ps, b):
            self.A(mq.ap[:, m, 0:N], ps, AF.Copy, [self.rpb[b]], mq.r(m))
        self.linear_fm(W, 0, TOK, hT, N, epi_q)
        self.linear_fm(W, TOK, TOK, hT, N, epi_g)
        self.linear_fm(W, 2 * TOK, 512, hT, N, epi_mq)
        return qT, sg, mq

    def fox_layer_p(self, l, tc):
        N = tc.N
        qT, sg, mq = self.fox_in(l, tc)
        cat = V(self, AR_H, [KC, N], BF16)
        self.tmp_reset()
        nkb = (tc.t + 1) * (NP // 128)
        biasK = self.tmp([nkb, H])
        Ksb = [self.tmp([2, NP], BF16) for _ in range(2)]
        Vsb = [self.tmp([NP // 128, 256], BF16) for _ in range(2)]
        pT = [self.tmp([N], BF16) for _ in range(3)]
        rden = self.tmp([N])
        ot = self.tmp([N])
        self.tmp_s = self.tmp([512])
        self.TT(biasK.ap[:, :, :], self.cendp[:, :].unsqueeze(1).to_broadcast([128, nkb, H]), self.Cp[:, 0:nkb, :], ALU.subtract,
                [self.r_cend, self.r_Cp], biasK.r())
        si = 0
        pi = 0
        for hg in range(H // 2):
            acc = [(self.pin(), self.pin()) for _ in range(2)]
            for sb_ in range(tc.t + 1):
                ks, vs = Ksb[si % 2], Vsb[si % 2]
                si += 1
                self.DMA(ks.ap[:, :, :], self.kscr[:, hg * 2:hg * 2 + 2, sb_ * NP:(sb_ + 1) * NP], ks.r()[0], [self.r_kscr[sb_]], ks.r())
                self.DMA(vs.ap[:, :, :], self.vscr[sb_ * NP:(sb_ + 1) * NP, hg * 256:(hg + 1) * 256].rearrange("(b p) c -> p b c", p=128),
                         vs.r()[0], [self.r_vscr[sb_]], vs.r())
                diag = sb_ == tc.t
                for hh in range(2):
                    h = hg * 2 + hh
                    bo, bd = acc[hh]
                    for kb in range(NP // 128):
                        kg = sb_ * (NP // 128) + kb
                        q0 = kb * 128 if diag else 0
                        nq = N - q0
                        b = self.bank()
                        self.MM(self.pb[b][:, 0:nq], ks.ap[:, hh, kb * 128:(kb + 1) * 128], qT.ap[:, h, q0:N], True, True,
                                ks.r() + qT.r(h), [self.rpb[b]])
                        p = pT[pi % 3]
                        pi += 1
                        self.A(p.ap[:, 0:nq], self.pb[b][:, 0:nq], AF.Exp, [self.rpb[b]] + biasK.r(), p.r(),
                               bias=biasK.ap[:, kg, h:h + 1], scale=SCALE)
                        if diag:
                            self.TT(p.ap[:, 0:128], p.ap[:, 0:128], self.triub, ALU.mult, p.r() + [self.r_cstb], p.r())
                        first = kg == 0
                        last = kg == nkb - 1
                        self.MM(self.pb[bo][:, q0:N], vs.ap[:, kb, hh * 128:(hh + 1) * 128], p.ap[:, 0:nq], first, last,
                                vs.r() + p.r(), [self.rpb[bo]])
                        self.MM(self.pb[bd][:, q0:N], self.onesb, p.ap[:, 0:nq], first, last, [self.r_cstb] + p.r(), [self.rpb[bd]])
            for hh in range(2):
                h = hg * 2 + hh
                bo, bd = acc[hh]
                self.REC(rden.ap[:, 0:N], self.pb[bd][:, 0:N], [self.rpb[bd]], rden.r())
                self.TT(ot.ap[:, 0:N], self.pb[bo][:, 0:N], rden.ap[:, 0:N], ALU.mult, [self.rpb[bo]] + rden.r(), ot.r())
                self.TT(cat.ap[:, h, 0:N], ot.ap[:, 0:N], sg.ap[:, h, 0:N], ALU.mult, ot.r() + sg.r(h), cat.r(h))
                self.unpin(bo)
                self.unpin(bd)
        self.mem_attend(l, tc, mq, cat)
        self.out_proj(l, tc, cat)

    def cumsum_sample(self, lf):
        nb = PAST // 128
        SH = NSEQ * H
        lst = self.tmp([nb, SH])
        for sq in range(NSEQ):
            self.DMA(lst.ap[:, :, sq * H:(sq + 1) * H], self.i["clf"][sq].rearrange("(b p) h -> p b h", p=128), lst.r()[0], [], lst.r())
        for blk in range(nb):
            b = self.bank()
            self.MM(self.pb[b][:, 0:SH], self.triu, lst.ap[:, blk, :], True, blk == 0, [self.r_cst] + lst.r(), [self.rpb[b]])
            if blk > 0:
                self.MM(self.pb[b][:, 0:SH], self.elast, self.Cs[:, blk - 1, :], False, True, [self.r_cst, self.r_Cs], [self.rpb[b]])
            self.A(self.Cs[:, blk, :], self.pb[b][:, 0:SH], AF.Copy, [self.rpb[b]], [self.r_Cs])
        lf2 = lf.ap[0:DSEQ].rearrange("p s h -> p (s h)")
        b = self.bank()
        self.MM(self.pb[b][0:DSEQ, 0:SH], self.triu[0:DSEQ, 0:DSEQ], lf2, True, False, [self.r_cst] + lf.r(), [self.rpb[b]])
        self.MM(self.pb[b][0:DSEQ, 0:SH], self.elast[:, 0:DSEQ], self.Cs[:, nb - 1, :], False, True, [self.r_cst, self.r_Cs], [self.rpb[b]])
        self.A(self.Cs[0:DSEQ, nb, :], self.pb[b][0:DSEQ, 0:SH], AF.Copy, [self.rpb[b]], [self.r_Cs])
        cend = V(self, AR_BIG + 60 * 1024, [SH], F32)
        b = self.bank()
        self.MM(self.pb[b][:, 0:SH], self.ones[0:DSEQ, :], lf2, True, False, [self.r_cst] + lf.r(), [self.rpb[b]])
        self.MM(self.pb[b][:, 0:SH], self.elast, self.Cs[:, nb - 1, :], False, True, [self.r_cst, self.r_Cs], [self.rpb[b]])
        self.A(cend.ap[:, :], self.pb[b][:, 0:SH], AF.Copy, [self.rpb[b]], cend.r())
        self.TT(self.Cs[:, :, :], cend.ap[:, :].unsqueeze(1).to_broadcast([128, nb + 1, SH]), self.Cs[:, :, :], ALU.subtract,
                cend.r() + [self.r_Cs], [self.r_Cs])

    def fox_layer_s(self, l, tc):
        N = tc.N
        qT, sg, mq = self.fox_in(l, tc)
        cat = V(self, AR_H, [KC, N], BF16)
        self.tmp_reset()
        nb = PAST // 128
        kst = [self.tmp([TOK])]
        KT = [self.tmp([H, 128], BF16) for _ in range(2)]
        VB = [self.tmp([TOK], BF16) for _ in range(2)]
        sc = self.tmp([H, DSEQ])
        pT = [self.tmp([H, DSEQ], BF16) for _ in range(2)]
        rden = self.tmp([H, DSEQ])
        ot = self.tmp([H, DSEQ])
        self.tmp_s = self.tmp([512])
        bi = 0
        for sq in range(NSEQ):
            q0 = sq * DSEQ
            bo, bd = self.pin(), self.pin()
            self.MS(self.pb[bo][:, 0:H * DSEQ], 0.0, [self.rpb[bo]])
            self.MS(self.pb[bd][:, 0:H * DSEQ], 0.0, [self.rpb[bd]])
            for blk in range(nb + 1):
                new = blk == nb
                R = DSEQ if new else 128
                kt, vb = KT[bi % 2], VB[bi % 2]
                ks = kst[0]
                bi += 1
                if not new:
                    self.DMA(ks.ap[:, :], self.i["ck"][sq, blk * 128:(blk + 1) * 128, :], ks.r()[0], [], ks.r())
                    for g in range(3):
                        b = self.bank()
                        for j in range(4):
                            hh = g * 4 + j
                            self.TR(self.pb[b][:, j * 128:(j + 1) * 128], ks.ap[:, hh * 128:(hh + 1) * 128], self.ident,
                                    ks.r() + [self.r_cst], [self.rpb[b]])
                        self.A(kt.ap[:, g * 4:(g + 1) * 4, :], self.pb[b][:, :].rearrange("p (j t) -> p j t", t=128), AF.Copy,
                               [self.rpb[b]], kt.r())
                    self.DMA(vb.ap[:, :], self.i["cv"][sq, blk * 128:(blk + 1) * 128, :], vb.r()[0], [], vb.r(), eng="pool")
                    ktap = lambda hh: kt.ap[:, hh, :]
                    vbap = lambda hh: vb.ap[:, hh * 128:(hh + 1) * 128]
                    ktr, vbr = kt.r(), vb.r()
                else:
                    ktap = lambda hh: self.KTnew.ap[:, hh, q0:q0 + DSEQ]
                    vbap = lambda hh: self.Vnew.ap[0:DSEQ, sq, hh * 128:(hh + 1) * 128]
                    ktr, vbr = self.KTnew.r(), self.Vnew.r()
                b = self.bank()
                for hh in range(H):
                    self.MM(self.pb[b][0:R, hh * DSEQ:(hh + 1) * DSEQ], ktap(hh), qT.ap[:, hh, q0:q0 + DSEQ], True, True,
                            ktr + qT.r(hh), [self.rpb[b]])
                self.STT(sc.ap[0:R], self.pb[b][0:R, 0:H * DSEQ].rearrange("p (h q) -> p h q", q=DSEQ), SCALE,
                         self.Cs[0:R, blk, sq * H:(sq + 1) * H].unsqueeze(2).to_broadcast([R, H, DSEQ]), ALU.mult, ALU.add,
                         [self.rpb[b], self.r_Cs], sc.r())
                p = pT[bi % 2]
                self.A(p.ap[0:R], sc.ap[0:R], AF.Exp, sc.r(), p.r())
                if new:
                    self.TT(p.ap[0:R], p.ap[0:R], self.triub[0:R, 0:DSEQ].unsqueeze(1).to_broadcast([R, H, DSEQ]), ALU.mult,
                            p.r() + [self.r_cstb], p.r())
                for hh in range(H):
                    cs = slice(hh * DSEQ, (hh + 1) * DSEQ)
                    self.MM(self.pb[bo][:, cs], vbap(hh), p.ap[0:R, hh, :], False, new and hh == H - 1, vbr + p.r(), [self.rpb[bo]], skip=True)
                    self.MM(self.pb[bd][:, cs], self.onesb[0:R, :], p.ap[0:R, hh, :], False, new and hh == H - 1, [self.r_cstb] + p.r(), [self.rpb[bd]], skip=True)
            p3 = lambda b: self.pb[b][:, 0:H * DSEQ].rearrange("p (h q) -> p h q", q=DSEQ)
            self.REC(rden.ap[:, :, :], p3(bd), [self.rpb[bd]], rden.r())
            self.TT(ot.ap[:, :, :], p3(bo), rden.ap[:, :, :], ALU.mult, [self.rpb[bo]] + rden.r(), ot.r())
            self.TT(cat.ap[:, 0:H, q0:q0 + DSEQ], ot.ap[:, :, :], sg.ap[:, :, q0:q0 + DSEQ], ALU.mult, ot.r() + sg.r(), cat.r(0, H))
            self.unpin(bo)
            self.unpin(bd)
        self.mem_attend(l, tc, mq, cat)
        self.out_proj(l, tc, cat)

    def run_tile(self, tc, src, dst):
        self.load_xT(src, tc.N)
        nl = self.nlayers
        for l in range(min(2, nl)):
            self.gdn_layer(l, tc)
            self.mlp(l, tc.N)
        if nl > 2:
            lf = self.kv_share(tc)
            if tc.kind == 'p':
                self.cumsum_prompt(tc, lf)
            else:
                self.cumsum_sample(lf)
            for l in range(2, nl):
                if tc.kind == 'p':
                    self.fox_layer_p(l, tc)
                else:
                    self.fox_layer_s(l, tc)
                self.mlp(l, tc.N)
        self.store_y(dst, tc.N)

    def cumsum_end_only(self, tc):
        cend = self.tmp([H])
        blk = tc.t * (NP // 128) + tc.N // 128 - 1
        b = self.bank()
        self.MM(self.pb[b][:, 0:H], self.elast, self.Cp[:, blk, :], True, True, [self.r_cst, self.r_Cp], [self.rpb[b]])
        self.A(cend.ap[:, :], self.pb[b][:, 0:H], AF.Copy, [self.rpb[b]], cend.r())
        return cend

    def build(self):
        self.prologue()
        if self.ntp > 0:
            self.memory_kv()
        for t in range(self.ntp):
            tc = TileCfg('p', NP, 64, t)
            self.run_tile(tc, self.i["xp"][t * NP:(t + 1) * NP, :], self.o["yp"][t * NP:(t + 1) * NP, :])
        if self.do_sample:
            tc = TileCfg('s', NS, DSEQ, 0)
            self.run_tile(tc, self.i["xs"], self.o["ys"])
        return self.P.emit()


def make_consts():
    c = np.zeros((128, 640), np.float32)
    c[:, 0:128] = np.eye(128)
    c[:, 128:256] = 1.0
    c[:, 256:384] = np.triu(np.ones((128, 128)))
    c[:, 384:512] = -np.triu(np.ones((128, 128)), 1)
    c[127, 512:640] = 1.0
    return c


def build_program(ntp=8, do_sample=True, nlayers=4):
    nc = bass.Bass("TRN2", target_bir_lowering=False)
    with ExitStack() as es:
        k = K(nc, es, ntp=ntp, do_sample=do_sample, nlayers=nlayers)
        stats = k.build()
    return nc, stats


def core_inputs(c, inp):
    b = c // 2
    s0 = c * NSEQ
    f = lambda a: np.ascontiguousarray(a, dtype=np.float32)
    m = dict(
        xp=f(inp["x_prompt"][b]), xs=f(inp["x_sample"][s0:s0 + NSEQ].reshape(NS, D)),
        sg=f(inp["state_gdn"][:, s0:s0 + NSEQ]), sc=f(inp["state_conv"][:, s0:s0 + NSEQ].reshape(2, NSEQ * 3, QKV)),
        ck=f(inp["cache_k"][s0:s0 + NSEQ].reshape(NSEQ, PAST, TOK)), cv=f(inp["cache_v"][s0:s0 + NSEQ].reshape(NSEQ, PAST, TOK)),
        clf=f(inp["cache_logf"][s0:s0 + NSEQ]),
        cmk=f(inp["cache_mem_k"][:, s0:s0 + NSEQ].reshape(4, NSEQ, NMEM, 512)),
        cmv=f(inp["cache_mem_v"][:, s0:s0 + NSEQ].reshape(4, NSEQ, NMEM, 512)),
        mp=f(inp["mem_prompt"][b]),
        g_pre=f(inp["norm_mix_pre"].reshape(64, 128)), g_post=f(inp["norm_mix_post"].reshape(64, 128)),
        g_mpre=f(inp["norm_mlp_pre"].reshape(64, 128)), g_mpost=f(inp["norm_mlp_post"].reshape(64, 128)),
        g_mem=f(inp["norm_mem"].reshape(64, 128)), g_kv=f(inp["norm_kv"].reshape(16, 128)),
        convw=f(inp["conv_w_a"].reshape(288, 128)), a_log=f(inp["a_log"].reshape(24)), dt_bias=f(inp["dt_bias"].reshape(24)),
        gdn_norm=f(inp["gdn_norm"]), b_f=f(inp["b_f"]),
        w_in_a=f(inp["w_in_a"]), w_in_b=f(inp["w_in_b"]), w_kvf=f(inp["w_kvf"]), w_mem=f(inp["w_mem_kv"]),
        w_o=f(inp["w_o"]), w_up=f(inp["w_up"]), w_down=f(inp["w_down"]),
        cst=make_consts(),
    )
    return m


def kernel(**inp):
    inp = {k: np.asarray(v) for k, v in inp.items()}
    nc, _ = build_program()
    in_maps = [core_inputs(c, inp) for c in range(8)]
    res = run_bass_kernel_spmd(nc, in_maps, core_ids=list(range(8))).results
    ev = [res[c] for c in range(0, 8, 2)]
    st = lambda key, sh: np.stack([r[key] for r in ev]).reshape(sh).astype(np.float32)
    cat = lambda key: np.concatenate([r[key] for r in res], axis=0)
    B = 4
    y_prompt = st("yp", (B, SEQ, D))
    y_sample = cat("ys").reshape(32, DSEQ, D)
    p_gdn = np.stack([r["pg"] for r in ev], axis=1)
    p_conv = np.stack([r["pc"].reshape(2, 3, QKV) for r in ev], axis=1)
    p_k = st("pk", (B, SEQ, H, HD))
    p_v = st("pv", (B, SEQ, H, HD))
    p_logf = st("plf", (B, SEQ, H))
    p_mem_k = np.stack([r["pmk"].reshape(4, NMEM, MH, HD) for r in ev], axis=1)
    p_mem_v = np.stack([r["pmv"].reshape(4, NMEM, MH, HD) for r in ev], axis=1)
    s_gdn = np.concatenate([r["sgo"] for r in res], axis=1)
    s_conv = np.concatenate([r["sco"].reshape(2, NSEQ, 3, QKV) for r in res], axis=1)
    s_k = cat("sk").reshape(32, DSEQ, H, HD)
    s_v = cat("sv").reshape(32, DSEQ, H, HD)
    s_logf = cat("slf").reshape(32, DSEQ, H)
    outs = (y_prompt, y_sample, p_gdn, p_conv, p_k, p_v, p_logf, p_mem_k, p_mem_v, s_gdn, s_conv, s_k, s_v, s_logf)
    return tuple(np.ascontiguousarray(o, dtype=np.float32) for o in outs)
```

```python
import numpy as np
from contextlib import ExitStack
import concourse.bass as bass
import concourse.mybir as mybir
from concourse.bass_utils import run_bass_kernel_spmd

F32 = mybir.dt.float32
BF16 = mybir.dt.bfloat16
AF = mybir.ActivationFunctionType
ALU = mybir.AluOpType


class Res:
    __slots__ = ("name", "excl", "lw", "rd", "sem", "semcnt")

    def __init__(self, name, excl=False):
        self.name = name
        self.excl = excl
        self.lw = None
        self.rd = []
        self.sem = None
        self.semcnt = 0


class Op:
    __slots__ = ("eng", "fn", "deps", "sig", "isdma", "sem", "val")

    def __init__(self, eng, fn, isdma):
        self.eng = eng
        self.fn = fn
        self.deps = []
        self.sig = False
        self.isdma = isdma
        self.sem = None
        self.val = 0


class Prog:
    ENGS = ["pe", "act", "dve", "pool", "sp"]

    def __init__(self, nc, es):
        self.nc = nc
        self.es = es
        self.ops = []
        self.esem = {}
        for e in ["pe", "act", "dve", "pool"]:
            self.esem[e] = es.enter_context(nc.semaphore("s_" + e))

    def _handle(self, eng):
        nc = self.nc
        return {"pe": nc.tensor, "act": nc.scalar, "dve": nc.vector,
                "pool": nc.gpsimd, "sp": nc.sync}[eng]

    def _add(self, op, reads, writes):
        deps = {}
        writes = list(writes)
        for r in reads:
            if r.excl:
                writes.append(r)
                continue
            if r.lw is not None:
                deps[id(r.lw)] = (r.lw, True)
        for w in writes:
            if w.lw is not None and id(w.lw) not in deps:
                deps[id(w.lw)] = (w.lw, w.excl)
            for rd in w.rd:
                if id(rd) not in deps:
                    deps[id(rd)] = (rd, False)
        for r in reads:
            if not r.excl:
                if not op.isdma:
                    r.rd = [x for x in r.rd if x.isdma or x.eng != op.eng]
                r.rd.append(op)
        for w in writes:
            w.lw = op
            w.rd = []
        op.deps = [d for d in deps.values() if d[0] is not op]
        self.ops.append(op)
        return op

    def op(self, eng, fn, reads=(), writes=()):
        return self._add(Op(eng, fn, False), reads, writes)

    def dma(self, eng, fn, semres, reads=(), writes=()):
        op = Op(eng, fn, True)
        if semres.sem is None:
            semres.sem = self.es.enter_context(self.nc.semaphore("d_" + semres.name))
        semres.semcnt += 16
        op.sem = semres.sem
        op.val = semres.semcnt
        return self._add(op, reads, writes)

    @staticmethod
    def _needs(d, raw, op):
        if d.isdma:
            return True
        if d.eng != op.eng:
            return True
        if d.eng == "pe":
            return False
        return raw

    def emit(self):
        for op in self.ops:
            for d, raw in op.deps:
                if self._needs(d, raw, op):
                    d.sig = True
        cnt = {e: 0 for e in self.ENGS}
        for op in self.ops:
            if not op.isdma and op.sig:
                cnt[op.eng] += 1
                op.sem = self.esem[op.eng]
                op.val = cnt[op.eng]
        nwait = 0
        for eng in self.ENGS:
            h = self._handle(eng)
            waited = {}
            for op in self.ops:
                if op.eng != eng:
                    continue
                need = {}
                for d, raw in op.deps:
                    if self._needs(d, raw, op):
                        k = id(d.sem)
                        if waited.get(k, 0) < d.val and need.get(k, (None, 0))[1] < d.val:
                            need[k] = (d.sem, d.val)
                for k, (sm, v) in need.items():
                    h.wait_ge(sm, v)
                    waited[k] = v
                    nwait += 1
                ins = op.fn(h)
                if op.isdma:
                    ins.then_inc(op.sem, 16)
                elif op.sig:
                    ins.then_inc(op.sem, 1)
        sp = self._handle("sp")
        seen = {}
        for op in self.ops:
            if op.isdma:
                k = id(op.sem)
                if seen.get(k, (None, 0))[1] < op.val:
                    seen[k] = (op.sem, op.val)
        for sm, v in seen.values():
            sp.wait_ge(sm, v)
        return dict(n_ops=len(self.ops), n_wait=nwait, sig=dict(cnt))


D = 2048
KC = 16
H = 12
HD = 128
MH = 4
NMEM = 256
DFF = 8192
QKV = 4608
TOK = 1536
IN_A = 6680
IN_B = 3584
SEQ = 4096
PAST = 2048
DSEQ = 32
NSEQ = 4
EPS = 1e-6
SCALE = float(HD ** -0.5)
NP = 512
NS = NSEQ * DSEQ
WCOLS = 256
NW = 3
SLAB = 1024
AR_H = 0
AR_TMP = 16 * 1024
AR_BIG = AR_TMP + 40 * 1024
AR_END = AR_BIG + 64 * 1024


def _prod(t):
    r = 1
    for x in t:
        r *= x
    return r


class V:
    def __init__(self, k, off, shape, dt):
        self.k = k
        self.off = off
        self.shape = tuple(shape)
        self.esz = 2 if dt == BF16 else 4
        n = _prod(shape)
        self.nbytes = n * self.esz
        assert off % 4 == 0 and self.nbytes % 4 == 0
        assert off + self.nbytes <= AR_END, (off, self.nbytes)
        ap = k.arena[:, off // 4:(off + self.nbytes) // 4]
        if dt == BF16:
            ap = ap.bitcast(BF16)
        if len(shape) == 2:
            ap = ap.rearrange("p (a b) -> p a b", b=shape[1])
        elif len(shape) == 3:
            ap = ap.rearrange("p (a b c) -> p a b c", b=shape[1], c=shape[2])
        self.ap = ap

    def r(self, i=None, j=None):
        if i is None:
            lo, hi = 0, self.nbytes
        else:
            st = _prod(self.shape[1:]) * self.esz
            lo = i * st
            hi = (i + 1 if j is None else j) * st
        return self.k.slabs(self.off + lo, self.off + hi)


class TileCfg:
    def __init__(self, kind, N, L, t=0):
        self.kind = kind
        self.N = N
        self.L = L
        self.nch = N // L
        self.t = t
        self.nsq = 1 if kind == 'p' else NSEQ


class K:
    def __init__(self, nc, es, ntp=8, do_sample=True, nlayers=4):
        self.nc, self.es = nc, es
        self.P = Prog(nc, es)
        self.ntp = ntp
        self.do_sample = do_sample
        self.nlayers = nlayers
        self._names = 0
        P = self.P
        din = lambda n, sh: nc.dram_tensor(n, sh, F32, kind="ExternalInput").ap()
        dout = lambda n, sh: nc.dram_tensor(n, sh, F32, kind="ExternalOutput").ap()
        self.i = dict(
            xp=din("xp", [SEQ, D]), xs=din("xs", [NS, D]),
            sg=din("sg", [2, NSEQ, H, HD, HD]), sc=din("sc", [2, NSEQ * 3, QKV]),
            ck=din("ck", [NSEQ, PAST, TOK]), cv=din("cv", [NSEQ, PAST, TOK]),
            clf=din("clf", [NSEQ, PAST, H]),
            cmk=din("cmk", [4, NSEQ, NMEM, 512]), cmv=din("cmv", [4, NSEQ, NMEM, 512]),
            mp=din("mp", [NMEM, D]),
            g_pre=din("g_pre", [64, 128]), g_post=din("g_post", [64, 128]),
            g_mpre=din("g_mpre", [64, 128]), g_mpost=din("g_mpost", [64, 128]),
            g_mem=din("g_mem", [64, 128]), g_kv=din("g_kv", [16, 128]),
            convw=din("convw", [288, 128]), a_log=din("a_log", [24]), dt_bias=din("dt_bias", [24]),
            gdn_norm=din("gdn_norm", [2, 128]), b_f=din("b_f", [12]),
            w_in_a=din("w_in_a", [2, D, IN_A]), w_in_b=din("w_in_b", [2, D, IN_B]),
            w_kvf=din("w_kvf", [D, 3084]), w_mem=din("w_mem", [4, D, 1024]),
            w_o=din("w_o", [4, D, D]), w_up=din("w_up", [4, D, DFF]), w_down=din("w_down", [4, DFF, D]),
            cst=din("cst", [128, 640]),
        )
        self.o = dict(
            yp=dout("yp", [SEQ, D]), ys=dout("ys", [NS, D]),
            pg=dout("pg", [2, H, HD, HD]), pc=dout("pc", [2 * 3, QKV]),
            pk=dout("pk", [SEQ, TOK]), pv=dout("pv", [SEQ, TOK]), plf=dout("plf", [SEQ, H]),
            pmk=dout("pmk", [4, NMEM, 512]), pmv=dout("pmv", [4, NMEM, 512]),
            sgo=dout("sgo", [2, NSEQ, H, HD, HD]), sco=dout("sco", [2 * NSEQ * 3, QKV]),
            sk=dout("sk", [NS, TOK]), sv=dout("sv", [NS, TOK]), slf=dout("slf", [NS, H]),
        )
        self.kscr = nc.dram_tensor("kscr", [128, H, SEQ], BF16).ap()
        self.vscr = nc.dram_tensor("vscr", [SEQ, TOK], BF16).ap()
        self.mkscr = nc.dram_tensor("mkscr", [4, 128, MH, NMEM], BF16).ap()
        self.mvscr = nc.dram_tensor("mvscr", [4, NMEM, 512], BF16).ap()
        self.r_kscr = [Res("kscr%d" % i) for i in range(SEQ // NP)]
        self.r_vscr = [Res("vscr%d" % i) for i in range(SEQ // NP)]
        self.r_mscr = [Res("mscr%d" % i) for i in range(4)]

        T = self.T
        self.arena = T("arena", [128, AR_END // 4])
        self.slab = [Res("slab%d" % i) for i in range(AR_END // SLAB)]
        self.xT = T("xT", [128, KC, NP])
        self.xr = [Res("xT%d" % i) for i in range(KC)]
        self.wring = [T("wr%d" % i, [128, KC, WCOLS], BF16) for i in range(NW)]
        self.wres = [Res("wr%d" % i) for i in range(NW)]
        self.wi = 0
        self.wsm = T("wsm", [128, KC, 24], BF16)
        self.r_wsm = Res("wsm")
        self.pb = [es.enter_context(nc.psum_tensor("pb%d" % i, [128, 512], F32)) for i in range(8)]
        self.rpb = [Res("pb%d" % i, True) for i in range(8)]
        self.pinned = set()
        self.bi = 0
        self.cst = T("cst_s", [128, 640])
        self.r_cst = Res("cst")
        self.cstb = T("cstb", [128, 384], BF16)
        self.r_cstb = Res("cstb")
        self.ident = self.cst[:, 0:128]
        self.ones = self.cst[:, 128:256]
        self.triu = self.cst[:, 256:384]
        self.triusn = self.cst[:, 384:512]
        self.elast = self.cst[:, 512:640]
        self.identb = self.cstb[:, 0:128]
        self.onesb = self.cstb[:, 128:256]
        self.triub = self.cstb[:, 256:384]
        self.gains = T("gains", [128, 6, 64])
        self.r_gains = Res("gains")
        self.convw = T("convw_s", [128, 288])
        self.gdng = T("gdng", [128, 2])
        self.vecs = T("vecs", [128, 60])
        self.nA = T("nA", [128, 24])
        self.r_misc = Res("misc")
        self.S = T("S", [128, 2, H, HD])
        self.r_S = [[Res("S%d_%d" % (l, h)) for h in range(H)] for l in range(2)]
        self.chist = T("chist", [128, 2, 36, 3])
        self.r_chist = [Res("chist%d" % l) for l in range(2)]
        self.Cp = T("Cp", [128, SEQ // 128, H])
        self.r_Cp = Res("Cp")
        self.Cs = T("Cs", [128, PAST // 128 + 1, NSEQ * H])
        self.r_Cs = Res("Cs")
        self.sS = [T("sS%d" % i, [128, HD]) for i in range(4)]
        self.r_sS = [Res("sS%d" % i) for i in range(4)]
        self.sSi = 0
        self.cendp = T("cendp", [128, H])
        self.r_cend = Res("cend")

    def T(self, name, shape, dt=F32):
        return self.es.enter_context(self.nc.sbuf_tensor(name, shape, dt))

    def slabs(self, lo, hi):
        return self.slab[lo // SLAB:(hi + SLAB - 1) // SLAB]

    def bank(self):
        for _ in range(8):
            b = self.bi
            self.bi = (self.bi + 1) % 8
            if b not in self.pinned:
                return b
        raise RuntimeError("no psum bank")

    def pin(self):
        b = self.bank()
        self.pinned.add(b)
        return b

    def unpin(self, b):
        self.pinned.discard(b)

    def tmp_reset(self, base=AR_TMP):
        self.tp = base

    def tmp(self, shape, dt=F32):
        v = V(self, self.tp, shape, dt)
        self.tp += (v.nbytes + SLAB - 1) // SLAB * SLAB
        assert self.tp <= AR_BIG, "tmp overflow %d" % self.tp
        return v

    def A(self, out, in_, func, rd, wr, bias=None, scale=None):
        kw = {}
        if bias is not None:
            kw["bias"] = bias
        if scale is not None:
            kw["scale"] = scale
        self.P.op("act", lambda e: e.activation(out=out, in_=in_, func=func, **kw), rd, wr)

    def TT(self, out, in0, in1, op, rd, wr):
        self.P.op("dve", lambda e: e.tensor_tensor(out=out, in0=in0, in1=in1, op=op), rd, wr)

    def TS(self, out, in0, s1, s2, op0, op1, rd, wr):
        if s2 is None:
            self.P.op("dve", lambda e: e.tensor_scalar(out=out, in0=in0, scalar1=s1, scalar2=None, op0=op0), rd, wr)
        else:
            self.P.op("dve", lambda e: e.tensor_scalar(out=out, in0=in0, scalar1=s1, scalar2=s2, op0=op0, op1=op1), rd, wr)

    def STT(self, out, in0, sc, in1, op0, op1, rd, wr):
        self.P.op("dve", lambda e: e.scalar_tensor_tensor(out=out, in0=in0, scalar=sc, in1=in1, op0=op0, op1=op1), rd, wr)

    def CP(self, out, in_, rd, wr):
        self.P.op("dve", lambda e: e.tensor_copy(out=out, in_=in_), rd, wr)

    def REC(self, out, in_, rd, wr):
        self.P.op("dve", lambda e: e.reciprocal(out=out, in_=in_), rd, wr)

    def MS(self, out, val, wr):
        self.P.op("dve", lambda e: e.memset(out, val), (), wr)

    def MM(self, out, lhsT, rhs, st, sp, rd, wr, skip=False):
        if skip:
            self.P.op("pe", lambda e: e.matmul(out, lhsT, rhs, start=st, stop=sp, skip_group_check=True), rd, wr)
        else:
            self.P.op("pe", lambda e: e.matmul(out, lhsT, rhs, start=st, stop=sp), rd, wr)

    def TR(self, out, in_, ident, rd, wr):
        self.P.op("pe", lambda e: e.transpose(out=out, in_=in_, identity=ident), rd, wr)

    def DMA(self, out, in_, semres, rd, wr, eng="sp"):
        self.P.dma(eng, lambda e: e.dma_start(out=out, in_=in_), semres, rd, wr)

    def pbf(self, b):
        return self.pb[b][:].bitcast(BF16)

    def wload(self, src, ncols):
        sl = self.wi
        self.wi = (self.wi + 1) % NW
        dst = self.wring[sl][:, :, 0:ncols]
        self.DMA(dst, src.rearrange("(kc p) c -> p kc c", p=128), self.wres[sl], [], [self.wres[sl]], eng="pool")
        return sl

    def linear_fm(self, W, c0, ncols, act, N, epi, Kdim=D):
        nkb = Kdim // D
        for cb in range(0, ncols, WCOLS):
            nc_ = min(WCOLS, ncols - cb)
            chunks = [(m0, min(128, nc_ - m0)) for m0 in range(0, nc_, 128)]
            banks = [self.pin() for _ in chunks] if nkb > 1 else None
            for kb in range(nkb):
                sl = self.wload(W[kb * D:(kb + 1) * D, c0 + cb:c0 + cb + nc_], nc_)
                for ci, (m0, rows) in enumerate(chunks):
                    b = banks[ci] if banks else self.bank()
                    for kc in range(KC):
                        kk = kb * KC + kc
                        self.MM(self.pb[b][0:rows, 0:N], self.wring[sl][:, kc, m0:m0 + rows], act.ap[:, kk, 0:N],
                                kk == 0, kk == nkb * KC - 1, [self.wres[sl]] + act.r(kk), [self.rpb[b]])
                    if kb == nkb - 1:
                        epi((cb + m0) // 128, rows, self.pb[b][0:rows, 0:N], b)
            if banks:
                for b in banks:
                    self.unpin(b)

    def linear_tm(self, W, c0, ncols, act, toks, epi):
        for cb in range(0, ncols, WCOLS):
            nc_ = min(WCOLS, ncols - cb)
            sl = self.wload(W[:, c0 + cb:c0 + cb + nc_], nc_)
            for ti, (t0, M) in enumerate(toks):
                b = self.bank()
                for kc in range(KC):
                    self.MM(self.pb[b][0:M, 0:nc_], act.ap[:, kc, t0:t0 + M], self.wring[sl][:, kc, 0:nc_],
                            kc == 0, kc == KC - 1, [self.wres[sl]] + act.r(kc), [self.rpb[b]])
                epi(ti, cb, nc_, self.pb[b][0:M, 0:nc_], b)

    def stats(self, srcs, N, Dn, rstd, base_rows=128):
        sq = [self.tmp([N]) for _ in range(2)]
        b = self.pin()
        n = len(srcs)
        for i, (ap, rl) in enumerate(srcs):
            q = sq[i % 2]
            self.A(q.ap[:, 0:N], ap, AF.Square, rl, q.r())
            self.MM(self.pb[b][:, 0:N], self.ones, q.ap[:, 0:N], i == 0, i == n - 1, [self.r_cst] + q.r(), [self.rpb[b]])
        self.rsqrt(rstd.ap[:, 0:N], self.pb[b][:, 0:N], [self.rpb[b]], rstd.r(), 1.0 / Dn)
        self.unpin(b)

    def rsqrt(self, out, in_, rd, wr, scale):
        self.A(out, in_, AF.Sqrt, rd, wr, bias=EPS, scale=scale)
        self.REC(out, out, wr, wr)

    def prenorm(self, gi, gl, dst, N):
        self.tmp_reset()
        rstd = self.tmp([N])
        self.stats([(self.xT[:, kc, 0:N], [self.xr[kc]]) for kc in range(KC)], N, D, rstd)
        for kc in range(KC):
            self.STT(dst.ap[:, kc, 0:N], self.xT[:, kc, 0:N], self.gains[:, gi, gl * 16 + kc:gl * 16 + kc + 1],
                     rstd.ap[:, 0:N], ALU.mult, ALU.mult, [self.xr[kc], self.r_gains] + rstd.r(), dst.r(kc))

    def postnorm_res(self, y, gi, gl, N, tmpbase):
        self.tmp_reset(tmpbase)
        rstd = self.tmp([N])
        self.stats([(y.ap[:, kc, 0:N], y.r(kc)) for kc in range(KC)], N, D, rstd)
        for kc in range(KC):
            self.STT(y.ap[:, kc, 0:N], y.ap[:, kc, 0:N], self.gains[:, gi, gl * 16 + kc:gl * 16 + kc + 1],
                     rstd.ap[:, 0:N], ALU.mult, ALU.mult, y.r(kc) + [self.r_gains] + rstd.r(), y.r(kc))
            self.TT(self.xT[:, kc, 0:N], self.xT[:, kc, 0:N], y.ap[:, kc, 0:N], ALU.add,
                    [self.xr[kc]] + y.r(kc), [self.xr[kc]])

    def load_T(self, src, R, dst, wr):
        self.tmp_reset()
        stg = self.tmp([128])
        self.DMA(stg.ap[0:R, :], src, stg.r()[0], [], stg.r())
        b = self.bank()
        self.TR(self.pb[b][:, 0:R], stg.ap[0:R, :], self.ident[0:R, 0:R], stg.r() + [self.r_cst], [self.rpb[b]])
        self.A(dst, self.pb[b][:, 0:R], AF.Copy, [self.rpb[b]], wr)

    def prologue(self):
        i = self.i
        self.DMA(self.cst[:], i["cst"], self.r_cst, [], [self.r_cst])
        self.A(self.cstb[:], self.cst[:, 0:384], AF.Copy, [self.r_cst], [self.r_cstb])
        for gi, nm in enumerate(["g_pre", "g_post", "g_mpre", "g_mpost", "g_mem"]):
            self.load_T(i[nm], 64, self.gains[:, gi, :], [self.r_gains])
        self.load_T(i["g_kv"], 16, self.gains[:, 5, 0:16], [self.r_gains])
        for j in range(3):
            self.load_T(i["convw"][j * 96:(j + 1) * 96, :], 96, self.convw[:, j * 96:(j + 1) * 96], [self.r_misc])
        self.load_T(i["gdn_norm"], 2, self.gdng[:, :], [self.r_misc])
        self.DMA(self.vecs[:, 0:24], i["a_log"].partition_broadcast(128), self.r_misc, [], [self.r_misc])
        self.DMA(self.vecs[:, 24:48], i["dt_bias"].partition_broadcast(128), self.r_misc, [], [self.r_misc])
        self.DMA(self.vecs[:, 48:60], i["b_f"].partition_broadcast(128), self.r_misc, [], [self.r_misc])
        self.A(self.nA[:], self.vecs[:, 0:24], AF.Exp, [self.r_misc], [self.r_misc])
        self.TS(self.nA[:], self.nA[:], -1.0, None, ALU.mult, None, [self.r_misc], [self.r_misc])
        for l in range(2):
            for h in range(H):
                self.MS(self.S[:, l, h, :], 0.0, [self.r_S[l][h]])
            self.MS(self.chist[:, l, :, :], 0.0, [self.r_chist[l]])

    def memory_kv(self):
        N = NMEM
        mT = V(self, AR_BIG, [KC, N], F32)
        hm = V(self, AR_H, [KC, N], BF16)
        stg = [V(self, AR_BIG + 16384 + j * 8192, [D], F32) for j in range(2)]
        ost = [V(self, AR_BIG + 32768 + j * 2048, [512], F32) for j in range(4)]
        osb = [V(self, AR_BIG + 40960 + j * 1024, [512], BF16) for j in range(4)]
        kst = V(self, AR_BIG + 45056, [MH, N], BF16)
        for tb in range(2):
            self.DMA(stg[tb].ap[:, :], self.i["mp"][tb * 128:(tb + 1) * 128, :], stg[tb].r()[0], [], stg[tb].r())
            for g in range(4):
                b = self.bank()
                for j in range(4):
                    kc = g * 4 + j
                    self.TR(self.pb[b][:, j * 128:(j + 1) * 128], stg[tb].ap[:, kc * 128:(kc + 1) * 128], self.ident,
                            stg[tb].r() + [self.r_cst], [self.rpb[b]])
                self.A(mT.ap[:, g * 4:(g + 1) * 4, tb * 128:(tb + 1) * 128],
                       self.pb[b][:, :].rearrange("p (j t) -> p j t", t=128), AF.Copy, [self.rpb[b]], mT.r(g * 4, g * 4 + 4))
        self.tmp_reset()
        rstd = self.tmp([N])
        self.stats([(mT.ap[:, kc, :], mT.r(kc)) for kc in range(KC)], N, D, rstd)
        oc = [0]
        for l in range(4):
            for kc in range(KC):
                self.STT(hm.ap[:, kc, :], mT.ap[:, kc, :], self.gains[:, 4, l * 16 + kc:l * 16 + kc + 1], rstd.ap[:, :],
                         ALU.mult, ALU.mult, mT.r(kc) + [self.r_gains] + rstd.r(), hm.r(kc))
            W = self.i["w_mem"][l]

            def epi_k(m, rows, ps, b, l=l):
                self.A(kst.ap[:, m, :], ps, AF.Copy, [self.rpb[b]], kst.r(m))
            self.linear_fm(W, 0, 512, hm, N, epi_k)
            self.DMA(self.mkscr[l], kst.ap[:, :, :], kst.r()[0], kst.r(), [self.r_mscr[l]])

            def epi_tm(ti, cb, ncb, ps, b, l=l, which=0):
                j = oc[0] % 4
                oc[0] += 1
                dst = self.o["pmk" if which == 0 else "pmv"]
                self.A(ost[j].ap[:, 0:ncb], ps, AF.Copy, [self.rpb[b]], ost[j].r())
                self.DMA(dst[l, ti * 128:(ti + 1) * 128, cb:cb + ncb], ost[j].ap[:, 0:ncb], ost[j].r()[0], ost[j].r(), [])
                if which == 1:
                    self.CP(osb[j].ap[:, 0:ncb], ost[j].ap[:, 0:ncb], ost[j].r(), osb[j].r())
                    self.DMA(self.mvscr[l, ti * 128:(ti + 1) * 128, cb:cb + ncb], osb[j].ap[:, 0:ncb], osb[j].r()[0],
                             osb[j].r(), [self.r_mscr[l]])
            self.linear_tm(W, 0, 512, hm, [(0, 128), (128, 128)], lambda *a, l=l: epi_tm(*a, l=l, which=0))
            self.linear_tm(W, 512, 512, hm, [(0, 128), (128, 128)], lambda *a, l=l: epi_tm(*a, l=l, which=1))

    def load_xT(self, src, N):
        stg = [V(self, AR_BIG + j * 8192, [D], F32) for j in range(2)]
        for tb in range(N // 128):
            st = stg[tb % 2]
            self.DMA(st.ap[:, :], src[tb * 128:(tb + 1) * 128, :], st.r()[0], [], st.r())
            for g in range(4):
                b = self.bank()
                for j in range(4):
                    kc = g * 4 + j
                    self.TR(self.pb[b][:, j * 128:(j + 1) * 128], st.ap[:, kc * 128:(kc + 1) * 128], self.ident,
                            st.r() + [self.r_cst], [self.rpb[b]])
                self.A(self.xT[:, g * 4:(g + 1) * 4, tb * 128:(tb + 1) * 128],
                       self.pb[b][:, :].rearrange("p (j t) -> p j t", t=128), AF.Copy,
                       [self.rpb[b]], self.xr[g * 4:(g + 1) * 4])

    def store_y(self, dst, N):
        stg = [V(self, AR_BIG + j * 8192, [D], F32) for j in range(2)]
        for tb in range(N // 128):
            st = stg[tb % 2]
            for g in range(4):
                b = self.bank()
                for j in range(4):
                    kc = g * 4 + j
                    self.TR(self.pb[b][:, j * 128:(j + 1) * 128], self.xT[:, kc, tb * 128:(tb + 1) * 128], self.ident,
                            [self.xr[kc], self.r_cst], [self.rpb[b]])
                self.A(st.ap[:, g * 512:(g + 1) * 512], self.pb[b][:, :], AF.Copy, [self.rpb[b]], st.r())
            self.DMA(dst[tb * 128:(tb + 1) * 128, :], st.ap[:, :], st.r()[0], st.r(), [])

    def mlp(self, l, N):
        hf = V(self, AR_H, [KC, N], BF16)
        self.prenorm(2, l, hf, N)
        hid = V(self, AR_BIG, [64, N], BF16)
        self.tmp_reset(AR_TMP + 16384)
        rl = [self.tmp([N]) for _ in range(3)]
        cnt = [0]

        def epi_up(m, rows, ps, b):
            r = rl[cnt[0] % 3]
            cnt[0] += 1
            self.A(r.ap[:, 0:N], ps, AF.Relu, [self.rpb[b]], r.r())
            self.TT(hid.ap[:, m, 0:N], r.ap[:, 0:N], r.ap[:, 0:N], ALU.mult, r.r(), hid.r(m))
        self.linear_fm(self.i["w_up"][l], 0, DFF, hf, N, epi_up)
        y = V(self, AR_H, [KC, N], F32)

        def epi_dn(m, rows, ps, b):
            self.A(y.ap[:, m, 0:N], ps, AF.Copy, [self.rpb[b]], y.r(m))
        self.linear_fm(self.i["w_down"][l], 0, D, hid, N, epi_dn, Kdim=DFF)
        self.postnorm_res(y, 3, l, N, AR_H + KC * N * 4 if KC * N * 4 > 16384 else AR_TMP)

    def mem_attend(self, l, tc, mq, cat):
        N = tc.N
        kT = self.tmp([MH, NMEM], BF16)
        vv = self.tmp([2, 512], BF16)
        pT = [self.tmp([N], BF16) for _ in range(2)]
        rden = self.tmp([N])
        nq = N // tc.nsq
        for sq in range(tc.nsq):
            q0 = sq * nq
            if tc.kind == 'p':
                self.DMA(kT.ap[:, :, :], self.mkscr[l], kT.r()[0], [self.r_mscr[l]], kT.r())
                self.DMA(vv.ap[:, :, :], self.mvscr[l].rearrange("(b p) c -> p b c", p=128), vv.r()[0], [self.r_mscr[l]], vv.r())
            else:
                stg = self.tmp_s
                for nb in range(2):
                    self.DMA(stg.ap[:, 0:512], self.i["cmk"][l, sq, nb * 128:(nb + 1) * 128, :], stg.r()[0], [], stg.r())
                    b = self.bank()
                    for hm in range(MH):
                        self.TR(self.pb[b][:, hm * 128:(hm + 1) * 128], stg.ap[:, hm * 128:(hm + 1) * 128], self.ident,
                                stg.r() + [self.r_cst], [self.rpb[b]])
                    self.A(kT.ap[:, :, nb * 128:(nb + 1) * 128], self.pb[b][:, :].rearrange("p (h n) -> p h n", n=128),
                           AF.Copy, [self.rpb[b]], kT.r())
                    self.DMA(stg.ap[:, 0:512], self.i["cmv"][l, sq, nb * 128:(nb + 1) * 128, :], stg.r()[0], [], stg.r())
                    self.A(vv.ap[:, nb, :], stg.ap[:, 0:512], AF.Copy, stg.r(), vv.r())
            for hm in range(MH):
                bo, bd = self.pin(), self.pin()
                for nb in range(2):
                    b = self.bank()
                    self.MM(self.pb[b][:, 0:nq], kT.ap[:, hm, nb * 128:(nb + 1) * 128], mq.ap[:, hm, q0:q0 + nq], True, True,
                            kT.r() + mq.r(hm), [self.rpb[b]])
                    p = pT[nb]
                    self.A(p.ap[:, 0:nq], self.pb[b][:, 0:nq], AF.Exp, [self.rpb[b]], p.r(), scale=SCALE)
                    self.MM(self.pb[bo][:, 0:nq], vv.ap[:, nb, hm * 128:(hm + 1) * 128], p.ap[:, 0:nq], nb == 0, nb == 1,
                            vv.r() + p.r(), [self.rpb[bo]])
                    self.MM(self.pb[bd][:, 0:nq], self.onesb, p.ap[:, 0:nq], nb == 0, nb == 1,
                            [self.r_cstb] + p.r(), [self.rpb[bd]])
                self.REC(rden.ap[:, 0:nq], self.pb[bd][:, 0:nq], [self.rpb[bd]], rden.r())
                self.TT(cat.ap[:, H + hm, q0:q0 + nq], self.pb[bo][:, 0:nq], rden.ap[:, 0:nq], ALU.mult,
                        [self.rpb[bo]] + rden.r(), cat.r(H + hm))
                self.unpin(bo)
                self.unpin(bd)

    def out_proj(self, l, tc, cat):
        N = tc.N
        y = V(self, AR_BIG, [KC, N], F32)

        def epi(m, rows, ps, b):
            self.A(y.ap[:, m, 0:N], ps, AF.Copy, [self.rpb[b]], y.r(m))
        self.linear_fm(self.i["w_o"][l], 0, D, cat, N, epi)
        self.postnorm_res(y, 1, l, N, AR_TMP)

    def gdn_layer(self, l, tc):
        N, L, nch = tc.N, tc.L, tc.nch
        nsq = tc.nsq
        hT = V(self, AR_H, [KC, N], BF16)
        self.prenorm(0, l, hT, N)
        qn = V(self, AR_BIG, [H, N], BF16)
        kn = V(self, AR_BIG + 12288, [H, N], BF16)
        vT = V(self, AR_BIG + 24576, [H, N], BF16)
        zT = V(self, AR_BIG + 36864, [H, N], BF16)
        mq = V(self, AR_BIG + 49152, [MH, N], BF16)
        sm = AR_BIG + 53248
        ab = V(self, sm, [nch, 24], F32)
        gam = V(self, sm + 1024, [nch, H], F32)
        bet = V(self, sm + 1536, [nch, H], F32)
        Gc = V(self, sm + 2048, [nch, H], F32)
        eG = V(self, sm + 2560, [nch, H], F32)
        eGm = V(self, sm + 3072, [nch, H], F32)
        eGL = V(self, sm + 3584, [nch, H], F32)
        t1 = V(self, sm + 4096, [nch, H], F32)
        t2 = V(self, sm + 4608, [nch, H], F32)
        W = self.i["w_in_a"][l]
        seqw = N // nsq
        self.tmp_reset()
        cbs = [self.tmp([nsq, seqw + 3]) for _ in range(2)]
        accs = [self.tmp([N]) for _ in range(2)]
        qks = [self.tmp([N]) for _ in range(2)]
        rss = [self.tmp([N]) for _ in range(2)]
        sqvs = [self.tmp([N]) for _ in range(2)]
        hist = self.chist[:, l, :, :] if tc.kind == 'p' else None
        if tc.kind == 's':
            hs = V(self, sm + 5120, [36, nsq * 3], F32)
            st = self.tmp([QKV])
            R = nsq * 3
            self.DMA(st.ap[0:R, :], self.i["sc"][l], st.r()[0], [], st.r())
            for g in range(9):
                b = self.bank()
                for j in range(4):
                    jj = g * 4 + j
                    self.TR(self.pb[b][:, j * R:(j + 1) * R], st.ap[0:R, jj * 128:(jj + 1) * 128], self.ident[0:R, 0:R],
                            st.r() + [self.r_cst], [self.rpb[b]])
                self.A(hs.ap[:, g * 4:(g + 1) * 4, :], self.pb[b][:, 0:4 * R].rearrange("p (j r) -> p j r", r=R), AF.Copy,
                       [self.rpb[b]], hs.r())
        newh = V(self, sm + 7168, [36, nsq * 3], F32) if tc.kind == 's' else None
        cw = self.convw
        ci = [0]

        def epi_qkv(m, rows, ps, b):
            cb = cbs[ci[0] % 2]
            acc, qk, rs, sqv = accs[ci[0] % 2], qks[ci[0] % 2], rss[ci[0] % 2], sqvs[ci[0] % 2]
            ci[0] += 1
            if tc.kind == 'p':
                self.CP(cb.ap[:, 0, 0:3], self.chist[:, l, m, :], [self.r_chist[l]], cb.r())
            else:
                self.CP(cb.ap[:, :, 0:3], hs.ap[:, m, :].rearrange("p (s t) -> p s t", t=3), hs.r(), cb.r())
            self.A(cb.ap[:, :, 3:3 + seqw], ps.rearrange("p (s t) -> p s t", t=seqw), AF.Copy, [self.rpb[b]], cb.r())
            if tc.kind == 'p':
                self.CP(self.chist[:, l, m, :], cb.ap[:, 0, seqw:seqw + 3], cb.r(), [self.r_chist[l]])
            else:
                self.CP(newh.ap[:, m, :].rearrange("p (s t) -> p s t", t=3), cb.ap[:, :, seqw:seqw + 3], cb.r(), newh.r())
            a3 = acc.ap[:, 0:N].rearrange("p (s t) -> p s t", t=seqw)
            wcol = lambda tap: cw[:, l * 144 + tap * 36 + m:l * 144 + tap * 36 + m + 1]
            self.TS(a3, cb.ap[:, :, 3:3 + seqw], wcol(3), None, ALU.mult, None, cb.r() + [self.r_misc], acc.r())
            for tap in range(3):
                self.STT(a3, cb.ap[:, :, tap:tap + seqw], wcol(tap), a3, ALU.mult, ALU.add,
                         cb.r() + [self.r_misc] + acc.r(), acc.r())
            kind, h = m // H, m % H
            if kind == 2:
                self.A(vT.ap[:, h, 0:N], acc.ap[:, 0:N], AF.Silu, acc.r(), vT.r(h))
                return
            self.A(qk.ap[:, 0:N], acc.ap[:, 0:N], AF.Silu, acc.r(), qk.r())
            self.A(sqv.ap[:, 0:N], qk.ap[:, 0:N], AF.Square, qk.r(), sqv.r())
            b2 = self.bank()
            self.MM(self.pb[b2][:, 0:N], self.ones, sqv.ap[:, 0:N], True, True, [self.r_cst] + sqv.r(), [self.rpb[b2]])
            self.rsqrt(rs.ap[:, 0:N], self.pb[b2][:, 0:N], [self.rpb[b2]], rs.r(), 1.0)
            dst = qn if kind == 0 else kn
            self.STT(dst.ap[:, h, 0:N], qk.ap[:, 0:N], SCALE if kind == 0 else 1.0, rs.ap[:, 0:N], ALU.mult, ALU.mult,
                     qk.r() + rs.r(), dst.r(h))
        self.linear_fm(W, 0, QKV, hT, N, epi_qkv)

        def epi_z(m, rows, ps, b):
            self.A(zT.ap[:, m, 0:N], ps, AF.Silu, [self.rpb[b]], zT.r(m))
        self.linear_fm(W, QKV, TOK, hT, N, epi_z)

        def epi_mq(m, rows, ps, b):
            self.A(mq.ap[:, m, 0:N], ps, AF.Copy, [self.rpb[b]], mq.r(m))
        self.linear_fm(W, QKV + TOK + 24, 512, hT, N, epi_mq)
        o1 = QKV + TOK
        self.DMA(self.wsm[:, :, :], W[:, o1:o1 + 24].rearrange("(kc p) c -> p kc c", p=128), self.r_wsm, [], [self.r_wsm], eng="pool")
        bab = self.bank()
        for c in range(nch):
            for kc in range(KC):
                self.MM(self.pb[bab][0:L, c * 24:(c + 1) * 24], hT.ap[:, kc, c * L:(c + 1) * L], self.wsm[:, kc, :],
                        kc == 0, kc == KC - 1, hT.r(kc) + [self.r_wsm], [self.rpb[bab]])
        self.A(ab.ap[0:L, :, :], self.pb[bab][0:L, 0:nch * 24].rearrange("p (c f) -> p c f", f=24), AF.Copy, [self.rpb[bab]], ab.r())
        self.conv_state_out(l, tc, newh)
        bc = lambda ap: ap.unsqueeze(1).to_broadcast([L, nch, H])
        av, bv = ab.ap[0:L, :, 0:H], ab.ap[0:L, :, H:2 * H]
        x_, ax, g_, be = t1.ap[0:L], t2.ap[0:L], gam.ap[0:L], bet.ap[0:L]
        self.TT(x_, av, bc(self.vecs[0:L, 24 + l * H:24 + (l + 1) * H]), ALU.add, ab.r() + [self.r_misc], t1.r())
        self.STT(ax, x_, -1.0, x_, ALU.mult, ALU.max, t1.r(), t2.r())
        self.A(ax, ax, AF.Exp, t2.r(), t2.r(), scale=-1.0)
        self.A(ax, ax, AF.Ln, t2.r(), t2.r(), bias=1.0)
        self.STT(x_, x_, 0.0, ax, ALU.max, ALU.add, t1.r() + t2.r(), t1.r())
        self.TT(g_, x_, bc(self.nA[0:L, l * H:(l + 1) * H]), ALU.mult, t1.r() + [self.r_misc], gam.r())
        self.A(be, bv, AF.Sigmoid, ab.r(), bet.r())
        g2 = gam.ap[0:L].rearrange("p c h -> p (c h)")
        b1 = self.bank()
        self.MM(self.pb[b1][0:L, 0:nch * H], self.triu[0:L, 0:L], g2, True, True, [self.r_cst] + gam.r(), [self.rpb[b1]])
        self.A(Gc.ap[0:L].rearrange("p c h -> p (c h)"), self.pb[b1][0:L, 0:nch * H], AF.Copy, [self.rpb[b1]], Gc.r())
        self.A(eG.ap[0:L].rearrange("p c h -> p (c h)"), self.pb[b1][0:L, 0:nch * H], AF.Exp, [self.rpb[b1]], eG.r())
        b2 = self.bank()
        self.MM(self.pb[b2][:, 0:nch * H], self.ones[0:L, :], g2, True, True, [self.r_cst] + gam.r(), [self.rpb[b2]])
        self.A(eGL.ap.rearrange("p c h -> p (c h)"), self.pb[b2][:, 0:nch * H], AF.Exp, [self.rpb[b2]], eGL.r())
        self.TT(eGm.ap[0:L].rearrange("p c h -> p (c h)"), self.pb[b2][0:L, 0:nch * H], Gc.ap[0:L].rearrange("p c h -> p (c h)"),
                ALU.subtract, [self.rpb[b2]] + Gc.r(), eGm.r())
        self.A(eGm.ap[0:L], eGm.ap[0:L], AF.Exp, eGm.r(), eGm.r())
        cat = V(self, AR_H, [KC, N], BF16)
        for h0 in range(0, H, 2):
            gens = [self.gdn_head(l, tc, h0 + j, j, qn, kn, vT, zT, gam, bet, Gc, eG, eGm, eGL, cat) for j in range(2)]
            while gens:
                for g in list(gens):
                    try:
                        next(g)
                    except StopIteration:
                        gens.remove(g)
        self.tmp_reset()
        self.tmp_s = self.tmp([512])
        self.mem_attend(l, tc, mq, cat)
        self.out_proj(l, tc, cat)

    def conv_state_out(self, l, tc, newh):
        if tc.kind == 'p':
            if tc.t != self.ntp - 1:
                return
            src = lambda j0, j1: self.chist[:, l, j0:j1, :]
            rr = [self.r_chist[l]]
            R, dst = 3, self.o["pc"][l * 3:(l + 1) * 3, :]
        else:
            src = lambda j0, j1: newh.ap[:, j0:j1, :]
            rr = newh.r()
            R, dst = 3 * NSEQ, self.o["sco"][l * 3 * NSEQ:(l + 1) * 3 * NSEQ, :]
        self.tmp_reset(AR_TMP + 16384)
        st = self.tmp([QKV])
        for j in range(36):
            b = self.bank()
            inp = self.chist[:, l, j, :] if tc.kind == 'p' else newh.ap[:, j, :]
            self.TR(self.pb[b][0:R, 0:128], inp, self.ident, rr + [self.r_cst], [self.rpb[b]])
            self.A(st.ap[0:R, j * 128:(j + 1) * 128], self.pb[b][0:R, 0:128], AF.Copy, [self.rpb[b]], st.r())
        self.DMA(dst, st.ap[0:R, :], st.r()[0], st.r(), [])

    def gdn_head(self, l, tc, h, slot, qn, kn, vT, zT, gam, bet, Gc, eG, eGm, eGL, cat):
        N, L, nch = tc.N, tc.L, tc.nch
        base = AR_TMP + slot * 20480
        rA, rB, rC, rD, rE = base, base + 6144, base + 8192, base + 10240, base + 16384
        gU = V(self, rA, [nch, L], F32)
        EG = V(self, rA + 2048, [N], F32)
        dec = V(self, rA + 4096, [nch, L], F32)
        Vtok = V(self, rA, [nch, HD], BF16)
        Kg = V(self, rA + 2048, [nch, HD], BF16)
        Kp = V(self, rA + 4096, [nch, HD], BF16)
        nNb = V(self, rB, [nch, L], F32)
        nyw = V(self, rB, [N], BF16)
        Wt = V(self, rC, [nch, L], F32)
        Xt = V(self, rD, [nch, L], F32)
        Wb = [[V(self, rD + 2048 + 1024 * (2 * i + j), [nch, L], BF16) for j in range(2)] for i in range(2)]
        o2 = V(self, rD, [N], F32)
        rstd = V(self, rD + 2048, [N], F32)
        ot = V(self, rD + 4096, [N], F32)
        QgT = V(self, rE, [N], BF16)
        Xtb = V(self, rE + 1024, [nch, L], BF16)
        attnT = V(self, rE + 2048, [nch, L], BF16)
        vnew = [V(self, rE + 3072 + 256 * i, [HD], BF16) for i in range(2)]
        Sb = [V(self, rE + 3584 + 256 * i, [HD], BF16) for i in range(2)]
        cstr = [self.r_cst]
        bcl = lambda v: v.ap[0:L, :, h:h + 1].to_broadcast([L, nch, L])
        bcd = lambda v: v.ap[0:L, :, h:h + 1].to_broadcast([L, nch, HD])
        mask_b = lambda m: m[0:L, 0:L].unsqueeze(1).to_broadcast([L, nch, L])
        p3 = lambda b: self.pb[b][0:L, 0:N].rearrange("p (c i) -> p c i", i=L)
        self.TT(gU.ap[0:L], bcl(gam), mask_b(self.triu), ALU.mult, gam.r() + cstr, gU.r())
        yield
        bg = self.bank()
        self.MM(self.pb[bg][:, 0:N], self.ones[0:L, :], gU.ap[0:L].rearrange("p c i -> p (c i)"), True, True, cstr + gU.r(), [self.rpb[bg]])
        yield
        self.A(EG.ap[:, 0:N], self.pb[bg][:, 0:N], AF.Exp, [self.rpb[bg]], EG.r())
        self.TT(dec.ap[0:L], p3(bg), bcl(Gc), ALU.subtract, [self.rpb[bg]] + Gc.r(), dec.r())
        yield
        self.TT(QgT.ap[:, 0:N], qn.ap[:, h, 0:N], EG.ap[:, 0:N], ALU.mult, qn.r(h) + EG.r(), QgT.r())
        self.TS(dec.ap[0:L], dec.ap[0:L], 0.0, None, ALU.min, None, dec.r(), dec.r())
        yield
        self.A(dec.ap[0:L], dec.ap[0:L], AF.Exp, dec.r(), dec.r())
        bk, bq = self.bank(), self.bank()
        for c in range(nch):
            cs = slice(c * L, (c + 1) * L)
            self.MM(self.pb[bk][0:L, cs], kn.ap[:, h, cs], kn.ap[:, h, cs], True, True, kn.r(h), [self.rpb[bk]])
        for c in range(nch):
            cs = slice(c * L, (c + 1) * L)
            self.MM(self.pb[bq][0:L, cs], kn.ap[:, h, cs], qn.ap[:, h, cs], True, True, kn.r(h) + qn.r(h), [self.rpb[bq]])
        yield
        self.TT(nNb.ap[0:L], dec.ap[0:L], mask_b(self.triusn), ALU.mult, dec.r() + cstr, nNb.r())
        self.TT(nNb.ap[0:L], nNb.ap[0:L], bcl(bet), ALU.mult, nNb.r() + bet.r(), nNb.r())
        self.TT(dec.ap[0:L], dec.ap[0:L], mask_b(self.triu), ALU.mult, dec.r() + cstr, dec.r())
        yield
        self.TT(Wt.ap[0:L], p3(bk), nNb.ap[0:L], ALU.mult, [self.rpb[bk]] + nNb.r(), Wt.r())
        self.TT(attnT.ap[0:L], p3(bq), dec.ap[0:L], ALU.mult, [self.rpb[bq]] + dec.r(), attnT.r())
        yield
        cur = Wb[0]
        self.A(cur[0].ap[0:L], Wt.ap[0:L], AF.Copy, Wt.r(), cur[0].r())
        self.TT(Xt.ap[0:L], Wt.ap[0:L], mask_b(self.ident), ALU.add, Wt.r() + cstr, Xt.r())
        yield
        bt = self.bank()
        ptb = self.pbf(bt)
        for c in range(nch):
            self.TR(ptb[0:L, c * L:(c + 1) * L], cur[0].ap[0:L, c, :], self.identb[0:L, 0:L], cur[0].r() + [self.r_cstb], [self.rpb[bt]])
        self.A(Xtb.ap[0:L], Xt.ap[0:L], AF.Copy, Xt.r(), Xtb.r())
        yield
        self.A(cur[1].ap[0:L], ptb[0:L, 0:N].rearrange("p (c i) -> p c i", i=L), AF.Copy, [self.rpb[bt]], cur[1].r())
        bv_ = self.bank()
        pv = self.pbf(bv_)
        for c in range(nch):
            self.TR(pv[0:L, c * HD:(c + 1) * HD], vT.ap[:, h, c * L:(c + 1) * L], self.identb, vT.r(h) + [self.r_cstb], [self.rpb[bv_]])
        bk_ = self.bank()
        pk = self.pbf(bk_)
        for c in range(nch):
            self.TR(pk[0:L, c * HD:(c + 1) * HD], kn.ap[:, h, c * L:(c + 1) * L], self.identb, kn.r(h) + [self.r_cstb], [self.rpb[bk_]])
        yield
        self.A(Vtok.ap[0:L], pv[0:L, 0:nch * HD].rearrange("p (c d) -> p c d", d=HD), AF.Copy, [self.rpb[bv_]], Vtok.r())
        pk3 = pk[0:L, 0:nch * HD].rearrange("p (c d) -> p c d", d=HD)
        self.TT(Kg.ap[0:L], pk3, bcd(eG), ALU.mult, [self.rpb[bk_]] + eG.r(), Kg.r())
        self.TT(Kp.ap[0:L], pk3, bcd(eGm), ALU.mult, [self.rpb[bk_]] + eGm.r(), Kp.r())
        yield
        nsteps = {64: 5, 32: 4}[L]
        for k in range(1, nsteps + 1):
            nxt = Wb[k % 2]
            last = k == nsteps
            b_p = self.bank()
            for c in range(nch):
                cs = slice(c * L, (c + 1) * L)
                self.MM(self.pb[b_p][0:L, cs], cur[0].ap[0:L, c, :], cur[1].ap[0:L, c, :], True, True, cur[0].r() + cur[1].r(), [self.rpb[b_p]])
            if not last:
                b_t = self.bank()
                for c in range(nch):
                    cs = slice(c * L, (c + 1) * L)
                    self.MM(self.pb[b_t][0:L, cs], cur[1].ap[0:L, c, :], cur[0].ap[0:L, c, :], True, True, cur[0].r() + cur[1].r(), [self.rpb[b_t]])
            yield
            self.A(nxt[1].ap[0:L], p3(b_p), AF.Copy, [self.rpb[b_p]], nxt[1].r())
            if not last:
                self.A(nxt[0].ap[0:L], p3(b_t), AF.Copy, [self.rpb[b_t]], nxt[0].r())
            yield
            b_x = self.bank()
            for c in range(nch):
                cs = slice(c * L, (c + 1) * L)
                self.MM(self.pb[b_x][0:L, cs], nxt[1].ap[0:L, c, :], Xtb.ap[0:L, c, :], True, True, nxt[1].r() + Xtb.r(), [self.rpb[b_x]])
            yield
            self.TT(Xt.ap[0:L], Xt.ap[0:L], p3(b_x), ALU.add, Xt.r() + [self.rpb[b_x]], Xt.r())
            yield
            self.A(Xtb.ap[0:L], Xt.ap[0:L], AF.Copy, Xt.r(), Xtb.r())
            yield
            cur = nxt
        by = self.bank()
        for c in range(nch):
            cs = slice(c * L, (c + 1) * L)
            self.MM(self.pb[by][:, cs], Kg.ap[0:L, c, :], Xtb.ap[0:L, c, :], True, True, Kg.r() + Xtb.r(), [self.rpb[by]])
        yield
        self.A(nyw.ap[:, 0:N], self.pb[by][:, 0:N], AF.Copy, [self.rpb[by]], nyw.r(), scale=-1.0)
        yield
        bo = self.pin()
        for c in range(nch):
            cs = slice(c * L, (c + 1) * L)
            if tc.kind == 'p':
                S_ap, S_r = self.S[:, l, h, :], [self.r_S[l][h]]
            else:
                si = self.sSi
                self.sSi = (self.sSi + 1) % 4
                S_ap, S_r = self.sS[si][:, :], [self.r_sS[si]]
                self.DMA(S_ap, self.i["sg"][l, c, h], self.r_sS[si], [], S_r)
            sb = Sb[c % 2]
            self.A(sb.ap[:, :], S_ap, AF.Copy, S_r, sb.r())
            vn = vnew[c % 2]
            b1 = self.bank()
            self.MM(self.pb[b1][0:L, 0:HD], Xtb.ap[0:L, c, :], Vtok.ap[0:L, c, :], True, False, Xtb.r() + Vtok.r(), [self.rpb[b1]])
            yield
            self.MM(self.pb[b1][0:L, 0:HD], nyw.ap[:, cs], sb.ap[:, :], False, True, nyw.r() + sb.r(), [self.rpb[b1]])
            self.MM(self.pb[bo][:, cs], sb.ap[:, :], QgT.ap[:, cs], True, False, sb.r() + QgT.r(), [self.rpb[bo]])
            yield
            self.A(vn.ap[0:L, :], self.pb[b1][0:L, 0:HD], AF.Copy, [self.rpb[b1]] + bet.r(), vn.r(), scale=bet.ap[0:L, c, h:h + 1])
            yield
            self.MM(self.pb[bo][:, cs], vn.ap[0:L, :], attnT.ap[0:L, c, :], False, True, vn.r() + attnT.r(), [self.rpb[bo]])
            b2 = self.bank()
            self.MM(self.pb[b2][:, 0:HD], Kp.ap[0:L, c, :], vn.ap[0:L, :], True, True, Kp.r() + vn.r(), [self.rpb[b2]])
            yield
            self.STT(S_ap, S_ap, eGL.ap[:, c, h:h + 1], self.pb[b2][:, 0:HD], ALU.mult, ALU.add, S_r + eGL.r() + [self.rpb[b2]], S_r)
            if tc.kind == 's':
                self.DMA(self.o["sgo"][l, c, h], S_ap, S_r[0], S_r, [])
            elif tc.t == self.ntp - 1 and c == nch - 1:
                self.DMA(self.o["pg"][l, h], S_ap, S_r[0], S_r, [])
            yield
        self.A(o2.ap[:, 0:N], self.pb[bo][:, 0:N], AF.Square, [self.rpb[bo]], o2.r())
        yield
        bs = self.bank()
        self.MM(self.pb[bs][:, 0:N], self.ones, o2.ap[:, 0:N], True, True, cstr + o2.r(), [self.rpb[bs]])
        yield
        self.rsqrt(rstd.ap[:, 0:N], self.pb[bs][:, 0:N], [self.rpb[bs]], rstd.r(), 1.0 / HD)
        yield
        self.TT(ot.ap[:, 0:N], self.pb[bo][:, 0:N], rstd.ap[:, 0:N], ALU.mult, [self.rpb[bo]] + rstd.r(), ot.r())
        self.unpin(bo)
        yield
        self.STT(cat.ap[:, h, 0:N], ot.ap[:, 0:N], self.gdng[:, l:l + 1], zT.ap[:, h, 0:N], ALU.mult, ALU.mult,
                 ot.r() + [self.r_misc] + zT.r(h), cat.r(h))

    def kv_share(self, tc):
        N = tc.N
        hT = V(self, AR_H, [KC, N], BF16)
        self.prenorm(5, 0, hT, N)
        W = self.i["w_kvf"]
        kst = V(self, AR_BIG if tc.kind == 'p' else AR_BIG + 40960, [H, N], BF16)
        ost = [V(self, AR_BIG + 16384 + j * 1024, [WCOLS], F32) for j in range(6)]
        osb = [V(self, AR_BIG + 24576 + j * 1024, [WCOLS], BF16) for j in range(6)]
        oc = [0]
        if tc.kind == 'p':
            toks = [(tb * 128, 128) for tb in range(N // 128)]
            row0 = tc.t * NP
            rows_of = lambda ti: slice(row0 + ti * 128, row0 + (ti + 1) * 128)
            M_of = lambda ti: 128
            ko, vo, lo = self.o["pk"], self.o["pv"], self.o["plf"]
        else:
            toks = [(sq * DSEQ, DSEQ) for sq in range(NSEQ)]
            rows_of = lambda ti: slice(ti * DSEQ, (ti + 1) * DSEQ)
            M_of = lambda ti: DSEQ
            ko, vo, lo = self.o["sk"], self.o["sv"], self.o["slf"]
            self.Vnew = V(self, AR_BIG + 44032, [NSEQ, TOK], BF16)
        self.KTnew = kst

        def epi_k(m, rows, ps, b):
            self.A(kst.ap[:, m, 0:N], ps, AF.Copy, [self.rpb[b]], kst.r(m))
        self.linear_fm(W, 0, TOK, hT, N, epi_k)
        if tc.kind == 'p':
            self.DMA(self.kscr[:, :, tc.t * NP:(tc.t + 1) * NP], kst.ap[:, :, :], kst.r()[0], kst.r(), [self.r_kscr[tc.t]])

        def epi_tm(ti, cb, ncb, ps, b, which=0):
            j = oc[0] % 6
            oc[0] += 1
            M = M_of(ti)
            self.A(ost[j].ap[0:M, 0:ncb], ps, AF.Copy, [self.rpb[b]], ost[j].r())
            self.DMA((ko if which == 0 else vo)[rows_of(ti), cb:cb + ncb], ost[j].ap[0:M, 0:ncb], ost[j].r()[0], ost[j].r(), [])
            if which == 1:
                if tc.kind == 'p':
                    self.CP(osb[j].ap[0:M, 0:ncb], ost[j].ap[0:M, 0:ncb], ost[j].r(), osb[j].r())
                    self.DMA(self.vscr[rows_of(ti), cb:cb + ncb], osb[j].ap[0:M, 0:ncb], osb[j].r()[0], osb[j].r(), [self.r_vscr[tc.t]])
                else:
                    self.CP(self.Vnew.ap[0:M, ti, cb:cb + ncb], ost[j].ap[0:M, 0:ncb], ost[j].r(), self.Vnew.r())
        self.linear_tm(W, 0, TOK, hT, toks, lambda *a: epi_tm(*a, which=0))
        self.linear_tm(W, TOK, TOK, hT, toks, lambda *a: epi_tm(*a, which=1))
        self.DMA(self.wsm[:, :, 0:H], W[:, 2 * TOK:2 * TOK + H].rearrange("(kc p) c -> p kc c", p=128), self.r_wsm, [], [self.r_wsm], eng="pool")
        self.tmp_reset()
        lf = self.tmp([len(toks), H])
        t1 = self.tmp([len(toks), H])
        t2 = self.tmp([len(toks), H])
        M = toks[0][1]
        nt = len(toks)
        b = self.bank()
        for ti, (t0, M_) in enumerate(toks):
            for kc in range(KC):
                self.MM(self.pb[b][0:M, ti * H:(ti + 1) * H], hT.ap[:, kc, t0:t0 + M], self.wsm[:, kc, 0:H], kc == 0, kc == KC - 1,
                        hT.r(kc) + [self.r_wsm], [self.rpb[b]])
        x_, ax = t1.ap[0:M], t2.ap[0:M]
        self.TT(x_, self.pb[b][0:M, 0:nt * H].rearrange("p (t h) -> p t h", h=H),
                self.vecs[0:M, 48:60].unsqueeze(1).to_broadcast([M, nt, H]), ALU.add, [self.rpb[b], self.r_misc], t1.r())
        self.STT(ax, x_, -1.0, x_, ALU.mult, ALU.max, t1.r(), t2.r())
        self.A(ax, ax, AF.Exp, t2.r(), t2.r(), scale=-1.0)
        self.A(ax, ax, AF.Ln, t2.r(), t2.r(), bias=1.0)
        self.STT(lf.ap[0:M], x_, 0.0, ax, ALU.min, ALU.subtract, t1.r() + t2.r(), lf.r())
        for ti in range(nt):
            self.DMA(lo[rows_of(ti), :], lf.ap[0:M, ti, :], lf.r()[0], lf.r(), [])
        return lf

    def cumsum_prompt(self, tc, lf):
        for ti in range(tc.N // 128):
            blk = tc.t * (NP // 128) + ti
            b = self.bank()
            first = blk == 0
            self.MM(self.pb[b][:, 0:H], self.triu, lf.ap[:, ti, :], True, first, [self.r_cst] + lf.r(), [self.rpb[b]])
            if not first:
                self.MM(self.pb[b][:, 0:H], self.elast, self.Cp[:, blk - 1, :], False, True, [self.r_cst, self.r_Cp], [self.rpb[b]])
            self.A(self.Cp[:, blk, :], self.pb[b][:, 0:H], AF.Copy, [self.rpb[b]], [self.r_Cp])
        blk = tc.t * (NP // 128) + tc.N // 128 - 1
        b = self.bank()
        self.MM(self.pb[b][:, 0:H], self.elast, self.Cp[:, blk, :], True, True, [self.r_cst, self.r_Cp], [self.rpb[b]])
        self.A(self.cendp[:, :], self.pb[b][:, 0:H], AF.Copy, [self.rpb[b]], [self.r_cend])

    def fox_in(self, l, tc):
        N = tc.N
        hT = V(self, AR_H, [KC, N], BF16)
        self.prenorm(0, l, hT, N)
        qT = V(self, AR_BIG, [H, N], BF16)
        sg = V(self, AR_BIG + 12288, [H, N], BF16)
        mq = V(self, AR_BIG + 24576, [MH, N], BF16)
        W = self.i["w_in_b"][l - 2]

        def epi_q(m, rows, ps, b):
            self.A(qT.ap[:, m, 0:N], ps, AF.Copy, [self.rpb[b]], qT.r(m))

        def epi_g(m, rows, ps, b):
            self.A(sg.ap[:, m, 0:N], ps, AF.Sigmoid, [self.rpb[b]], sg.r(m))

        def epi_mq(m, rows, ps, b):
            self.A(mq.ap[:, m, 0:N], ps, AF.Copy, [self.rpb[b]], mq.r(m))
        self.linear_fm(W, 0, TOK, hT, N, epi_q)
        self.linear_fm(W, TOK, TOK, hT, N, epi_g)
        self.linear_fm(W, 2 * TOK, 512, hT, N, epi_mq)
        return qT, sg, mq

    def fox_layer_p(self, l, tc):
        N = tc.N
        qT, sg, mq = self.fox_in(l, tc)
        cat = V(self, AR_H, [KC, N], BF16)
        self.tmp_reset()
        nkb = (tc.t + 1) * (NP // 128)
        biasK = self.tmp([nkb, H])
        Ksb = [self.tmp([2, NP], BF16) for _ in range(2)]
        Vsb = [self.tmp([NP // 128, 256], BF16) for _ in range(2)]
        pT = [self.tmp([N], BF16) for _ in range(4)]
        rden = self.tmp([N])
        ot = self.tmp([N])
        self.tmp_s = self.tmp([512])
        self.TT(biasK.ap[:, :, :], self.cendp[:, :].unsqueeze(1).to_broadcast([128, nkb, H]), self.Cp[:, 0:nkb, :], ALU.subtract,
                [self.r_cend, self.r_Cp], biasK.r())
        SK = 2
        si = 0
        for hg in range(H // 2):
            acc = [(self.pin(), self.pin()) for _ in range(2)]
            units = [(sb_, hh, kb) for sb_ in range(tc.t + 1) for hh in range(2) for kb in range(NP // 128)]
            bufs = {}
            pend = {}

            def stage1(ui, u):
                nonlocal si
                sb_, hh, kb = u
                if sb_ not in bufs:
                    ks, vs = Ksb[si % 2], Vsb[si % 2]
                    si += 1
                    self.DMA(ks.ap[:, :, :], self.kscr[:, hg * 2:hg * 2 + 2, sb_ * NP:(sb_ + 1) * NP], ks.r()[0], [self.r_kscr[sb_]], ks.r())
                    self.DMA(vs.ap[:, :, :], self.vscr[sb_ * NP:(sb_ + 1) * NP, hg * 256:(hg + 1) * 256].rearrange("(b p) c -> p b c", p=128),
                             vs.r()[0], [self.r_vscr[sb_]], vs.r())
                    bufs[sb_] = (ks, vs)
                ks, vs = bufs[sb_]
                h = hg * 2 + hh
                diag = sb_ == tc.t
                kg = sb_ * (NP // 128) + kb
                q0 = kb * 128 if diag else 0
                nq = N - q0
                b = self.bank()
                self.MM(self.pb[b][:, 0:nq], ks.ap[:, hh, kb * 128:(kb + 1) * 128], qT.ap[:, h, q0:N], True, True,
                        ks.r() + qT.r(h), [self.rpb[b]])
                p = pT[ui % len(pT)]
                self.A(p.ap[:, 0:nq], self.pb[b][:, 0:nq], AF.Exp, [self.rpb[b]] + biasK.r(), p.r(),
                       bias=biasK.ap[:, kg, h:h + 1], scale=SCALE)
                if diag:
                    self.TT(p.ap[:, 0:128], p.ap[:, 0:128], self.triub, ALU.mult, p.r() + [self.r_cstb], p.r())
                pend[ui] = (p, vs, q0, nq, kg, hh, kb)

            def stage2(ui):
                p, vs, q0, nq, kg, hh, kb = pend.pop(ui)
                bo, bd = acc[hh]
                first = kg == 0
                last = kg == nkb - 1
                self.MM(self.pb[bo][:, q0:N], vs.ap[:, kb, hh * 128:(hh + 1) * 128], p.ap[:, 0:nq], first, last,
                        vs.r() + p.r(), [self.rpb[bo]])
                self.MM(self.pb[bd][:, q0:N], self.onesb, p.ap[:, 0:nq], first, last, [self.r_cstb] + p.r(), [self.rpb[bd]])

            nu = len(units)
            for ui in range(min(SK, nu)):
                stage1(ui, units[ui])
            for ui in range(nu):
                if ui + SK < nu:
                    stage1(ui + SK, units[ui + SK])
                stage2(ui)
            for hh in range(2):
                h = hg * 2 + hh
                bo, bd = acc[hh]
                self.REC(rden.ap[:, 0:N], self.pb[bd][:, 0:N], [self.rpb[bd]], rden.r())
                self.TT(ot.ap[:, 0:N], self.pb[bo][:, 0:N], rden.ap[:, 0:N], ALU.mult, [self.rpb[bo]] + rden.r(), ot.r())
                self.TT(cat.ap[:, h, 0:N], ot.ap[:, 0:N], sg.ap[:, h, 0:N], ALU.mult, ot.r() + sg.r(h), cat.r(h))
                self.unpin(bo)
                self.unpin(bd)
        self.mem_attend(l, tc, mq, cat)
        self.out_proj(l, tc, cat)

    def cumsum_sample(self, lf):
        nb = PAST // 128
        SH = NSEQ * H
        lst = self.tmp([nb, SH])
        for sq in range(NSEQ):
            self.DMA(lst.ap[:, :, sq * H:(sq + 1) * H], self.i["clf"][sq].rearrange("(b p) h -> p b h", p=128), lst.r()[0], [], lst.r())
        for blk in range(nb):
            b = self.bank()
            self.MM(self.pb[b][:, 0:SH], self.triu, lst.ap[:, blk, :], True, blk == 0, [self.r_cst] + lst.r(), [self.rpb[b]])
            if blk > 0:
                self.MM(self.pb[b][:, 0:SH], self.elast, self.Cs[:, blk - 1, :], False, True, [self.r_cst, self.r_Cs], [self.rpb[b]])
            self.A(self.Cs[:, blk, :], self.pb[b][:, 0:SH], AF.Copy, [self.rpb[b]], [self.r_Cs])
        lf2 = lf.ap[0:DSEQ].rearrange("p s h -> p (s h)")
        b = self.bank()
        self.MM(self.pb[b][0:DSEQ, 0:SH], self.triu[0:DSEQ, 0:DSEQ], lf2, True, False, [self.r_cst] + lf.r(), [self.rpb[b]])
        self.MM(self.pb[b][0:DSEQ, 0:SH], self.elast[:, 0:DSEQ], self.Cs[:, nb - 1, :], False, True, [self.r_cst, self.r_Cs], [self.rpb[b]])
        self.A(self.Cs[0:DSEQ, nb, :], self.pb[b][0:DSEQ, 0:SH], AF.Copy, [self.rpb[b]], [self.r_Cs])
        cend = V(self, AR_BIG + 60 * 1024, [SH], F32)
        b = self.bank()
        self.MM(self.pb[b][:, 0:SH], self.ones[0:DSEQ, :], lf2, True, False, [self.r_cst] + lf.r(), [self.rpb[b]])
        self.MM(self.pb[b][:, 0:SH], self.elast, self.Cs[:, nb - 1, :], False, True, [self.r_cst, self.r_Cs], [self.rpb[b]])
        self.A(cend.ap[:, :], self.pb[b][:, 0:SH], AF.Copy, [self.rpb[b]], cend.r())
        self.TT(self.Cs[:, :, :], cend.ap[:, :].unsqueeze(1).to_broadcast([128, nb + 1, SH]), self.Cs[:, :, :], ALU.subtract,
                cend.r() + [self.r_Cs], [self.r_Cs])

    def fox_layer_s(self, l, tc):
        N = tc.N
        qT, sg, mq = self.fox_in(l, tc)
        cat = V(self, AR_H, [KC, N], BF16)
        self.tmp_reset()
        nb = PAST // 128
        kst = [self.tmp([TOK])]
        KT = [self.tmp([H, 128], BF16) for _ in range(2)]
        VB = [self.tmp([TOK], BF16) for _ in range(2)]
        sc = self.tmp([H, DSEQ])
        pT = [self.tmp([H, DSEQ], BF16) for _ in range(2)]
        rden = self.tmp([H, DSEQ])
        ot = self.tmp([H, DSEQ])
        self.tmp_s = self.tmp([512])
        bi = 0
        for sq in range(NSEQ):
            q0 = sq * DSEQ
            bo, bd = self.pin(), self.pin()
            self.MS(self.pb[bo][:, 0:H * DSEQ], 0.0, [self.rpb[bo]])
            self.MS(self.pb[bd][:, 0:H * DSEQ], 0.0, [self.rpb[bd]])
            for blk in range(nb + 1):
                new = blk == nb
                R = DSEQ if new else 128
                kt, vb = KT[bi % 2], VB[bi % 2]
                ks = kst[0]
                bi += 1
                if not new:
                    self.DMA(ks.ap[:, :], self.i["ck"][sq, blk * 128:(blk + 1) * 128, :], ks.r()[0], [], ks.r())
                    for g in range(3):
                        b = self.bank()
                        for j in range(4):
                            hh = g * 4 + j
                            self.TR(self.pb[b][:, j * 128:(j + 1) * 128], ks.ap[:, hh * 128:(hh + 1) * 128], self.ident,
                                    ks.r() + [self.r_cst], [self.rpb[b]])
                        self.A(kt.ap[:, g * 4:(g + 1) * 4, :], self.pb[b][:, :].rearrange("p (j t) -> p j t", t=128), AF.Copy,
                               [self.rpb[b]], kt.r())
                    self.DMA(vb.ap[:, :], self.i["cv"][sq, blk * 128:(blk + 1) * 128, :], vb.r()[0], [], vb.r(), eng="pool")
                    ktap = lambda hh: kt.ap[:, hh, :]
                    vbap = lambda hh: vb.ap[:, hh * 128:(hh + 1) * 128]
                    ktr, vbr = kt.r(), vb.r()
                else:
                    ktap = lambda hh: self.KTnew.ap[:, hh, q0:q0 + DSEQ]
                    vbap = lambda hh: self.Vnew.ap[0:DSEQ, sq, hh * 128:(hh + 1) * 128]
                    ktr, vbr = self.KTnew.r(), self.Vnew.r()
                b = self.bank()
                for hh in range(H):
                    self.MM(self.pb[b][0:R, hh * DSEQ:(hh + 1) * DSEQ], ktap(hh), qT.ap[:, hh, q0:q0 + DSEQ], True, True,
                            ktr + qT.r(hh), [self.rpb[b]])
                self.STT(sc.ap[0:R], self.pb[b][0:R, 0:H * DSEQ].rearrange("p (h q) -> p h q", q=DSEQ), SCALE,
                         self.Cs[0:R, blk, sq * H:(sq + 1) * H].unsqueeze(2).to_broadcast([R, H, DSEQ]), ALU.mult, ALU.add,
                         [self.rpb[b], self.r_Cs], sc.r())
                p = pT[bi % 2]
                self.A(p.ap[0:R], sc.ap[0:R], AF.Exp, sc.r(), p.r())
                if new:
                    self.TT(p.ap[0:R], p.ap[0:R], self.triub[0:R, 0:DSEQ].unsqueeze(1).to_broadcast([R, H, DSEQ]), ALU.mult,
                            p.r() + [self.r_cstb], p.r())
                for hh in range(H):
                    cs = slice(hh * DSEQ, (hh + 1) * DSEQ)
                    self.MM(self.pb[bo][:, cs], vbap(hh), p.ap[0:R, hh, :], False, new and hh == H - 1, vbr + p.r(), [self.rpb[bo]], skip=True)
                    self.MM(self.pb[bd][:, cs], self.onesb[0:R, :], p.ap[0:R, hh, :], False, new and hh == H - 1, [self.r_cstb] + p.r(), [self.rpb[bd]], skip=True)
            p3 = lambda b: self.pb[b][:, 0:H * DSEQ].rearrange("p (h q) -> p h q", q=DSEQ)
            self.REC(rden.ap[:, :, :], p3(bd), [self.rpb[bd]], rden.r())
            self.TT(ot.ap[:, :, :], p3(bo), rden.ap[:, :, :], ALU.mult, [self.rpb[bo]] + rden.r(), ot.r())
            self.TT(cat.ap[:, 0:H, q0:q0 + DSEQ], ot.ap[:, :, :], sg.ap[:, :, q0:q0 + DSEQ], ALU.mult, ot.r() + sg.r(), cat.r(0, H))
            self.unpin(bo)
            self.unpin(bd)
        self.mem_attend(l, tc, mq, cat)
        self.out_proj(l, tc, cat)

    def run_tile(self, tc, src, dst):
        self.load_xT(src, tc.N)
        nl = self.nlayers
        for l in range(min(2, nl)):
            self.gdn_layer(l, tc)
            self.mlp(l, tc.N)
        if nl > 2:
            lf = self.kv_share(tc)
            if tc.kind == 'p':
                self.cumsum_prompt(tc, lf)
            else:
                self.cumsum_sample(lf)
            for l in range(2, nl):
                if tc.kind == 'p':
                    self.fox_layer_p(l, tc)
                else:
                    self.fox_layer_s(l, tc)
                self.mlp(l, tc.N)
        self.store_y(dst, tc.N)

    def cumsum_end_only(self, tc):
        cend = self.tmp([H])
        blk = tc.t * (NP // 128) + tc.N // 128 - 1
        b = self.bank()
        self.MM(self.pb[b][:, 0:H], self.elast, self.Cp[:, blk, :], True, True, [self.r_cst, self.r_Cp], [self.rpb[b]])
        self.A(cend.ap[:, :], self.pb[b][:, 0:H], AF.Copy, [self.rpb[b]], cend.r())
        return cend

    def build(self):
        self.prologue()
        if self.ntp > 0:
            self.memory_kv()
        for t in range(self.ntp):
            tc = TileCfg('p', NP, 64, t)
            self.run_tile(tc, self.i["xp"][t * NP:(t + 1) * NP, :], self.o["yp"][t * NP:(t + 1) * NP, :])
        if self.do_sample:
            tc = TileCfg('s', NS, DSEQ, 0)
            self.run_tile(tc, self.i["xs"], self.o["ys"])
        return self.P.emit()


def make_consts():
    c = np.zeros((128, 640), np.float32)
    c[:, 0:128] = np.eye(128)
    c[:, 128:256] = 1.0
    c[:, 256:384] = np.triu(np.ones((128, 128)))
    c[:, 384:512] = -np.triu(np.ones((128, 128)), 1)
    c[127, 512:640] = 1.0
    return c


def build_program(ntp=8, do_sample=True, nlayers=4):
    nc = bass.Bass("TRN2", target_bir_lowering=False)
    with ExitStack() as es:
        k = K(nc, es, ntp=ntp, do_sample=do_sample, nlayers=nlayers)
        stats = k.build()
    return nc, stats


def core_inputs(c, inp):
    b = c // 2
    s0 = c * NSEQ
    f = lambda a: np.ascontiguousarray(a, dtype=np.float32)
    m = dict(
        xp=f(inp["x_prompt"][b]), xs=f(inp["x_sample"][s0:s0 + NSEQ].reshape(NS, D)),
        sg=f(inp["state_gdn"][:, s0:s0 + NSEQ]), sc=f(inp["state_conv"][:, s0:s0 + NSEQ].reshape(2, NSEQ * 3, QKV)),
        ck=f(inp["cache_k"][s0:s0 + NSEQ].reshape(NSEQ, PAST, TOK)), cv=f(inp["cache_v"][s0:s0 + NSEQ].reshape(NSEQ, PAST, TOK)),
        clf=f(inp["cache_logf"][s0:s0 + NSEQ]),
        cmk=f(inp["cache_mem_k"][:, s0:s0 + NSEQ].reshape(4, NSEQ, NMEM, 512)),
        cmv=f(inp["cache_mem_v"][:, s0:s0 + NSEQ].reshape(4, NSEQ, NMEM, 512)),
        mp=f(inp["mem_prompt"][b]),
        g_pre=f(inp["norm_mix_pre"].reshape(64, 128)), g_post=f(inp["norm_mix_post"].reshape(64, 128)),
        g_mpre=f(inp["norm_mlp_pre"].reshape(64, 128)), g_mpost=f(inp["norm_mlp_post"].reshape(64, 128)),
        g_mem=f(inp["norm_mem"].reshape(64, 128)), g_kv=f(inp["norm_kv"].reshape(16, 128)),
        convw=f(inp["conv_w_a"].reshape(288, 128)), a_log=f(inp["a_log"].reshape(24)), dt_bias=f(inp["dt_bias"].reshape(24)),
        gdn_norm=f(inp["gdn_norm"]), b_f=f(inp["b_f"]),
        w_in_a=f(inp["w_in_a"]), w_in_b=f(inp["w_in_b"]), w_kvf=f(inp["w_kvf"]), w_mem=f(inp["w_mem_kv"]),
        w_o=f(inp["w_o"]), w_up=f(inp["w_up"]), w_down=f(inp["w_down"]),
        cst=make_consts(),
    )
    return m


def kernel(**inp):
    inp = {k: np.asarray(v) for k, v in inp.items()}
    nc, _ = build_program()
    in_maps = [core_inputs(c, inp) for c in range(8)]
    res = run_bass_kernel_spmd(nc, in_maps, core_ids=list(range(8))).results
    ev = [res[c] for c in range(0, 8, 2)]
    st = lambda key, sh: np.stack([r[key] for r in ev]).reshape(sh).astype(np.float32)
    cat = lambda key: np.concatenate([r[key] for r in res], axis=0)
    B = 4
    y_prompt = st("yp", (B, SEQ, D))
    y_sample = cat("ys").reshape(32, DSEQ, D)
    p_gdn = np.stack([r["pg"] for r in ev], axis=1)
    p_conv = np.stack([r["pc"].reshape(2, 3, QKV) for r in ev], axis=1)
    p_k = st("pk", (B, SEQ, H, HD))
    p_v = st("pv", (B, SEQ, H, HD))
    p_logf = st("plf", (B, SEQ, H))
    p_mem_k = np.stack([r["pmk"].reshape(4, NMEM, MH, HD) for r in ev], axis=1)
    p_mem_v = np.stack([r["pmv"].reshape(4, NMEM, MH, HD) for r in ev], axis=1)
    s_gdn = np.concatenate([r["sgo"] for r in res], axis=1)
    s_conv = np.concatenate([r["sco"].reshape(2, NSEQ, 3, QKV) for r in res], axis=1)
    s_k = cat("sk").reshape(32, DSEQ, H, HD)
    s_v = cat("sv").reshape(32, DSEQ, H, HD)
    s_logf = cat("slf").reshape(32, DSEQ, H)
    outs = (y_prompt, y_sample, p_gdn, p_conv, p_k, p_v, p_logf, p_mem_k, p_mem_v, s_gdn, s_conv, s_k, s_v, s_logf)
    return tuple(np.ascontiguousarray(o, dtype=np.float32) for o in outs)
```

```python
import numpy as np
from contextlib import ExitStack
import concourse.bass as bass
import concourse.mybir as mybir
from concourse.bass_utils import run_bass_kernel_spmd

F32 = mybir.dt.float32
BF16 = mybir.dt.bfloat16
AF = mybir.ActivationFunctionType
ALU = mybir.AluOpType


class Res:
    __slots__ = ("name", "excl", "lw", "rd", "sem", "semcnt")

    def __init__(self, name, excl=False):
        self.name = name
        self.excl = excl
        self.lw = None
        self.rd = []
        self.sem = None
        self.semcnt = 0


class Op:
    __slots__ = ("eng", "fn", "deps", "sig", "isdma", "sem", "val")

    def __init__(self, eng, fn, isdma):
        self.eng = eng
        self.fn = fn
        self.deps = []
        self.sig = False
        self.isdma = isdma
        self.sem = None
        self.val = 0


class Prog:
    ENGS = ["pe", "act", "dve", "pool", "sp"]

    def __init__(self, nc, es):
        self.nc = nc
        self.es = es
        self.ops = []
        self.esem = {}
        for e in ["pe", "act", "dve", "pool"]:
            self.esem[e] = es.enter_context(nc.semaphore("s_" + e))

    def _handle(self, eng):
        nc = self.nc
        return {"pe": nc.tensor, "act": nc.scalar, "dve": nc.vector,
                "pool": nc.gpsimd, "sp": nc.sync}[eng]

    def _add(self, op, reads, writes):
        deps = {}
        writes = list(writes)
        for r in reads:
            if r.excl:
                writes.append(r)
                continue
            if r.lw is not None:
                deps[id(r.lw)] = (r.lw, True)
        for w in writes:
            if w.lw is not None and id(w.lw) not in deps:
                deps[id(w.lw)] = (w.lw, w.excl)
            for rd in w.rd:
                if id(rd) not in deps:
                    deps[id(rd)] = (rd, False)
        for r in reads:
            if not r.excl:
                if not op.isdma:
                    r.rd = [x for x in r.rd if x.isdma or x.eng != op.eng]
                r.rd.append(op)
        for w in writes:
            w.lw = op
            w.rd = []
        op.deps = [d for d in deps.values() if d[0] is not op]
        self.ops.append(op)
        return op

    def op(self, eng, fn, reads=(), writes=()):
        return self._add(Op(eng, fn, False), reads, writes)

    def dma(self, eng, fn, semres, reads=(), writes=()):
        op = Op(eng, fn, True)
        if semres.sem is None:
            semres.sem = self.es.enter_context(self.nc.semaphore("d_" + semres.name))
        semres.semcnt += 16
        op.sem = semres.sem
        op.val = semres.semcnt
        return self._add(op, reads, writes)

    @staticmethod
    def _needs(d, raw, op):
        if d.isdma:
            return True
        if d.eng != op.eng:
            return True
        if d.eng == "pe":
            return False
        return raw

    def emit(self):
        for op in self.ops:
            for d, raw in op.deps:
                if self._needs(d, raw, op):
                    d.sig = True
        cnt = {e: 0 for e in self.ENGS}
        for op in self.ops:
            if not op.isdma and op.sig:
                cnt[op.eng] += 1
                op.sem = self.esem[op.eng]
                op.val = cnt[op.eng]
        nwait = 0
        for eng in self.ENGS:
            h = self._handle(eng)
            waited = {}
            for op in self.ops:
                if op.eng != eng:
                    continue
                need = {}
                for d, raw in op.deps:
                    if self._needs(d, raw, op):
                        k = id(d.sem)
                        if waited.get(k, 0) < d.val and need.get(k, (None, 0))[1] < d.val:
                            need[k] = (d.sem, d.val)
                for k, (sm, v) in need.items():
                    h.wait_ge(sm, v)
                    waited[k] = v
                    nwait += 1
                ins = op.fn(h)
                if op.isdma:
                    ins.then_inc(op.sem, 16)
                elif op.sig:
                    ins.then_inc(op.sem, 1)
        sp = self._handle("sp")
        seen = {}
        for op in self.ops:
            if op.isdma:
                k = id(op.sem)
                if seen.get(k, (None, 0))[1] < op.val:
                    seen[k] = (op.sem, op.val)
        for sm, v in seen.values():
            sp.wait_ge(sm, v)
        return dict(n_ops=len(self.ops), n_wait=nwait, sig=dict(cnt))


D = 2048
KC = 16
H = 12
HD = 128
MH = 4
NMEM = 256
DFF = 8192
QKV = 4608
TOK = 1536
IN_A = 6680
IN_B = 3584
SEQ = 4096
PAST = 2048
DSEQ = 32
NSEQ = 4
EPS = 1e-6
SCALE = float(HD ** -0.5)
NP = 512
NS = NSEQ * DSEQ
WCOLS = 256
NW = 3
SLAB = 1024
AR_H = 0
AR_TMP = 16 * 1024
AR_BIG = AR_TMP + 40 * 1024
AR_END = AR_BIG + 64 * 1024


def _prod(t):
    r = 1
    for x in t:
        r *= x
    return r


class V:
    def __init__(self, k, off, shape, dt):
        self.k = k
        self.off = off
        self.shape = tuple(shape)
        self.esz = 2 if dt == BF16 else 4
        n = _prod(shape)
        self.nbytes = n * self.esz
        assert off % 4 == 0 and self.nbytes % 4 == 0
        assert off + self.nbytes <= AR_END, (off, self.nbytes)
        ap = k.arena[:, off // 4:(off + self.nbytes) // 4]
        if dt == BF16:
            ap = ap.bitcast(BF16)
        if len(shape) == 2:
            ap = ap.rearrange("p (a b) -> p a b", b=shape[1])
        elif len(shape) == 3:
            ap = ap.rearrange("p (a b c) -> p a b c", b=shape[1], c=shape[2])
        self.ap = ap

    def r(self, i=None, j=None):
        if i is None:
            lo, hi = 0, self.nbytes
        else:
            st = _prod(self.shape[1:]) * self.esz
            lo = i * st
            hi = (i + 1 if j is None else j) * st
        return self.k.slabs(self.off + lo, self.off + hi)


class TileCfg:
    def __init__(self, kind, N, L, t=0):
        self.kind = kind
        self.N = N
        self.L = L
        self.nch = N // L
        self.t = t
        self.nsq = 1 if kind == 'p' else NSEQ


class K:
    def __init__(self, nc, es, ntp=8, do_sample=True, nlayers=4):
        self.nc, self.es = nc, es
        self.P = Prog(nc, es)
        self.ntp = ntp
        self.do_sample = do_sample
        self.nlayers = nlayers
        self._names = 0
        P = self.P
        din = lambda n, sh: nc.dram_tensor(n, sh, F32, kind="ExternalInput").ap()
        dout = lambda n, sh: nc.dram_tensor(n, sh, F32, kind="ExternalOutput").ap()
        self.i = dict(
            xp=din("xp", [SEQ, D]), xs=din("xs", [NS, D]),
            sg=din("sg", [2, NSEQ, H, HD, HD]), sc=din("sc", [2, NSEQ * 3, QKV]),
            ck=din("ck", [NSEQ, PAST, TOK]), cv=din("cv", [NSEQ, PAST, TOK]),
            clf=din("clf", [NSEQ, PAST, H]),
            cmk=din("cmk", [4, NSEQ, NMEM, 512]), cmv=din("cmv", [4, NSEQ, NMEM, 512]),
            mp=din("mp", [NMEM, D]),
            g_pre=din("g_pre", [64, 128]), g_post=din("g_post", [64, 128]),
            g_mpre=din("g_mpre", [64, 128]), g_mpost=din("g_mpost", [64, 128]),
            g_mem=din("g_mem", [64, 128]), g_kv=din("g_kv", [16, 128]),
            convw=din("convw", [288, 128]), a_log=din("a_log", [24]), dt_bias=din("dt_bias", [24]),
            gdn_norm=din("gdn_norm", [2, 128]), b_f=din("b_f", [12]),
            w_in_a=din("w_in_a", [2, D, IN_A]), w_in_b=din("w_in_b", [2, D, IN_B]),
            w_kvf=din("w_kvf", [D, 3084]), w_mem=din("w_mem", [4, D, 1024]),
            w_o=din("w_o", [4, D, D]), w_up=din("w_up", [4, D, DFF]), w_down=din("w_down", [4, DFF, D]),
            cst=din("cst", [128, 640]),
        )
        self.o = dict(
            yp=dout("yp", [SEQ, D]), ys=dout("ys", [NS, D]),
            pg=dout("pg", [2, H, HD, HD]), pc=dout("pc", [2 * 3, QKV]),
            pk=dout("pk", [SEQ, TOK]), pv=dout("pv", [SEQ, TOK]), plf=dout("plf", [SEQ, H]),
            pmk=dout("pmk", [4, NMEM, 512]), pmv=dout("pmv", [4, NMEM, 512]),
            sgo=dout("sgo", [2, NSEQ, H, HD, HD]), sco=dout("sco", [2 * NSEQ * 3, QKV]),
            sk=dout("sk", [NS, TOK]), sv=dout("sv", [NS, TOK]), slf=dout("slf", [NS, H]),
        )
        self.kscr = nc.dram_tensor("kscr", [128, H, SEQ], BF16).ap()
        self.vscr = nc.dram_tensor("vscr", [SEQ, TOK], BF16).ap()
        self.mkscr = nc.dram_tensor("mkscr", [4, 128, MH, NMEM], BF16).ap()
        self.mvscr = nc.dram_tensor("mvscr", [4, NMEM, 512], BF16).ap()
        self.r_kscr = [Res("kscr%d" % i) for i in range(SEQ // NP)]
        self.r_vscr = [Res("vscr%d" % i) for i in range(SEQ // NP)]
        self.r_mscr = [Res("mscr%d" % i) for i in range(4)]

        T = self.T
        self.arena = T("arena", [128, AR_END // 4])
        self.slab = [Res("slab%d" % i) for i in range(AR_END // SLAB)]
        self.xT = T("xT", [128, KC, NP])
        self.xr = [Res("xT%d" % i) for i in range(KC)]
        self.wring = [T("wr%d" % i, [128, KC, WCOLS], BF16) for i in range(NW)]
        self.wres = [Res("wr%d" % i) for i in range(NW)]
        self.wi = 0
        self.wsm = T("wsm", [128, KC, 24], BF16)
        self.r_wsm = Res("wsm")
        self.pb = [es.enter_context(nc.psum_tensor("pb%d" % i, [128, 512], F32)) for i in range(8)]
        self.rpb = [Res("pb%d" % i, True) for i in range(8)]
        self.pinned = set()
        self.bi = 0
        self.cst = T("cst_s", [128, 640])
        self.r_cst = Res("cst")
        self.cstb = T("cstb", [128, 384], BF16)
        self.r_cstb = Res("cstb")
        self.ident = self.cst[:, 0:128]
        self.ones = self.cst[:, 128:256]
        self.triu = self.cst[:, 256:384]
        self.triusn = self.cst[:, 384:512]
        self.elast = self.cst[:, 512:640]
        self.identb = self.cstb[:, 0:128]
        self.onesb = self.cstb[:, 128:256]
        self.triub = self.cstb[:, 256:384]
        self.gains = T("gains", [128, 6, 64])
        self.r_gains = Res("gains")
        self.convw = T("convw_s", [128, 288])
        self.gdng = T("gdng", [128, 2])
        self.vecs = T("vecs", [128, 60])
        self.nA = T("nA", [128, 24])
        self.r_misc = Res("misc")
        self.S = T("S", [128, 2, H, HD])
        self.r_S = [[Res("S%d_%d" % (l, h)) for h in range(H)] for l in range(2)]
        self.chist = T("chist", [128, 2, 36, 3])
        self.r_chist = [Res("chist%d" % l) for l in range(2)]
        self.Cp = T("Cp", [128, SEQ // 128, H])
        self.r_Cp = Res("Cp")
        self.Cs = T("Cs", [128, PAST // 128 + 1, NSEQ * H])
        self.r_Cs = Res("Cs")
        self.sS = [T("sS%d" % i, [128, HD]) for i in range(4)]
        self.r_sS = [Res("sS%d" % i) for i in range(4)]
        self.sSi = 0
        self.cendp = T("cendp", [128, H])
        self.r_cend = Res("cend")

    def T(self, name, shape, dt=F32):
        return self.es.enter_context(self.nc.sbuf_tensor(name, shape, dt))

    def slabs(self, lo, hi):
        return self.slab[lo // SLAB:(hi + SLAB - 1) // SLAB]

    def bank(self):
        for _ in range(8):
            b = self.bi
            self.bi = (self.bi + 1) % 8
            if b not in self.pinned:
                return b
        raise RuntimeError("no psum bank")

    def pin(self):
        b = self.bank()
        self.pinned.add(b)
        return b

    def unpin(self, b):
        self.pinned.discard(b)

    def tmp_reset(self, base=AR_TMP):
        self.tp = base

    def tmp(self, shape, dt=F32):
        v = V(self, self.tp, shape, dt)
        self.tp += (v.nbytes + SLAB - 1) // SLAB * SLAB
        assert self.tp <= AR_BIG, "tmp overflow %d" % self.tp
        return v

    def A(self, out, in_, func, rd, wr, bias=None, scale=None):
        kw = {}
        if bias is not None:
            kw["bias"] = bias
        if scale is not None:
            kw["scale"] = scale
        self.P.op("act", lambda e: e.activation(out=out, in_=in_, func=func, **kw), rd, wr)

    def TT(self, out, in0, in1, op, rd, wr):
        self.P.op("dve", lambda e: e.tensor_tensor(out=out, in0=in0, in1=in1, op=op), rd, wr)

    def TS(self, out, in0, s1, s2, op0, op1, rd, wr):
        if s2 is None:
            self.P.op("dve", lambda e: e.tensor_scalar(out=out, in0=in0, scalar1=s1, scalar2=None, op0=op0), rd, wr)
        else:
            self.P.op("dve", lambda e: e.tensor_scalar(out=out, in0=in0, scalar1=s1, scalar2=s2, op0=op0, op1=op1), rd, wr)

    def STT(self, out, in0, sc, in1, op0, op1, rd, wr):
        self.P.op("dve", lambda e: e.scalar_tensor_tensor(out=out, in0=in0, scalar=sc, in1=in1, op0=op0, op1=op1), rd, wr)

    def CP(self, out, in_, rd, wr):
        self.P.op("dve", lambda e: e.tensor_copy(out=out, in_=in_), rd, wr)

    def REC(self, out, in_, rd, wr):
        self.P.op("dve", lambda e: e.reciprocal(out=out, in_=in_), rd, wr)

    def MS(self, out, val, wr):
        self.P.op("dve", lambda e: e.memset(out, val), (), wr)

    def MM(self, out, lhsT, rhs, st, sp, rd, wr, skip=False):
        if skip:
            self.P.op("pe", lambda e: e.matmul(out, lhsT, rhs, start=st, stop=sp, skip_group_check=True), rd, wr)
        else:
            self.P.op("pe", lambda e: e.matmul(out, lhsT, rhs, start=st, stop=sp), rd, wr)

    def TR(self, out, in_, ident, rd, wr):
        self.P.op("pe", lambda e: e.transpose(out=out, in_=in_, identity=ident), rd, wr)

    def DMA(self, out, in_, semres, rd, wr, eng="sp"):
        self.P.dma(eng, lambda e: e.dma_start(out=out, in_=in_), semres, rd, wr)

    def pbf(self, b):
        return self.pb[b][:].bitcast(BF16)

    def wload(self, src, ncols):
        sl = self.wi
        self.wi = (self.wi + 1) % NW
        dst = self.wring[sl][:, :, 0:ncols]
        self.DMA(dst, src.rearrange("(kc p) c -> p kc c", p=128), self.wres[sl], [], [self.wres[sl]], eng="pool")
        return sl

    def linear_fm(self, W, c0, ncols, act, N, epi, Kdim=D):
        nkb = Kdim // D
        for cb in range(0, ncols, WCOLS):
            nc_ = min(WCOLS, ncols - cb)
            chunks = [(m0, min(128, nc_ - m0)) for m0 in range(0, nc_, 128)]
            banks = [self.pin() for _ in chunks] if nkb > 1 else None
            for kb in range(nkb):
                sl = self.wload(W[kb * D:(kb + 1) * D, c0 + cb:c0 + cb + nc_], nc_)
                for ci, (m0, rows) in enumerate(chunks):
                    b = banks[ci] if banks else self.bank()
                    for kc in range(KC):
                        kk = kb * KC + kc
                        self.MM(self.pb[b][0:rows, 0:N], self.wring[sl][:, kc, m0:m0 + rows], act.ap[:, kk, 0:N],
                                kk == 0, kk == nkb * KC - 1, [self.wres[sl]] + act.r(kk), [self.rpb[b]])
                    if kb == nkb - 1:
                        epi((cb + m0) // 128, rows, self.pb[b][0:rows, 0:N], b)
            if banks:
                for b in banks:
                    self.unpin(b)

    def linear_tm(self, W, c0, ncols, act, toks, epi):
        for cb in range(0, ncols, WCOLS):
            nc_ = min(WCOLS, ncols - cb)
            sl = self.wload(W[:, c0 + cb:c0 + cb + nc_], nc_)
            for ti, (t0, M) in enumerate(toks):
                b = self.bank()
                for kc in range(KC):
                    self.MM(self.pb[b][0:M, 0:nc_], act.ap[:, kc, t0:t0 + M], self.wring[sl][:, kc, 0:nc_],
                            kc == 0, kc == KC - 1, [self.wres[sl]] + act.r(kc), [self.rpb[b]])
                epi(ti, cb, nc_, self.pb[b][0:M, 0:nc_], b)

    def stats(self, srcs, N, Dn, rstd, base_rows=128):
        sq = [self.tmp([N]) for _ in range(2)]
        b = self.pin()
        n = len(srcs)
        for i, (ap, rl) in enumerate(srcs):
            q = sq[i % 2]
            self.A(q.ap[:, 0:N], ap, AF.Square, rl, q.r())
            self.MM(self.pb[b][:, 0:N], self.ones, q.ap[:, 0:N], i == 0, i == n - 1, [self.r_cst] + q.r(), [self.rpb[b]])
        self.rsqrt(rstd.ap[:, 0:N], self.pb[b][:, 0:N], [self.rpb[b]], rstd.r(), 1.0 / Dn)
        self.unpin(b)

    def rsqrt(self, out, in_, rd, wr, scale):
        self.A(out, in_, AF.Sqrt, rd, wr, bias=EPS, scale=scale)
        self.REC(out, out, wr, wr)

    def prenorm(self, gi, gl, dst, N):
        self.tmp_reset()
        rstd = self.tmp([N])
        self.stats([(self.xT[:, kc, 0:N], [self.xr[kc]]) for kc in range(KC)], N, D, rstd)
        for kc in range(KC):
            self.STT(dst.ap[:, kc, 0:N], self.xT[:, kc, 0:N], self.gains[:, gi, gl * 16 + kc:gl * 16 + kc + 1],
                     rstd.ap[:, 0:N], ALU.mult, ALU.mult, [self.xr[kc], self.r_gains] + rstd.r(), dst.r(kc))

    def postnorm_res(self, y, gi, gl, N, tmpbase):
        self.tmp_reset(tmpbase)
        rstd = self.tmp([N])
        self.stats([(y.ap[:, kc, 0:N], y.r(kc)) for kc in range(KC)], N, D, rstd)
        for kc in range(KC):
            self.STT(y.ap[:, kc, 0:N], y.ap[:, kc, 0:N], self.gains[:, gi, gl * 16 + kc:gl * 16 + kc + 1],
                     rstd.ap[:, 0:N], ALU.mult, ALU.mult, y.r(kc) + [self.r_gains] + rstd.r(), y.r(kc))
            self.TT(self.xT[:, kc, 0:N], self.xT[:, kc, 0:N], y.ap[:, kc, 0:N], ALU.add,
                    [self.xr[kc]] + y.r(kc), [self.xr[kc]])

    def load_T(self, src, R, dst, wr):
        self.tmp_reset()
        stg = self.tmp([128])
        self.DMA(stg.ap[0:R, :], src, stg.r()[0], [], stg.r())
        b = self.bank()
        self.TR(self.pb[b][:, 0:R], stg.ap[0:R, :], self.ident[0:R, 0:R], stg.r() + [self.r_cst], [self.rpb[b]])
        self.A(dst, self.pb[b][:, 0:R], AF.Copy, [self.rpb[b]], wr)

    def prologue(self):
        i = self.i
        self.DMA(self.cst[:], i["cst"], self.r_cst, [], [self.r_cst])
        self.A(self.cstb[:], self.cst[:, 0:384], AF.Copy, [self.r_cst], [self.r_cstb])
        for gi, nm in enumerate(["g_pre", "g_post", "g_mpre", "g_mpost", "g_mem"]):
            self.load_T(i[nm], 64, self.gains[:, gi, :], [self.r_gains])
        self.load_T(i["g_kv"], 16, self.gains[:, 5, 0:16], [self.r_gains])
        for j in range(3):
            self.load_T(i["convw"][j * 96:(j + 1) * 96, :], 96, self.convw[:, j * 96:(j + 1) * 96], [self.r_misc])
        self.load_T(i["gdn_norm"], 2, self.gdng[:, :], [self.r_misc])
        self.DMA(self.vecs[:, 0:24], i["a_log"].partition_broadcast(128), self.r_misc, [], [self.r_misc])
        self.DMA(self.vecs[:, 24:48], i["dt_bias"].partition_broadcast(128), self.r_misc, [], [self.r_misc])
        self.DMA(self.vecs[:, 48:60], i["b_f"].partition_broadcast(128), self.r_misc, [], [self.r_misc])
        self.A(self.nA[:], self.vecs[:, 0:24], AF.Exp, [self.r_misc], [self.r_misc])
        self.TS(self.nA[:], self.nA[:], -1.0, None, ALU.mult, None, [self.r_misc], [self.r_misc])
        for l in range(2):
            for h in range(H):
                self.MS(self.S[:, l, h, :], 0.0, [self.r_S[l][h]])
            self.MS(self.chist[:, l, :, :], 0.0, [self.r_chist[l]])

    def memory_kv(self):
        N = NMEM
        mT = V(self, AR_BIG, [KC, N], F32)
        hm = V(self, AR_H, [KC, N], BF16)
        stg = [V(self, AR_BIG + 16384 + j * 8192, [D], F32) for j in range(2)]
        ost = [V(self, AR_BIG + 32768 + j * 2048, [512], F32) for j in range(4)]
        osb = [V(self, AR_BIG + 40960 + j * 1024, [512], BF16) for j in range(4)]
        kst = V(self, AR_BIG + 45056, [MH, N], BF16)
        for tb in range(2):
            self.DMA(stg[tb].ap[:, :], self.i["mp"][tb * 128:(tb + 1) * 128, :], stg[tb].r()[0], [], stg[tb].r())
            for g in range(4):
                b = self.bank()
                for j in range(4):
                    kc = g * 4 + j
                    self.TR(self.pb[b][:, j * 128:(j + 1) * 128], stg[tb].ap[:, kc * 128:(kc + 1) * 128], self.ident,
                            stg[tb].r() + [self.r_cst], [self.rpb[b]])
                self.A(mT.ap[:, g * 4:(g + 1) * 4, tb * 128:(tb + 1) * 128],
                       self.pb[b][:, :].rearrange("p (j t) -> p j t", t=128), AF.Copy, [self.rpb[b]], mT.r(g * 4, g * 4 + 4))
        self.tmp_reset()
        rstd = self.tmp([N])
        self.stats([(mT.ap[:, kc, :], mT.r(kc)) for kc in range(KC)], N, D, rstd)
        oc = [0]
        for l in range(4):
            for kc in range(KC):
                self.STT(hm.ap[:, kc, :], mT.ap[:, kc, :], self.gains[:, 4, l * 16 + kc:l * 16 + kc + 1], rstd.ap[:, :],
                         ALU.mult, ALU.mult, mT.r(kc) + [self.r_gains] + rstd.r(), hm.r(kc))
            W = self.i["w_mem"][l]

            def epi_k(m, rows, ps, b, l=l):
                self.A(kst.ap[:, m, :], ps, AF.Copy, [self.rpb[b]], kst.r(m))
            self.linear_fm(W, 0, 512, hm, N, epi_k)
            self.DMA(self.mkscr[l], kst.ap[:, :, :], kst.r()[0], kst.r(), [self.r_mscr[l]])

            def epi_tm(ti, cb, ncb, ps, b, l=l, which=0):
                j = oc[0] % 4
                oc[0] += 1
                dst = self.o["pmk" if which == 0 else "pmv"]
                self.A(ost[j].ap[:, 0:ncb], ps, AF.Copy, [self.rpb[b]], ost[j].r())
                self.DMA(dst[l, ti * 128:(ti + 1) * 128, cb:cb + ncb], ost[j].ap[:, 0:ncb], ost[j].r()[0], ost[j].r(), [])
                if which == 1:
                    self.CP(osb[j].ap[:, 0:ncb], ost[j].ap[:, 0:ncb], ost[j].r(), osb[j].r())
                    self.DMA(self.mvscr[l, ti * 128:(ti + 1) * 128, cb:cb + ncb], osb[j].ap[:, 0:ncb], osb[j].r()[0],
                             osb[j].r(), [self.r_mscr[l]])
            self.linear_tm(W, 0, 512, hm, [(0, 128), (128, 128)], lambda *a, l=l: epi_tm(*a, l=l, which=0))
            self.linear_tm(W, 512, 512, hm, [(0, 128), (128, 128)], lambda *a, l=l: epi_tm(*a, l=l, which=1))

    def load_xT(self, src, N):
        stg = [V(self, AR_BIG + j * 8192, [D], F32) for j in range(2)]
        for tb in range(N // 128):
            st = stg[tb % 2]
            self.DMA(st.ap[:, :], src[tb * 128:(tb + 1) * 128, :], st.r()[0], [], st.r())
            for g in range(4):
                b = self.bank()
                for j in range(4):
                    kc = g * 4 + j
                    self.TR(self.pb[b][:, j * 128:(j + 1) * 128], st.ap[:, kc * 128:(kc + 1) * 128], self.ident,
                            st.r() + [self.r_cst], [self.rpb[b]])
                self.A(self.xT[:, g * 4:(g + 1) * 4, tb * 128:(tb + 1) * 128],
                       self.pb[b][:, :].rearrange("p (j t) -> p j t", t=128), AF.Copy,
                       [self.rpb[b]], self.xr[g * 4:(g + 1) * 4])

    def store_y(self, dst, N):
        stg = [V(self, AR_BIG + j * 8192, [D], F32) for j in range(2)]
        for tb in range(N // 128):
            st = stg[tb % 2]
            for g in range(4):
                b = self.bank()
                for j in range(4):
                    kc = g * 4 + j
                    self.TR(self.pb[b][:, j * 128:(j + 1) * 128], self.xT[:, kc, tb * 128:(tb + 1) * 128], self.ident,
                            [self.xr[kc], self.r_cst], [self.rpb[b]])
                self.A(st.ap[:, g * 512:(g + 1) * 512], self.pb[b][:, :], AF.Copy, [self.rpb[b]], st.r())
            self.DMA(dst[tb * 128:(tb + 1) * 128, :], st.ap[:, :], st.r()[0], st.r(), [])

    def mlp(self, l, N):
        hf = V(self, AR_H, [KC, N], BF16)
        self.prenorm(2, l, hf, N)
        hid = V(self, AR_BIG, [64, N], BF16)
        self.tmp_reset(AR_TMP + 16384)
        rl = [self.tmp([N]) for _ in range(3)]
        cnt = [0]

        def epi_up(m, rows, ps, b):
            r = rl[cnt[0] % 3]
            cnt[0] += 1
            self.A(r.ap[:, 0:N], ps, AF.Relu, [self.rpb[b]], r.r())
            self.TT(hid.ap[:, m, 0:N], r.ap[:, 0:N], r.ap[:, 0:N], ALU.mult, r.r(), hid.r(m))
        self.linear_fm(self.i["w_up"][l], 0, DFF, hf, N, epi_up)
        y = V(self, AR_H, [KC, N], F32)

        def epi_dn(m, rows, ps, b):
            self.A(y.ap[:, m, 0:N], ps, AF.Copy, [self.rpb[b]], y.r(m))
        self.linear_fm(self.i["w_down"][l], 0, D, hid, N, epi_dn, Kdim=DFF)
        self.postnorm_res(y, 3, l, N, AR_H + KC * N * 4 if KC * N * 4 > 16384 else AR_TMP)

    def mem_attend(self, l, tc, mq, cat):
        N = tc.N
        kT = self.tmp([MH, NMEM], BF16)
        vv = self.tmp([2, 512], BF16)
        pT = [self.tmp([N], BF16) for _ in range(2)]
        rden = self.tmp([N])
        nq = N // tc.nsq
        for sq in range(tc.nsq):
            q0 = sq * nq
            if tc.kind == 'p':
                self.DMA(kT.ap[:, :, :], self.mkscr[l], kT.r()[0], [self.r_mscr[l]], kT.r())
                self.DMA(vv.ap[:, :, :], self.mvscr[l].rearrange("(b p) c -> p b c", p=128), vv.r()[0], [self.r_mscr[l]], vv.r())
            else:
                stg = self.tmp_s
                for nb in range(2):
                    self.DMA(stg.ap[:, 0:512], self.i["cmk"][l, sq, nb * 128:(nb + 1) * 128, :], stg.r()[0], [], stg.r())
                    b = self.bank()
                    for hm in range(MH):
                        self.TR(self.pb[b][:, hm * 128:(hm + 1) * 128], stg.ap[:, hm * 128:(hm + 1) * 128], self.ident,
                                stg.r() + [self.r_cst], [self.rpb[b]])
                    self.A(kT.ap[:, :, nb * 128:(nb + 1) * 128], self.pb[b][:, :].rearrange("p (h n) -> p h n", n=128),
                           AF.Copy, [self.rpb[b]], kT.r())
                    self.DMA(stg.ap[:, 0:512], self.i["cmv"][l, sq, nb * 128:(nb + 1) * 128, :], stg.r()[0], [], stg.r())
                    self.A(vv.ap[:, nb, :], stg.ap[:, 0:512], AF.Copy, stg.r(), vv.r())
            for hm in range(MH):
                bo, bd = self.pin(), self.pin()
                for nb in range(2):
                    b = self.bank()
                    self.MM(self.pb[b][:, 0:nq], kT.ap[:, hm, nb * 128:(nb + 1) * 128], mq.ap[:, hm, q0:q0 + nq], True, True,
                            kT.r() + mq.r(hm), [self.rpb[b]])
                    p = pT[nb]
                    self.A(p.ap[:, 0:nq], self.pb[b][:, 0:nq], AF.Exp, [self.rpb[b]], p.r(), scale=SCALE)
                    self.MM(self.pb[bo][:, 0:nq], vv.ap[:, nb, hm * 128:(hm + 1) * 128], p.ap[:, 0:nq], nb == 0, nb == 1,
                            vv.r() + p.r(), [self.rpb[bo]])
                    self.MM(self.pb[bd][:, 0:nq], self.onesb, p.ap[:, 0:nq], nb == 0, nb == 1,
                            [self.r_cstb] + p.r(), [self.rpb[bd]])
                self.REC(rden.ap[:, 0:nq], self.pb[bd][:, 0:nq], [self.rpb[bd]], rden.r())
                self.TT(cat.ap[:, H + hm, q0:q0 + nq], self.pb[bo][:, 0:nq], rden.ap[:, 0:nq], ALU.mult,
                        [self.rpb[bo]] + rden.r(), cat.r(H + hm))
                self.unpin(bo)
                self.unpin(bd)

    def out_proj(self, l, tc, cat):
        N = tc.N
        y = V(self, AR_BIG, [KC, N], F32)

        def epi(m, rows, ps, b):
            self.A(y.ap[:, m, 0:N], ps, AF.Copy, [self.rpb[b]], y.r(m))
        self.linear_fm(self.i["w_o"][l], 0, D, cat, N, epi)
        self.postnorm_res(y, 1, l, N, AR_TMP)

    def gdn_layer(self, l, tc):
        N, L, nch = tc.N, tc.L, tc.nch
        nsq = tc.nsq
        hT = V(self, AR_H, [KC, N], BF16)
        self.prenorm(0, l, hT, N)
        qn = V(self, AR_BIG, [H, N], BF16)
        kn = V(self, AR_BIG + 12288, [H, N], BF16)
        vT = V(self, AR_BIG + 24576, [H, N], BF16)
        zT = V(self, AR_BIG + 36864, [H, N], BF16)
        mq = V(self, AR_BIG + 49152, [MH, N], BF16)
        sm = AR_BIG + 53248
        ab = V(self, sm, [nch, 24], F32)
        gam = V(self, sm + 1024, [nch, H], F32)
        bet = V(self, sm + 1536, [nch, H], F32)
        Gc = V(self, sm + 2048, [nch, H], F32)
        eG = V(self, sm + 2560, [nch, H], F32)
        eGm = V(self, sm + 3072, [nch, H], F32)
        eGL = V(self, sm + 3584, [nch, H], F32)
        t1 = V(self, sm + 4096, [nch, H], F32)
        t2 = V(self, sm + 4608, [nch, H], F32)
        W = self.i["w_in_a"][l]
        seqw = N // nsq
        self.tmp_reset()
        cbs = [self.tmp([nsq, seqw + 3]) for _ in range(3)]
        accs = [self.tmp([N]) for _ in range(3)]
        qks = [self.tmp([N]) for _ in range(3)]
        rss = [self.tmp([N]) for _ in range(3)]
        sqvs = [self.tmp([N]) for _ in range(3)]
        hist = self.chist[:, l, :, :] if tc.kind == 'p' else None
        if tc.kind == 's':
            hs = V(self, sm + 5120, [36, nsq * 3], F32)
            st = self.tmp([QKV])
            R = nsq * 3
            self.DMA(st.ap[0:R, :], self.i["sc"][l], st.r()[0], [], st.r())
            for g in range(9):
                b = self.bank()
                for j in range(4):
                    jj = g * 4 + j
                    self.TR(self.pb[b][:, j * R:(j + 1) * R], st.ap[0:R, jj * 128:(jj + 1) * 128], self.ident[0:R, 0:R],
                            st.r() + [self.r_cst], [self.rpb[b]])
                self.A(hs.ap[:, g * 4:(g + 1) * 4, :], self.pb[b][:, 0:4 * R].rearrange("p (j r) -> p j r", r=R), AF.Copy,
                       [self.rpb[b]], hs.r())
        newh = V(self, sm + 7168, [36, nsq * 3], F32) if tc.kind == 's' else None
        cw = self.convw
        ci = [0]

        pipe = []

        def pstep():
            for st in list(pipe):
                st.pop(0)()
            pipe[:] = [st for st in pipe if st]

        def epi_qkv(m, rows, ps, b):
            i = ci[0]
            ci[0] += 1
            cb = cbs[i % 3]
            acc, qk, rs, sqv = accs[i % 3], qks[i % 3], rss[i % 3], sqvs[i % 3]
            kind, h = m // H, m % H
            a3 = acc.ap[:, 0:N].rearrange("p (s t) -> p s t", t=seqw)
            wcol = lambda tap: cw[:, l * 144 + tap * 36 + m:l * 144 + tap * 36 + m + 1]
            hold = {}

            def s1():
                if tc.kind == 'p':
                    self.CP(cb.ap[:, 0, 0:3], self.chist[:, l, m, :], [self.r_chist[l]], cb.r())
                else:
                    self.CP(cb.ap[:, :, 0:3], hs.ap[:, m, :].rearrange("p (s t) -> p s t", t=3), hs.r(), cb.r())
                self.A(cb.ap[:, :, 3:3 + seqw], ps.rearrange("p (s t) -> p s t", t=seqw), AF.Copy, [self.rpb[b]], cb.r())

            def s2():
                if tc.kind == 'p':
                    self.CP(self.chist[:, l, m, :], cb.ap[:, 0, seqw:seqw + 3], cb.r(), [self.r_chist[l]])
                else:
                    self.CP(newh.ap[:, m, :].rearrange("p (s t) -> p s t", t=3), cb.ap[:, :, seqw:seqw + 3], cb.r(), newh.r())
                self.TS(a3, cb.ap[:, :, 3:3 + seqw], wcol(3), None, ALU.mult, None, cb.r() + [self.r_misc], acc.r())
                for tap in range(3):
                    self.STT(a3, cb.ap[:, :, tap:tap + seqw], wcol(tap), a3, ALU.mult, ALU.add,
                             cb.r() + [self.r_misc] + acc.r(), acc.r())

            def s3():
                if kind == 2:
                    self.A(vT.ap[:, h, 0:N], acc.ap[:, 0:N], AF.Silu, acc.r(), vT.r(h))
                    return
                self.A(qk.ap[:, 0:N], acc.ap[:, 0:N], AF.Silu, acc.r(), qk.r())
                self.A(sqv.ap[:, 0:N], qk.ap[:, 0:N], AF.Square, qk.r(), sqv.r())
                b2 = self.bank()
                hold["b2"] = b2
                self.MM(self.pb[b2][:, 0:N], self.ones, sqv.ap[:, 0:N], True, True, [self.r_cst] + sqv.r(), [self.rpb[b2]])

            def s4():
                b2 = hold["b2"]
                self.A(rs.ap[:, 0:N], self.pb[b2][:, 0:N], AF.Sqrt, [self.rpb[b2]], rs.r(), bias=EPS, scale=1.0)

            def s5():
                self.REC(rs.ap[:, 0:N], rs.ap[:, 0:N], rs.r(), rs.r())
                dst = qn if kind == 0 else kn
                self.STT(dst.ap[:, h, 0:N], qk.ap[:, 0:N], SCALE if kind == 0 else 1.0, rs.ap[:, 0:N], ALU.mult, ALU.mult,
                         qk.r() + rs.r(), dst.r(h))
            pipe.append([s1, s2, s3] if kind == 2 else [s1, s2, s3, s4, s5])
            pstep()
        self.linear_fm(W, 0, QKV, hT, N, epi_qkv)
        while pipe:
            pstep()

        def epi_z(m, rows, ps, b):
            self.A(zT.ap[:, m, 0:N], ps, AF.Silu, [self.rpb[b]], zT.r(m))
        self.linear_fm(W, QKV, TOK, hT, N, epi_z)

        def epi_mq(m, rows, ps, b):
            self.A(mq.ap[:, m, 0:N], ps, AF.Copy, [self.rpb[b]], mq.r(m))
        self.linear_fm(W, QKV + TOK + 24, 512, hT, N, epi_mq)
        o1 = QKV + TOK
        self.DMA(self.wsm[:, :, :], W[:, o1:o1 + 24].rearrange("(kc p) c -> p kc c", p=128), self.r_wsm, [], [self.r_wsm], eng="pool")
        bab = self.bank()
        for c in range(nch):
            for kc in range(KC):
                self.MM(self.pb[bab][0:L, c * 24:(c + 1) * 24], hT.ap[:, kc, c * L:(c + 1) * L], self.wsm[:, kc, :],
                        kc == 0, kc == KC - 1, hT.r(kc) + [self.r_wsm], [self.rpb[bab]])
        self.A(ab.ap[0:L, :, :], self.pb[bab][0:L, 0:nch * 24].rearrange("p (c f) -> p c f", f=24), AF.Copy, [self.rpb[bab]], ab.r())
        self.conv_state_out(l, tc, newh)
        bc = lambda ap: ap.unsqueeze(1).to_broadcast([L, nch, H])
        av, bv = ab.ap[0:L, :, 0:H], ab.ap[0:L, :, H:2 * H]
        x_, ax, g_, be = t1.ap[0:L], t2.ap[0:L], gam.ap[0:L], bet.ap[0:L]
        self.TT(x_, av, bc(self.vecs[0:L, 24 + l * H:24 + (l + 1) * H]), ALU.add, ab.r() + [self.r_misc], t1.r())
        self.STT(ax, x_, -1.0, x_, ALU.mult, ALU.max, t1.r(), t2.r())
        self.A(ax, ax, AF.Exp, t2.r(), t2.r(), scale=-1.0)
        self.A(ax, ax, AF.Ln, t2.r(), t2.r(), bias=1.0)
        self.STT(x_, x_, 0.0, ax, ALU.max, ALU.add, t1.r() + t2.r(), t1.r())
        self.TT(g_, x_, bc(self.nA[0:L, l * H:(l + 1) * H]), ALU.mult, t1.r() + [self.r_misc], gam.r())
        self.A(be, bv, AF.Sigmoid, ab.r(), bet.r())
        g2 = gam.ap[0:L].rearrange("p c h -> p (c h)")
        b1 = self.bank()
        self.MM(self.pb[b1][0:L, 0:nch * H], self.triu[0:L, 0:L], g2, True, True, [self.r_cst] + gam.r(), [self.rpb[b1]])
        self.A(Gc.ap[0:L].rearrange("p c h -> p (c h)"), self.pb[b1][0:L, 0:nch * H], AF.Copy, [self.rpb[b1]], Gc.r())
        self.A(eG.ap[0:L].rearrange("p c h -> p (c h)"), self.pb[b1][0:L, 0:nch * H], AF.Exp, [self.rpb[b1]], eG.r())
        b2 = self.bank()
        self.MM(self.pb[b2][:, 0:nch * H], self.ones[0:L, :], g2, True, True, [self.r_cst] + gam.r(), [self.rpb[b2]])
        self.A(eGL.ap.rearrange("p c h -> p (c h)"), self.pb[b2][:, 0:nch * H], AF.Exp, [self.rpb[b2]], eGL.r())
        self.TT(eGm.ap[0:L].rearrange("p c h -> p (c h)"), self.pb[b2][0:L, 0:nch * H], Gc.ap[0:L].rearrange("p c h -> p (c h)"),
                ALU.subtract, [self.rpb[b2]] + Gc.r(), eGm.r())
        self.A(eGm.ap[0:L], eGm.ap[0:L], AF.Exp, eGm.r(), eGm.r())
        cat = V(self, AR_H, [KC, N], BF16)
        for h0 in range(0, H, 2):
            gens = [self.gdn_head(l, tc, h0 + j, j, qn, kn, vT, zT, gam, bet, Gc, eG, eGm, eGL, cat) for j in range(2)]
            while gens:
                for g in list(gens):
                    try:
                        next(g)
                    except StopIteration:
                        gens.remove(g)
        self.tmp_reset()
        self.tmp_s = self.tmp([512])
        self.mem_attend(l, tc, mq, cat)
        self.out_proj(l, tc, cat)

    def conv_state_out(self, l, tc, newh):
        if tc.kind == 'p':
            if tc.t != self.ntp - 1:
                return
            src = lambda j0, j1: self.chist[:, l, j0:j1, :]
            rr = [self.r_chist[l]]
            R, dst = 3, self.o["pc"][l * 3:(l + 1) * 3, :]
        else:
            src = lambda j0, j1: newh.ap[:, j0:j1, :]
            rr = newh.r()
            R, dst = 3 * NSEQ, self.o["sco"][l * 3 * NSEQ:(l + 1) * 3 * NSEQ, :]
        self.tmp_reset(AR_TMP + 16384)
        st = self.tmp([QKV])
        for j in range(36):
            b = self.bank()
            inp = self.chist[:, l, j, :] if tc.kind == 'p' else newh.ap[:, j, :]
            self.TR(self.pb[b][0:R, 0:128], inp, self.ident, rr + [self.r_cst], [self.rpb[b]])
            self.A(st.ap[0:R, j * 128:(j + 1) * 128], self.pb[b][0:R, 0:128], AF.Copy, [self.rpb[b]], st.r())
        self.DMA(dst, st.ap[0:R, :], st.r()[0], st.r(), [])

    def gdn_head(self, l, tc, h, slot, qn, kn, vT, zT, gam, bet, Gc, eG, eGm, eGL, cat):
        N, L, nch = tc.N, tc.L, tc.nch
        base = AR_TMP + slot * 20480
        rA, rB, rC, rD, rE = base, base + 6144, base + 8192, base + 10240, base + 16384
        gU = V(self, rA, [nch, L], F32)
        EG = V(self, rA + 2048, [N], F32)
        dec = V(self, rA + 4096, [nch, L], F32)
        Vtok = V(self, rA, [nch, HD], BF16)
        Kg = V(self, rA + 2048, [nch, HD], BF16)
        Kp = V(self, rA + 4096, [nch, HD], BF16)
        nNb = V(self, rB, [nch, L], F32)
        nyw = V(self, rB, [N], BF16)
        Wt = V(self, rC, [nch, L], F32)
        Xt = V(self, rD, [nch, L], F32)
        Wb = [[V(self, rD + 2048 + 1024 * (2 * i + j), [nch, L], BF16) for j in range(2)] for i in range(2)]
        o2 = V(self, rD, [N], F32)
        rstd = V(self, rD + 2048, [N], F32)
        ot = V(self, rD + 4096, [N], F32)
        QgT = V(self, rE, [N], BF16)
        Xtb = V(self, rE + 1024, [nch, L], BF16)
        attnT = V(self, rE + 2048, [nch, L], BF16)
        vnew = [V(self, rE + 3072 + 256 * i, [HD], BF16) for i in range(2)]
        Sb = [V(self, rE + 3584 + 256 * i, [HD], BF16) for i in range(2)]
        cstr = [self.r_cst]
        bcl = lambda v: v.ap[0:L, :, h:h + 1].to_broadcast([L, nch, L])
        bcd = lambda v: v.ap[0:L, :, h:h + 1].to_broadcast([L, nch, HD])
        mask_b = lambda m: m[0:L, 0:L].unsqueeze(1).to_broadcast([L, nch, L])
        p3 = lambda b: self.pb[b][0:L, 0:N].rearrange("p (c i) -> p c i", i=L)
        self.TT(gU.ap[0:L], bcl(gam), mask_b(self.triu), ALU.mult, gam.r() + cstr, gU.r())
        yield
        bg = self.bank()
        self.MM(self.pb[bg][:, 0:N], self.ones[0:L, :], gU.ap[0:L].rearrange("p c i -> p (c i)"), True, True, cstr + gU.r(), [self.rpb[bg]])
        yield
        self.A(EG.ap[:, 0:N], self.pb[bg][:, 0:N], AF.Exp, [self.rpb[bg]], EG.r())
        self.TT(dec.ap[0:L], p3(bg), bcl(Gc), ALU.subtract, [self.rpb[bg]] + Gc.r(), dec.r())
        yield
        self.TT(QgT.ap[:, 0:N], qn.ap[:, h, 0:N], EG.ap[:, 0:N], ALU.mult, qn.r(h) + EG.r(), QgT.r())
        self.TS(dec.ap[0:L], dec.ap[0:L], 0.0, None, ALU.min, None, dec.r(), dec.r())
        yield
        self.A(dec.ap[0:L], dec.ap[0:L], AF.Exp, dec.r(), dec.r())
        bk, bq = self.bank(), self.bank()
        for c in range(nch):
            cs = slice(c * L, (c + 1) * L)
            self.MM(self.pb[bk][0:L, cs], kn.ap[:, h, cs], kn.ap[:, h, cs], True, True, kn.r(h), [self.rpb[bk]])
        for c in range(nch):
            cs = slice(c * L, (c + 1) * L)
            self.MM(self.pb[bq][0:L, cs], kn.ap[:, h, cs], qn.ap[:, h, cs], True, True, kn.r(h) + qn.r(h), [self.rpb[bq]])
        yield
        self.TT(nNb.ap[0:L], dec.ap[0:L], mask_b(self.triusn), ALU.mult, dec.r() + cstr, nNb.r())
        self.TT(nNb.ap[0:L], nNb.ap[0:L], bcl(bet), ALU.mult, nNb.r() + bet.r(), nNb.r())
        self.TT(dec.ap[0:L], dec.ap[0:L], mask_b(self.triu), ALU.mult, dec.r() + cstr, dec.r())
        yield
        self.TT(Wt.ap[0:L], p3(bk), nNb.ap[0:L], ALU.mult, [self.rpb[bk]] + nNb.r(), Wt.r())
        self.TT(attnT.ap[0:L], p3(bq), dec.ap[0:L], ALU.mult, [self.rpb[bq]] + dec.r(), attnT.r())
        yield
        cur = Wb[0]
        self.A(cur[0].ap[0:L], Wt.ap[0:L], AF.Copy, Wt.r(), cur[0].r())
        self.TT(Xt.ap[0:L], Wt.ap[0:L], mask_b(self.ident), ALU.add, Wt.r() + cstr, Xt.r())
        yield
        bt = self.bank()
        ptb = self.pbf(bt)
        for c in range(nch):
            self.TR(ptb[0:L, c * L:(c + 1) * L], cur[0].ap[0:L, c, :], self.identb[0:L, 0:L], cur[0].r() + [self.r_cstb], [self.rpb[bt]])
        self.A(Xtb.ap[0:L], Xt.ap[0:L], AF.Copy, Xt.r(), Xtb.r())
        yield
        self.A(cur[1].ap[0:L], ptb[0:L, 0:N].rearrange("p (c i) -> p c i", i=L), AF.Copy, [self.rpb[bt]], cur[1].r())
        bv_ = self.bank()
        pv = self.pbf(bv_)
        for c in range(nch):
            self.TR(pv[0:L, c * HD:(c + 1) * HD], vT.ap[:, h, c * L:(c + 1) * L], self.identb, vT.r(h) + [self.r_cstb], [self.rpb[bv_]])
        bk_ = self.bank()
        pk = self.pbf(bk_)
        for c in range(nch):
            self.TR(pk[0:L, c * HD:(c + 1) * HD], kn.ap[:, h, c * L:(c + 1) * L], self.identb, kn.r(h) + [self.r_cstb], [self.rpb[bk_]])
        yield
        self.A(Vtok.ap[0:L], pv[0:L, 0:nch * HD].rearrange("p (c d) -> p c d", d=HD), AF.Copy, [self.rpb[bv_]], Vtok.r())
        pk3 = pk[0:L, 0:nch * HD].rearrange("p (c d) -> p c d", d=HD)
        self.TT(Kg.ap[0:L], pk3, bcd(eG), ALU.mult, [self.rpb[bk_]] + eG.r(), Kg.r())
        self.TT(Kp.ap[0:L], pk3, bcd(eGm), ALU.mult, [self.rpb[bk_]] + eGm.r(), Kp.r())
        yield
        nsteps = {64: 5, 32: 4}[L]
        for k in range(1, nsteps + 1):
            nxt = Wb[k % 2]
            last = k == nsteps
            b_p = self.bank()
            for c in range(nch):
                cs = slice(c * L, (c + 1) * L)
                self.MM(self.pb[b_p][0:L, cs], cur[0].ap[0:L, c, :], cur[1].ap[0:L, c, :], True, True, cur[0].r() + cur[1].r(), [self.rpb[b_p]])
            if not last:
                b_t = self.bank()
                for c in range(nch):
                    cs = slice(c * L, (c + 1) * L)
                    self.MM(self.pb[b_t][0:L, cs], cur[1].ap[0:L, c, :], cur[0].ap[0:L, c, :], True, True, cur[0].r() + cur[1].r(), [self.rpb[b_t]])
            yield
            self.A(nxt[1].ap[0:L], p3(b_p), AF.Copy, [self.rpb[b_p]], nxt[1].r())
            if not last:
                self.A(nxt[0].ap[0:L], p3(b_t), AF.Copy, [self.rpb[b_t]], nxt[0].r())
            yield
            b_x = self.bank()
            for c in range(nch):
                cs = slice(c * L, (c + 1) * L)
                self.MM(self.pb[b_x][0:L, cs], nxt[1].ap[0:L, c, :], Xtb.ap[0:L, c, :], True, True, nxt[1].r() + Xtb.r(), [self.rpb[b_x]])
            yield
            self.TT(Xt.ap[0:L], Xt.ap[0:L], p3(b_x), ALU.add, Xt.r() + [self.rpb[b_x]], Xt.r())
            yield
            self.A(Xtb.ap[0:L], Xt.ap[0:L], AF.Copy, Xt.r(), Xtb.r())
            yield
            cur = nxt
        by = self.bank()
        for c in range(nch):
            cs = slice(c * L, (c + 1) * L)
            self.MM(self.pb[by][:, cs], Kg.ap[0:L, c, :], Xtb.ap[0:L, c, :], True, True, Kg.r() + Xtb.r(), [self.rpb[by]])
        yield
        self.A(nyw.ap[:, 0:N], self.pb[by][:, 0:N], AF.Copy, [self.rpb[by]], nyw.r(), scale=-1.0)
        yield
        bo = self.pin()
        for c in range(nch):
            cs = slice(c * L, (c + 1) * L)
            if tc.kind == 'p':
                S_ap, S_r = self.S[:, l, h, :], [self.r_S[l][h]]
            else:
                si = self.sSi
                self.sSi = (self.sSi + 1) % 4
                S_ap, S_r = self.sS[si][:, :], [self.r_sS[si]]
                self.DMA(S_ap, self.i["sg"][l, c, h], self.r_sS[si], [], S_r)
            sb = Sb[c % 2]
            self.A(sb.ap[:, :], S_ap, AF.Copy, S_r, sb.r())
            vn = vnew[c % 2]
            b1 = self.bank()
            self.MM(self.pb[b1][0:L, 0:HD], Xtb.ap[0:L, c, :], Vtok.ap[0:L, c, :], True, False, Xtb.r() + Vtok.r(), [self.rpb[b1]])
            yield
            self.MM(self.pb[b1][0:L, 0:HD], nyw.ap[:, cs], sb.ap[:, :], False, True, nyw.r() + sb.r(), [self.rpb[b1]])
            self.MM(self.pb[bo][:, cs], sb.ap[:, :], QgT.ap[:, cs], True, False, sb.r() + QgT.r(), [self.rpb[bo]])
            yield
            self.A(vn.ap[0:L, :], self.pb[b1][0:L, 0:HD], AF.Copy, [self.rpb[b1]] + bet.r(), vn.r(), scale=bet.ap[0:L, c, h:h + 1])
            yield
            self.MM(self.pb[bo][:, cs], vn.ap[0:L, :], attnT.ap[0:L, c, :], False, True, vn.r() + attnT.r(), [self.rpb[bo]])
            b2 = self.bank()
            self.MM(self.pb[b2][:, 0:HD], Kp.ap[0:L, c, :], vn.ap[0:L, :], True, True, Kp.r() + vn.r(), [self.rpb[b2]])
            yield
            self.STT(S_ap, S_ap, eGL.ap[:, c, h:h + 1], self.pb[b2][:, 0:HD], ALU.mult, ALU.add, S_r + eGL.r() + [self.rpb[b2]], S_r)
            if tc.kind == 's':
                self.DMA(self.o["sgo"][l, c, h], S_ap, S_r[0], S_r, [])
            elif tc.t == self.ntp - 1 and c == nch - 1:
                self.DMA(self.o["pg"][l, h], S_ap, S_r[0], S_r, [])
            yield
        self.A(o2.ap[:, 0:N], self.pb[bo][:, 0:N], AF.Square, [self.rpb[bo]], o2.r())
        yield
        bs = self.bank()
        self.MM(self.pb[bs][:, 0:N], self.ones, o2.ap[:, 0:N], True, True, cstr + o2.r(), [self.rpb[bs]])
        yield
        self.rsqrt(rstd.ap[:, 0:N], self.pb[bs][:, 0:N], [self.rpb[bs]], rstd.r(), 1.0 / HD)
        yield
        self.TT(ot.ap[:, 0:N], self.pb[bo][:, 0:N], rstd.ap[:, 0:N], ALU.mult, [self.rpb[bo]] + rstd.r(), ot.r())
        self.unpin(bo)
        yield
        self.STT(cat.ap[:, h, 0:N], ot.ap[:, 0:N], self.gdng[:, l:l + 1], zT.ap[:, h, 0:N], ALU.mult, ALU.mult,
                 ot.r() + [self.r_misc] + zT.r(h), cat.r(h))

    def kv_share(self, tc):
        N = tc.N
        hT = V(self, AR_H, [KC, N], BF16)
        self.prenorm(5, 0, hT, N)
        W = self.i["w_kvf"]
        kst = V(self, AR_BIG if tc.kind == 'p' else AR_BIG + 40960, [H, N], BF16)
        ost = [V(self, AR_BIG + 16384 + j * 1024, [WCOLS], F32) for j in range(6)]
        osb = [V(self, AR_BIG + 24576 + j * 1024, [WCOLS], BF16) for j in range(6)]
        oc = [0]
        if tc.kind == 'p':
            toks = [(tb * 128, 128) for tb in range(N // 128)]
            row0 = tc.t * NP
            rows_of = lambda ti: slice(row0 + ti * 128, row0 + (ti + 1) * 128)
            M_of = lambda ti: 128
            ko, vo, lo = self.o["pk"], self.o["pv"], self.o["plf"]
        else:
            toks = [(sq * DSEQ, DSEQ) for sq in range(NSEQ)]
            rows_of = lambda ti: slice(ti * DSEQ, (ti + 1) * DSEQ)
            M_of = lambda ti: DSEQ
            ko, vo, lo = self.o["sk"], self.o["sv"], self.o["slf"]
            self.Vnew = V(self, AR_BIG + 44032, [NSEQ, TOK], BF16)
        self.KTnew = kst

        def epi_k(m, rows, ps, b):
            self.A(kst.ap[:, m, 0:N], ps, AF.Copy, [self.rpb[b]], kst.r(m))
        self.linear_fm(W, 0, TOK, hT, N, epi_k)
        if tc.kind == 'p':
            self.DMA(self.kscr[:, :, tc.t * NP:(tc.t + 1) * NP], kst.ap[:, :, :], kst.r()[0], kst.r(), [self.r_kscr[tc.t]])

        def epi_tm(ti, cb, ncb, ps, b, which=0):
            j = oc[0] % 6
            oc[0] += 1
            M = M_of(ti)
            self.A(ost[j].ap[0:M, 0:ncb], ps, AF.Copy, [self.rpb[b]], ost[j].r())
            self.DMA((ko if which == 0 else vo)[rows_of(ti), cb:cb + ncb], ost[j].ap[0:M, 0:ncb], ost[j].r()[0], ost[j].r(), [])
            if which == 1:
                if tc.kind == 'p':
                    self.CP(osb[j].ap[0:M, 0:ncb], ost[j].ap[0:M, 0:ncb], ost[j].r(), osb[j].r())
                    self.DMA(self.vscr[rows_of(ti), cb:cb + ncb], osb[j].ap[0:M, 0:ncb], osb[j].r()[0], osb[j].r(), [self.r_vscr[tc.t]])
                else:
                    self.CP(self.Vnew.ap[0:M, ti, cb:cb + ncb], ost[j].ap[0:M, 0:ncb], ost[j].r(), self.Vnew.r())
        self.linear_tm(W, 0, TOK, hT, toks, lambda *a: epi_tm(*a, which=0))
        self.linear_tm(W, TOK, TOK, hT, toks, lambda *a: epi_tm(*a, which=1))
        self.DMA(self.wsm[:, :, 0:H], W[:, 2 * TOK:2 * TOK + H].rearrange("(kc p) c -> p kc c", p=128), self.r_wsm, [], [self.r_wsm], eng="pool")
        self.tmp_reset()
        lf = self.tmp([len(toks), H])
        t1 = self.tmp([len(toks), H])
        t2 = self.tmp([len(toks), H])
        M = toks[0][1]
        nt = len(toks)
        b = self.bank()
        for ti, (t0, M_) in enumerate(toks):
            for kc in range(KC):
                self.MM(self.pb[b][0:M, ti * H:(ti + 1) * H], hT.ap[:, kc, t0:t0 + M], self.wsm[:, kc, 0:H], kc == 0, kc == KC - 1,
                        hT.r(kc) + [self.r_wsm], [self.rpb[b]])
        x_, ax = t1.ap[0:M], t2.ap[0:M]
        self.TT(x_, self.pb[b][0:M, 0:nt * H].rearrange("p (t h) -> p t h", h=H),
                self.vecs[0:M, 48:60].unsqueeze(1).to_broadcast([M, nt, H]), ALU.add, [self.rpb[b], self.r_misc], t1.r())
        self.STT(ax, x_, -1.0, x_, ALU.mult, ALU.max, t1.r(), t2.r())
        self.A(ax, ax, AF.Exp, t2.r(), t2.r(), scale=-1.0)
        self.A(ax, ax, AF.Ln, t2.r(), t2.r(), bias=1.0)
        self.STT(lf.ap[0:M], x_, 0.0, ax, ALU.min, ALU.subtract, t1.r() + t2.r(), lf.r())
        for ti in range(nt):
            self.DMA(lo[rows_of(ti), :], lf.ap[0:M, ti, :], lf.r()[0], lf.r(), [])
        return lf

    def cumsum_prompt(self, tc, lf):
        for ti in range(tc.N // 128):
            blk = tc.t * (NP // 128) + ti
            b = self.bank()
            first = blk == 0
            self.MM(self.pb[b][:, 0:H], self.triu, lf.ap[:, ti, :], True, first, [self.r_cst] + lf.r(), [self.rpb[b]])
            if not first:
                self.MM(self.pb[b][:, 0:H], self.elast, self.Cp[:, blk - 1, :], False, True, [self.r_cst, self.r_Cp], [self.rpb[b]])
            self.A(self.Cp[:, blk, :], self.pb[b][:, 0:H], AF.Copy, [self.rpb[b]], [self.r_Cp])
        blk = tc.t * (NP // 128) + tc.N // 128 - 1
        b = self.bank()
        self.MM(self.pb[b][:, 0:H], self.elast, self.Cp[:, blk, :], True, True, [self.r_cst, self.r_Cp], [self.rpb[b]])
        self.A(self.cendp[:, :], self.pb[b][:, 0:H], AF.Copy, [self.rpb[b]], [self.r_cend])

    def fox_in(self, l, tc):
        N = tc.N
        hT = V(self, AR_H, [KC, N], BF16)
        self.prenorm(0, l, hT, N)
        qT = V(self, AR_BIG, [H, N], BF16)
        sg = V(self, AR_BIG + 12288, [H, N], BF16)
        mq = V(self, AR_BIG + 24576, [MH, N], BF16)
        W = self.i["w_in_b"][l - 2]

        def epi_q(m, rows, ps, b):
            self.A(qT.ap[:, m, 0:N], ps, AF.Copy, [self.rpb[b]], qT.r(m))

        def epi_g(m, rows, ps, b):
            self.A(sg.ap[:, m, 0:N], ps, AF.Sigmoid, [self.rpb[b]], sg.r(m))

        def epi_mq(m, rows, ps, b):
            self.A(mq.ap[:, m, 0:N], ps, AF.Copy, [self.rpb[b]], mq.r(m))
        self.linear_fm(W, 0, TOK, hT, N, epi_q)
        self.linear_fm(W, TOK, TOK, hT, N, epi_g)
        self.linear_fm(W, 2 * TOK, 512, hT, N, epi_mq)
        return qT, sg, mq

    def fox_layer_p(self, l, tc):
        N = tc.N
        qT, sg, mq = self.fox_in(l, tc)
        cat = V(self, AR_H, [KC, N], BF16)
        self.tmp_reset()
        nkb = (tc.t + 1) * (NP // 128)
        biasK = self.tmp([nkb, H])
        Ksb = [self.tmp([2, NP], BF16) for _ in range(2)]
        Vsb = [self.tmp([NP // 128, 256], BF16) for _ in range(2)]
        pT = [self.tmp([N], BF16) for _ in range(4)]
        rden = self.tmp([N])
        ot = self.tmp([N])
        self.tmp_s = self.tmp([512])
        self.TT(biasK.ap[:, :, :], self.cendp[:, :].unsqueeze(1).to_broadcast([128, nkb, H]), self.Cp[:, 0:nkb, :], ALU.subtract,
                [self.r_cend, self.r_Cp], biasK.r())
        SK = 2
        si = 0
        for hg in range(H // 2):
            acc = [(self.pin(), self.pin()) for _ in range(2)]
            units = [(sb_, hh, kb) for sb_ in range(tc.t + 1) for hh in range(2) for kb in range(NP // 128)]
            bufs = {}
            pend = {}

            def stage1(ui, u):
                nonlocal si
                sb_, hh, kb = u
                if sb_ not in bufs:
                    ks, vs = Ksb[si % 2], Vsb[si % 2]
                    si += 1
                    self.DMA(ks.ap[:, :, :], self.kscr[:, hg * 2:hg * 2 + 2, sb_ * NP:(sb_ + 1) * NP], ks.r()[0], [self.r_kscr[sb_]], ks.r())
                    self.DMA(vs.ap[:, :, :], self.vscr[sb_ * NP:(sb_ + 1) * NP, hg * 256:(hg + 1) * 256].rearrange("(b p) c -> p b c", p=128),
                             vs.r()[0], [self.r_vscr[sb_]], vs.r())
                    bufs[sb_] = (ks, vs)
                ks, vs = bufs[sb_]
                h = hg * 2 + hh
                diag = sb_ == tc.t
                kg = sb_ * (NP // 128) + kb
                q0 = kb * 128 if diag else 0
                nq = N - q0
                b = self.bank()
                self.MM(self.pb[b][:, 0:nq], ks.ap[:, hh, kb * 128:(kb + 1) * 128], qT.ap[:, h, q0:N], True, True,
                        ks.r() + qT.r(h), [self.rpb[b]])
                p = pT[ui % len(pT)]
                self.A(p.ap[:, 0:nq], self.pb[b][:, 0:nq], AF.Exp, [self.rpb[b]] + biasK.r(), p.r(),
                       bias=biasK.ap[:, kg, h:h + 1], scale=SCALE)
                if diag:
                    self.TT(p.ap[:, 0:128], p.ap[:, 0:128], self.triub, ALU.mult, p.r() + [self.r_cstb], p.r())
                pend[ui] = (p, vs, q0, nq, kg, hh, kb)

            def stage2(ui):
                p, vs, q0, nq, kg, hh, kb = pend.pop(ui)
                bo, bd = acc[hh]
                first = kg == 0
                last = kg == nkb - 1
                self.MM(self.pb[bo][:, q0:N], vs.ap[:, kb, hh * 128:(hh + 1) * 128], p.ap[:, 0:nq], first, last,
                        vs.r() + p.r(), [self.rpb[bo]])
                self.MM(self.pb[bd][:, q0:N], self.onesb, p.ap[:, 0:nq], first, last, [self.r_cstb] + p.r(), [self.rpb[bd]])

            nu = len(units)
            for ui in range(min(SK, nu)):
                stage1(ui, units[ui])
            for ui in range(nu):
                if ui + SK < nu:
                    stage1(ui + SK, units[ui + SK])
                stage2(ui)
            for hh in range(2):
                h = hg * 2 + hh
                bo, bd = acc[hh]
                self.REC(rden.ap[:, 0:N], self.pb[bd][:, 0:N], [self.rpb[bd]], rden.r())
                self.TT(ot.ap[:, 0:N], self.pb[bo][:, 0:N], rden.ap[:, 0:N], ALU.mult, [self.rpb[bo]] + rden.r(), ot.r())
                self.TT(cat.ap[:, h, 0:N], ot.ap[:, 0:N], sg.ap[:, h, 0:N], ALU.mult, ot.r() + sg.r(h), cat.r(h))
                self.unpin(bo)
                self.unpin(bd)
        self.mem_attend(l, tc, mq, cat)
        self.out_proj(l, tc, cat)

    def cumsum_sample(self, lf):
        nb = PAST // 128
        SH = NSEQ * H
        lst = self.tmp([nb, SH])
        for sq in range(NSEQ):
            self.DMA(lst.ap[:, :, sq * H:(sq + 1) * H], self.i["clf"][sq].rearrange("(b p) h -> p b h", p=128), lst.r()[0], [], lst.r())
        for blk in range(nb):
            b = self.bank()
            self.MM(self.pb[b][:, 0:SH], self.triu, lst.ap[:, blk, :], True, blk == 0, [self.r_cst] + lst.r(), [self.rpb[b]])
            if blk > 0:
                self.MM(self.pb[b][:, 0:SH], self.elast, self.Cs[:, blk - 1, :], False, True, [self.r_cst, self.r_Cs], [self.rpb[b]])
            self.A(self.Cs[:, blk, :], self.pb[b][:, 0:SH], AF.Copy, [self.rpb[b]], [self.r_Cs])
        lf2 = lf.ap[0:DSEQ].rearrange("p s h -> p (s h)")
        b = self.bank()
        self.MM(self.pb[b][0:DSEQ, 0:SH], self.triu[0:DSEQ, 0:DSEQ], lf2, True, False, [self.r_cst] + lf.r(), [self.rpb[b]])
        self.MM(self.pb[b][0:DSEQ, 0:SH], self.elast[:, 0:DSEQ], self.Cs[:, nb - 1, :], False, True, [self.r_cst, self.r_Cs], [self.rpb[b]])
        self.A(self.Cs[0:DSEQ, nb, :], self.pb[b][0:DSEQ, 0:SH], AF.Copy, [self.rpb[b]], [self.r_Cs])
        cend = V(self, AR_BIG + 60 * 1024, [SH], F32)
        b = self.bank()
        self.MM(self.pb[b][:, 0:SH], self.ones[0:DSEQ, :], lf2, True, False, [self.r_cst] + lf.r(), [self.rpb[b]])
        self.MM(self.pb[b][:, 0:SH], self.elast, self.Cs[:, nb - 1, :], False, True, [self.r_cst, self.r_Cs], [self.rpb[b]])
        self.A(cend.ap[:, :], self.pb[b][:, 0:SH], AF.Copy, [self.rpb[b]], cend.r())
        self.TT(self.Cs[:, :, :], cend.ap[:, :].unsqueeze(1).to_broadcast([128, nb + 1, SH]), self.Cs[:, :, :], ALU.subtract,
                cend.r() + [self.r_Cs], [self.r_Cs])

    def fox_layer_s(self, l, tc):
        N = tc.N
        qT, sg, mq = self.fox_in(l, tc)
        cat = V(self, AR_H, [KC, N], BF16)
        self.tmp_reset()
        nb = PAST // 128
        kst = [self.tmp([TOK])]
        KT = [self.tmp([H, 128], BF16) for _ in range(2)]
        VB = [self.tmp([TOK], BF16) for _ in range(2)]
        sc = self.tmp([H, DSEQ])
        pT = [self.tmp([H, DSEQ], BF16) for _ in range(2)]
        rden = self.tmp([H, DSEQ])
        ot = self.tmp([H, DSEQ])
        self.tmp_s = self.tmp([512])
        bi = 0
        for sq in range(NSEQ):
            q0 = sq * DSEQ
            bo, bd = self.pin(), self.pin()
            self.MS(self.pb[bo][:, 0:H * DSEQ], 0.0, [self.rpb[bo]])
            self.MS(self.pb[bd][:, 0:H * DSEQ], 0.0, [self.rpb[bd]])
            for blk in range(nb + 1):
                new = blk == nb
                R = DSEQ if new else 128
                kt, vb = KT[bi % 2], VB[bi % 2]
                ks = kst[0]
                bi += 1
                if not new:
                    self.DMA(ks.ap[:, :], self.i["ck"][sq, blk * 128:(blk + 1) * 128, :], ks.r()[0], [], ks.r())
                    for g in range(3):
                        b = self.bank()
                        for j in range(4):
                            hh = g * 4 + j
                            self.TR(self.pb[b][:, j * 128:(j + 1) * 128], ks.ap[:, hh * 128:(hh + 1) * 128], self.ident,
                                    ks.r() + [self.r_cst], [self.rpb[b]])
                        self.A(kt.ap[:, g * 4:(g + 1) * 4, :], self.pb[b][:, :].rearrange("p (j t) -> p j t", t=128), AF.Copy,
                               [self.rpb[b]], kt.r())
                    self.DMA(vb.ap[:, :], self.i["cv"][sq, blk * 128:(blk + 1) * 128, :], vb.r()[0], [], vb.r(), eng="pool")
                    ktap = lambda hh: kt.ap[:, hh, :]
                    vbap = lambda hh: vb.ap[:, hh * 128:(hh + 1) * 128]
                    ktr, vbr = kt.r(), vb.r()
                else:
                    ktap = lambda hh: self.KTnew.ap[:, hh, q0:q0 + DSEQ]
                    vbap = lambda hh: self.Vnew.ap[0:DSEQ, sq, hh * 128:(hh + 1) * 128]
                    ktr, vbr = self.KTnew.r(), self.Vnew.r()
                b = self.bank()
                for hh in range(H):
                    self.MM(self.pb[b][0:R, hh * DSEQ:(hh + 1) * DSEQ], ktap(hh), qT.ap[:, hh, q0:q0 + DSEQ], True, True,
                            ktr + qT.r(hh), [self.rpb[b]])
                self.STT(sc.ap[0:R], self.pb[b][0:R, 0:H * DSEQ].rearrange("p (h q) -> p h q", q=DSEQ), SCALE,
                         self.Cs[0:R, blk, sq * H:(sq + 1) * H].unsqueeze(2).to_broadcast([R, H, DSEQ]), ALU.mult, ALU.add,
                         [self.rpb[b], self.r_Cs], sc.r())
                p = pT[bi % 2]
                self.A(p.ap[0:R], sc.ap[0:R], AF.Exp, sc.r(), p.r())
                if new:
                    self.TT(p.ap[0:R], p.ap[0:R], self.triub[0:R, 0:DSEQ].unsqueeze(1).to_broadcast([R, H, DSEQ]), ALU.mult,
                            p.r() + [self.r_cstb], p.r())
                for hh in range(H):
                    cs = slice(hh * DSEQ, (hh + 1) * DSEQ)
                    self.MM(self.pb[bo][:, cs], vbap(hh), p.ap[0:R, hh, :], False, new and hh == H - 1, vbr + p.r(), [self.rpb[bo]], skip=True)
                    self.MM(self.pb[bd][:, cs], self.onesb[0:R, :], p.ap[0:R, hh, :], False, new and hh == H - 1, [self.r_cstb] + p.r(), [self.rpb[bd]], skip=True)
            p3 = lambda b: self.pb[b][:, 0:H * DSEQ].rearrange("p (h q) -> p h q", q=DSEQ)
            self.REC(rden.ap[:, :, :], p3(bd), [self.rpb[bd]], rden.r())
            self.TT(ot.ap[:, :, :], p3(bo), rden.ap[:, :, :], ALU.mult, [self.rpb[bo]] + rden.r(), ot.r())
            self.TT(cat.ap[:, 0:H, q0:q0 + DSEQ], ot.ap[:, :, :], sg.ap[:, :, q0:q0 + DSEQ], ALU.mult, ot.r() + sg.r(), cat.r(0, H))
            self.unpin(bo)
            self.unpin(bd)
        self.mem_attend(l, tc, mq, cat)
        self.out_proj(l, tc, cat)

    def run_tile(self, tc, src, dst):
        self.load_xT(src, tc.N)
        nl = self.nlayers
        for l in range(min(2, nl)):
            self.gdn_layer(l, tc)
            self.mlp(l, tc.N)
        if nl > 2:
            lf = self.kv_share(tc)
            if tc.kind == 'p':
                self.cumsum_prompt(tc, lf)
            else:
                self.cumsum_sample(lf)
            for l in range(2, nl):
                if tc.kind == 'p':
                    self.fox_layer_p(l, tc)
                else:
                    self.fox_layer_s(l, tc)
                self.mlp(l, tc.N)
        self.store_y(dst, tc.N)

    def cumsum_end_only(self, tc):
        cend = self.tmp([H])
        blk = tc.t * (NP // 128) + tc.N // 128 - 1
        b = self.bank()
        self.MM(self.pb[b][:, 0:H], self.elast, self.Cp[:, blk, :], True, True, [self.r_cst, self.r_Cp], [self.rpb[b]])
        self.A(cend.ap[:, :], self.pb[b][:, 0:H], AF.Copy, [self.rpb[b]], cend.r())
        return cend

    def build(self):
        self.prologue()
        if self.ntp > 0:
            self.memory_kv()
        for t in range(self.ntp):
            tc = TileCfg('p', NP, 64, t)
            self.run_tile(tc, self.i["xp"][t * NP:(t + 1) * NP, :], self.o["yp"][t * NP:(t + 1) * NP, :])
        if self.do_sample:
            tc = TileCfg('s', NS, DSEQ, 0)
            self.run_tile(tc, self.i["xs"], self.o["ys"])
        return self.P.emit()


def make_consts():
    c = np.zeros((128, 640), np.float32)
    c[:, 0:128] = np.eye(128)
    c[:, 128:256] = 1.0
    c[:, 256:384] = np.triu(np.ones((128, 128)))
    c[:, 384:512] = -np.triu(np.ones((128, 128)), 1)
    c[127, 512:640] = 1.0
    return c


def build_program(ntp=8, do_sample=True, nlayers=4):
    nc = bass.Bass("TRN2", target_bir_lowering=False)
    with ExitStack() as es:
        k = K(nc, es, ntp=ntp, do_sample=do_sample, nlayers=nlayers)
        stats = k.build()
    return nc, stats


def core_inputs(c, inp):
    b = c // 2
    s0 = c * NSEQ
    f = lambda a: np.ascontiguousarray(a, dtype=np.float32)
    m = dict(
        xp=f(inp["x_prompt"][b]), xs=f(inp["x_sample"][s0:s0 + NSEQ].reshape(NS, D)),
        sg=f(inp["state_gdn"][:, s0:s0 + NSEQ]), sc=f(inp["state_conv"][:, s0:s0 + NSEQ].reshape(2, NSEQ * 3, QKV)),
        ck=f(inp["cache_k"][s0:s0 + NSEQ].reshape(NSEQ, PAST, TOK)), cv=f(inp["cache_v"][s0:s0 + NSEQ].reshape(NSEQ, PAST, TOK)),
        clf=f(inp["cache_logf"][s0:s0 + NSEQ]),
        cmk=f(inp["cache_mem_k"][:, s0:s0 + NSEQ].reshape(4, NSEQ, NMEM, 512)),
        cmv=f(inp["cache_mem_v"][:, s0:s0 + NSEQ].reshape(4, NSEQ, NMEM, 512)),
        mp=f(inp["mem_prompt"][b]),
        g_pre=f(inp["norm_mix_pre"].reshape(64, 128)), g_post=f(inp["norm_mix_post"].reshape(64, 128)),
        g_mpre=f(inp["norm_mlp_pre"].reshape(64, 128)), g_mpost=f(inp["norm_mlp_post"].reshape(64, 128)),
        g_mem=f(inp["norm_mem"].reshape(64, 128)), g_kv=f(inp["norm_kv"].reshape(16, 128)),
        convw=f(inp["conv_w_a"].reshape(288, 128)), a_log=f(inp["a_log"].reshape(24)), dt_bias=f(inp["dt_bias"].reshape(24)),
        gdn_norm=f(inp["gdn_norm"]), b_f=f(inp["b_f"]),
        w_in_a=f(inp["w_in_a"]), w_in_b=f(inp["w_in_b"]), w_kvf=f(inp["w_kvf"]), w_mem=f(inp["w_mem_kv"]),
        w_o=f(inp["w_o"]), w_up=f(inp["w_up"]), w_down=f(inp["w_down"]),
        cst=make_consts(),
    )
    return m


def kernel(**inp):
    inp = {k: np.asarray(v) for k, v in inp.items()}
    nc, _ = build_program()
    in_maps = [core_inputs(c, inp) for c in range(8)]
    res = run_bass_kernel_spmd(nc, in_maps, core_ids=list(range(8))).results
    ev = [res[c] for c in range(0, 8, 2)]
    st = lambda key, sh: np.stack([r[key] for r in ev]).reshape(sh).astype(np.float32)
    cat = lambda key: np.concatenate([r[key] for r in res], axis=0)
    B = 4
    y_prompt = st("yp", (B, SEQ, D))
    y_sample = cat("ys").reshape(32, DSEQ, D)
    p_gdn = np.stack([r["pg"] for r in ev], axis=1)
    p_conv = np.stack([r["pc"].reshape(2, 3, QKV) for r in ev], axis=1)
    p_k = st("pk", (B, SEQ, H, HD))
    p_v = st("pv", (B, SEQ, H, HD))
    p_logf = st("plf", (B, SEQ, H))
    p_mem_k = np.stack([r["pmk"].reshape(4, NMEM, MH, HD) for r in ev], axis=1)
    p_mem_v = np.stack([r["pmv"].reshape(4, NMEM, MH, HD) for r in ev], axis=1)
    s_gdn = np.concatenate([r["sgo"] for r in res], axis=1)
    s_conv = np.concatenate([r["sco"].reshape(2, NSEQ, 3, QKV) for r in res], axis=1)
    s_k = cat("sk").reshape(32, DSEQ, H, HD)
    s_v = cat("sv").reshape(32, DSEQ, H, HD)
    s_logf = cat("slf").reshape(32, DSEQ, H)
    outs = (y_prompt, y_sample, p_gdn, p_conv, p_k, p_v, p_logf, p_mem_k, p_mem_v, s_gdn, s_conv, s_k, s_v, s_logf)
    return tuple(np.ascontiguousarray(o, dtype=np.float32) for o in outs)
```
